# Optimizing a Trainium2 kernel written in Bass

```python
import math
import jax
import jax.numpy as jnp
from jax import lax
import numpy as np

D_MODEL = 1024
BATCH = 4
SEQ = 8192
DEPTH = 4
DEC_BATCH = 2
DEC_SEQ = 8192
PAST_LEN = 128

CHUNK = 64
CONV_K = 5
NORM_EPS = 1e-6
D_FF = 2816
N_BRANCH = 3
MIX_W = 512

GDN_HEADS = 4
GDN_DK = 128
GDN_DV = 128
GDN_KW = GDN_HEADS * GDN_DK
GDN_VW = GDN_HEADS * GDN_DV
GDN_CONV_CH = 2 * GDN_KW + GDN_VW

GLA_HEADS = 4
GLA_DK = 64
GLA_DV = 128
GLA_KW = GLA_HEADS * GLA_DK
GLA_VW = GLA_HEADS * GLA_DV
GLA_RANK = 16
GLA_TAU = 16.0

SSD_HEADS = 8
SSD_HEADDIM = 64
SSD_GROUPS = 2
SSD_STATE = 128
SSD_HPG = SSD_HEADS // SSD_GROUPS
SSD_INNER = SSD_HEADS * SSD_HEADDIM
SSD_BCW = SSD_GROUPS * SSD_STATE
SSD_CONV_CH = SSD_INNER + 2 * SSD_BCW

IN_SIZES = (GDN_CONV_CH, GDN_VW, 2 * GDN_HEADS, 2 * GDN_HEADS,
            GLA_KW, GLA_KW, GLA_VW, GLA_VW, 2 * GLA_RANK,
            SSD_INNER, SSD_CONV_CH, 2 * SSD_HEADS,
            N_BRANCH * D_MODEL)
IN_COLS = sum(IN_SIZES)

kernel_name = "hybrid_bidir_gdn_gla_ssd_macaron"

F32 = jnp.float32


def _rmsnorm(x, g):
    xf = x.astype(F32)
    y = xf * lax.rsqrt(jnp.mean(xf * xf, axis=-1, keepdims=True) + NORM_EPS)
    return (y * g.astype(F32)).astype(x.dtype)


def _l2norm(t):
    return t * lax.rsqrt(jnp.sum(t * t, axis=-1, keepdims=True) + NORM_EPS)


def _swiglu(h, w_gate, w_up, w_down):
    return (jax.nn.silu(h @ w_gate) * (h @ w_up)) @ w_down


def _split_cols(z, sizes):
    parts, start = [], 0
    for s in sizes:
        parts.append(z[..., start:start + s])
        start += s
    return parts


def _dwconv_centred(u, w, b):
    ch = u.shape[-1]
    y = lax.conv_general_dilated(
        u, w.astype(u.dtype)[:, None, :], window_strides=(1,),
        padding=((CONV_K // 2, CONV_K // 2),),
        dimension_numbers=('NWC', 'WIO', 'NWC'), feature_group_count=ch)
    if b is not None:
        y = y + b.astype(u.dtype)
    return y


def _flip(t):
    return jnp.flip(t, axis=1)


def _to_chunks(t):
    b, l = t.shape[:2]
    return jnp.moveaxis(t.reshape((b, l // CHUNK, CHUNK) + t.shape[2:]), 1, 0)


def _from_chunks(t):
    t = jnp.moveaxis(t, 0, 1)
    return t.reshape((t.shape[0], t.shape[1] * t.shape[2]) + t.shape[3:])


def _gated_delta_chunked(q, k, v, g, beta):
    bsz, _, h, dk = q.shape
    dv = v.shape[-1]
    qc, kc, vc, gc, bc = (_to_chunks(t) for t in (q, k, v, g, beta))
    incl = jnp.tril(jnp.ones((CHUNK, CHUNK), bool))
    strict = jnp.tril(jnp.ones((CHUNK, CHUNK), bool), -1)
    G = jnp.cumsum(gc, axis=2)
    Gh = jnp.moveaxis(G, 2, -1)
    diff = Gh[..., :, None] - Gh[..., None, :]
    kb = kc * bc[..., None]
    a_mat = jnp.einsum('nbihd,nbjhd->nbhij', kb, kc) * jnp.exp(jnp.where(strict, diff, -jnp.inf))
    rhs = jnp.concatenate([kb * jnp.exp(G)[..., None], vc * bc[..., None]], axis=-1)
    rhs = jnp.swapaxes(rhs, 2, 3)
    eye = jnp.eye(CHUNK, dtype=F32)
    sol = lax.linalg.triangular_solve(eye + a_mat, rhs, left_side=True, lower=True,
                                      unit_diagonal=True)
    w_c, u_c = sol[..., :dk], sol[..., dk:]
    att = jnp.einsum('nbihd,nbjhd->nbhij', qc, kc) * jnp.exp(jnp.where(incl, diff, -jnp.inf))
    qg = jnp.swapaxes(qc * jnp.exp(G)[..., None], 2, 3)
    kg = jnp.swapaxes(kc * jnp.exp(G[:, :, -1:] - G)[..., None], 2, 3)
    gl = jnp.exp(G[:, :, -1])

    def step(S, xs):
        w_, u_, att_, qg_, kg_, gl_ = xs
        v_new = u_ - jnp.einsum('bhcd,bhde->bhce', w_, S)
        o = jnp.einsum('bhcd,bhde->bhce', qg_, S) + jnp.einsum('bhij,bhje->bhie', att_, v_new)
        S = S * gl_[..., None, None] + jnp.einsum('bhcd,bhce->bhde', kg_, v_new)
        return S, o

    _, o = lax.scan(step, jnp.zeros((bsz, h, dk, dv), F32), (w_c, u_c, att, qg, kg, gl))
    return _from_chunks(jnp.swapaxes(o, 2, 3))


def _gla_chunked(q, k, v, gk):
    bsz, _, h, dk = q.shape
    dv = v.shape[-1]
    qc, kc, vc, gc = (_to_chunks(t) for t in (q, k, v, gk))
    Gc = jnp.cumsum(gc, axis=2)
    incl = jnp.tril(jnp.ones((CHUNK, CHUNK), bool))[None, :, :, None, None]

    def step(S, xs):
        q_, k_, v_, G_ = xs
        Gl = G_[:, -1]
        o = jnp.einsum('bchd,bhde->bche', q_ * jnp.exp(G_), S)
        diff = G_[:, :, None] - G_[:, None, :]
        dec = jnp.exp(jnp.where(incl, diff, -jnp.inf))
        att = jnp.einsum('bihd,bjhd,bijhd->bhij', q_, k_, dec)
        o = o + jnp.einsum('bhij,bjhe->bihe', att, v_)
        S = S * jnp.exp(Gl)[..., None] + jnp.einsum('bchd,bche->bhde', k_ * jnp.exp(Gl[:, None] - G_), v_)
        return S, o

    _, o = lax.scan(step, jnp.zeros((bsz, h, dk, dv), F32), (qc, kc, vc, Gc))
    return _from_chunks(o)


def _ssd_chunked(x, dt, A, Bm, Cm):
    bsz = x.shape[0]
    xc, dtc, Bc, Cc = (_to_chunks(t) for t in (x, dt, Bm, Cm))
    incl = jnp.tril(jnp.ones((CHUNK, CHUNK), bool))[:, :, None, None]
    acum = jnp.cumsum(dtc * A, axis=2)
    diff = acum[:, :, :, None] - acum[:, :, None, :]
    lmat = jnp.exp(jnp.where(incl, diff, -jnp.inf))
    scores = jnp.einsum('nbigs,nbjgs->nbijg', Cc, Bc)[..., None] * lmat * dtc[:, :, None]
    y_diag = jnp.einsum('nbijgh,nbjghp->nbighp', scores, xc)
    w_state = jnp.exp(acum[:, :, -1:] - acum) * dtc

    def step(S, xs):
        c_, b_, ac_, w_, x_ = xs
        y_off = jnp.einsum('bcgs,bghps->bcghp', c_, S) * jnp.exp(ac_)[..., None]
        S = S * jnp.exp(ac_[:, -1])[..., None, None] + jnp.einsum('bcgs,bcgh,bcghp->bghps', b_, w_, x_)
        return S, y_off

    S0 = jnp.zeros((bsz, SSD_GROUPS, SSD_HPG, SSD_HEADDIM, SSD_STATE), F32)
    _, y_off = lax.scan(step, S0, (Cc, Bc, acum, w_state, xc))
    return _from_chunks(y_diag + y_off)


def _gdn_mixer(qkv_raw, z, b_raw, a_raw, conv_w, a_log, dt_bias, norm_g):
    bsz, l = z.shape[:2]
    qkv = jax.nn.silu(_dwconv_centred(qkv_raw, conv_w, None)).astype(F32)
    q = _l2norm(qkv[..., :GDN_KW].reshape(bsz, l, GDN_HEADS, GDN_DK)) * (GDN_DK ** -0.5)
    k = _l2norm(qkv[..., GDN_KW:2 * GDN_KW].reshape(bsz, l, GDN_HEADS, GDN_DK))
    v = qkv[..., 2 * GDN_KW:].reshape(bsz, l, GDN_HEADS, GDN_DV)
    beta = jax.nn.sigmoid(b_raw.astype(F32)).reshape(bsz, l, 2, GDN_HEADS)
    g = -jnp.exp(a_log.astype(F32)) * jax.nn.softplus(
        a_raw.astype(F32).reshape(bsz, l, 2, GDN_HEADS) + dt_bias.astype(F32))
    o = (_gated_delta_chunked(q, k, v, g[:, :, 0], beta[:, :, 0])
         + _flip(_gated_delta_chunked(_flip(q), _flip(k), _flip(v),
                                      _flip(g[:, :, 1]), _flip(beta[:, :, 1]))))
    o = _rmsnorm(o, norm_g) * jax.nn.silu(z.astype(F32).reshape(bsz, l, GDN_HEADS, GDN_DV))
    return o.reshape(bsz, l, GDN_VW).astype(z.dtype)


def _gla_mixer(q_raw, k_raw, v_raw, r, g_lr, w_gup, b_g, norm_g):
    bsz, l = r.shape[:2]
    q = q_raw.astype(F32).reshape(bsz, l, GLA_HEADS, GLA_DK) * (GLA_DK ** -0.5)
    k = k_raw.astype(F32).reshape(bsz, l, GLA_HEADS, GLA_DK)
    v = v_raw.astype(F32).reshape(bsz, l, GLA_HEADS, GLA_DV)
    lr = g_lr.astype(F32).reshape(bsz, l, 2, GLA_RANK)
    gk = jax.nn.log_sigmoid(jnp.einsum('blsr,srk->blsk', lr, w_gup.astype(F32))
                            + b_g.astype(F32)) / GLA_TAU
    gk = gk.reshape(bsz, l, 2, GLA_HEADS, GLA_DK)
    o = (_gla_chunked(q, k, v, gk[:, :, 0])
         + _flip(_gla_chunked(_flip(q), _flip(k), _flip(v), _flip(gk[:, :, 1]))))
    o = _rmsnorm(o, norm_g) * jax.nn.silu(r.astype(F32).reshape(bsz, l, GLA_HEADS, GLA_DV))
    return o.reshape(bsz, l, GLA_VW).astype(r.dtype)


def _ssd_mixer(z, xbc_raw, dt_raw, conv_w, conv_b, a_log, dt_bias, d_skip, norm_g):
    bsz, l = z.shape[:2]
    xbc = jax.nn.silu(_dwconv_centred(xbc_raw, conv_w, conv_b)).astype(F32)
    x = xbc[..., :SSD_INNER].reshape(bsz, l, SSD_GROUPS, SSD_HPG, SSD_HEADDIM)
    Bm = xbc[..., SSD_INNER:SSD_INNER + SSD_BCW].reshape(bsz, l, SSD_GROUPS, SSD_STATE)
    Cm = xbc[..., SSD_INNER + SSD_BCW:].reshape(bsz, l, SSD_GROUPS, SSD_STATE)
    dt = jax.nn.softplus(dt_raw.astype(F32).reshape(bsz, l, 2, SSD_HEADS) + dt_bias.astype(F32))
    dt = dt.reshape(bsz, l, 2, SSD_GROUPS, SSD_HPG)
    A = -jnp.exp(a_log.astype(F32)).reshape(2, SSD_GROUPS, SSD_HPG)
    y = (_ssd_chunked(x, dt[:, :, 0], A[0], Bm, Cm)
         + _flip(_ssd_chunked(_flip(x), _flip(dt[:, :, 1]), A[1], _flip(Bm), _flip(Cm))))
    y = y + x * d_skip.astype(F32).reshape(SSD_GROUPS, SSD_HPG)[..., None]
    y = y.reshape(bsz, l, SSD_INNER) * jax.nn.silu(z.astype(F32))
    y = _rmsnorm(y.reshape(bsz, l, SSD_GROUPS, SSD_INNER // SSD_GROUPS),
                 norm_g.reshape(SSD_GROUPS, SSD_INNER // SSD_GROUPS))
    return y.reshape(bsz, l, SSD_INNER).astype(z.dtype)


def _layer(x, ffn_norm, ffn_w_gate, ffn_w_up, ffn_w_down, mix_norm, w_in,
           gdn_conv_w, gdn_a_log, gdn_dt_bias, gdn_norm, gla_w_gup, gla_b_g, gla_norm,
           ssd_conv_w, ssd_conv_b, ssd_a_log, ssd_dt_bias, ssd_d, ssd_norm, w_branch, w_out):
    bsz, l = x.shape[:2]
    x = x + 0.5 * _swiglu(_rmsnorm(x, ffn_norm[0]), ffn_w_gate[0], ffn_w_up[0], ffn_w_down[0])
    h = _rmsnorm(x, mix_norm)
    zin = h @ w_in
    (a_qkv, a_z, a_b, a_a, b_q, b_k, b_v, b_r, b_g, c_z, c_xbc, c_dt,
     gate_raw) = _split_cols(zin, IN_SIZES)
    y_a = _gdn_mixer(a_qkv, a_z, a_b, a_a, gdn_conv_w, gdn_a_log, gdn_dt_bias, gdn_norm)
    y_b = _gla_mixer(b_q, b_k, b_v, b_r, b_g, gla_w_gup, gla_b_g, gla_norm)
    y_c = _ssd_mixer(c_z, c_xbc, c_dt, ssd_conv_w, ssd_conv_b, ssd_a_log, ssd_dt_bias, ssd_d, ssd_norm)
    gates = jax.nn.sigmoid(gate_raw).reshape(bsz, l, N_BRANCH, D_MODEL)
    merged = (gates[:, :, 0] * (y_a @ w_branch[0])
              + gates[:, :, 1] * (y_b @ w_branch[1])
              + gates[:, :, 2] * (y_c @ w_branch[2]))
    x = x + merged @ w_out
    x = x + 0.5 * _swiglu(_rmsnorm(x, ffn_norm[1]), ffn_w_gate[1], ffn_w_up[1], ffn_w_down[1])
    return x


def setup_inputs(seed: int = 0) -> dict:
    key = jax.random.key(seed)
    ks = jax.random.split(key, 24)

    def nrm(k, shape, fan_in):
        return jax.random.normal(k, shape, F32) * (fan_in ** -0.5)

    def gain(k, shape):
        return 1.0 + 0.02 * jax.random.normal(k, shape, F32)

    def a_log_init(k, shape):
        return jnp.log(jax.random.uniform(k, shape, F32, minval=1.0, maxval=16.0))

    def dt_bias_init(k, shape):
        dt = jnp.exp(jax.random.uniform(k, shape, F32, minval=math.log(1e-3), maxval=math.log(1e-1)))
        return dt + jnp.log(-jnp.expm1(-dt))

    return {
        "x_prompt": jax.random.normal(ks[0], (BATCH, SEQ, D_MODEL), F32),
        "x_sample": jax.random.normal(ks[1], (DEC_BATCH, DEC_SEQ, D_MODEL), F32),
        "ffn_norm": gain(ks[2], (DEPTH, 2, D_MODEL)),
        "ffn_w_gate": nrm(ks[3], (DEPTH, 2, D_MODEL, D_FF), D_MODEL),
        "ffn_w_up": nrm(ks[4], (DEPTH, 2, D_MODEL, D_FF), D_MODEL),
        "ffn_w_down": nrm(ks[5], (DEPTH, 2, D_FF, D_MODEL), D_FF),
        "mix_norm": gain(ks[6], (DEPTH, D_MODEL)),
        "w_in": nrm(ks[7], (DEPTH, D_MODEL, IN_COLS), D_MODEL),
        "gdn_conv_w": nrm(ks[8], (DEPTH, CONV_K, GDN_CONV_CH), CONV_K),
        "gdn_a_log": a_log_init(ks[9], (DEPTH, 2, GDN_HEADS)),
        "gdn_dt_bias": dt_bias_init(ks[10], (DEPTH, 2, GDN_HEADS)),
        "gdn_norm": gain(ks[11], (DEPTH, GDN_DV)),
        "gla_w_gup": nrm(ks[12], (DEPTH, 2, GLA_RANK, GLA_KW), GLA_RANK),
        "gla_b_g": 0.1 * jax.random.normal(ks[13], (DEPTH, 2, GLA_KW), F32),
        "gla_norm": gain(ks[14], (DEPTH, GLA_DV)),
        "ssd_conv_w": nrm(ks[15], (DEPTH, CONV_K, SSD_CONV_CH), CONV_K),
        "ssd_conv_b": 0.02 * jax.random.normal(ks[16], (DEPTH, SSD_CONV_CH), F32),
        "ssd_a_log": a_log_init(ks[17], (DEPTH, 2, SSD_HEADS)),
        "ssd_dt_bias": dt_bias_init(ks[18], (DEPTH, 2, SSD_HEADS)),
        "ssd_d": gain(ks[19], (DEPTH, SSD_HEADS)),
        "ssd_norm": gain(ks[20], (DEPTH, SSD_INNER)),
        "w_branch": nrm(ks[21], (DEPTH, N_BRANCH, MIX_W, D_MODEL), MIX_W),
        "w_out": nrm(ks[22], (DEPTH, D_MODEL, D_MODEL), D_MODEL),
        "final_norm": gain(ks[23], (D_MODEL,)),
    }


def reference(x_prompt, x_sample, ffn_norm, ffn_w_gate, ffn_w_up, ffn_w_down, mix_norm, w_in,
              gdn_conv_w, gdn_a_log, gdn_dt_bias, gdn_norm, gla_w_gup, gla_b_g, gla_norm,
              ssd_conv_w, ssd_conv_b, ssd_a_log, ssd_dt_bias, ssd_d, ssd_norm,
              w_branch, w_out, final_norm):
    def trunk(x):
        for i in range(DEPTH):
            x = _layer(x, ffn_norm[i], ffn_w_gate[i], ffn_w_up[i], ffn_w_down[i], mix_norm[i], w_in[i],
                       gdn_conv_w[i], gdn_a_log[i], gdn_dt_bias[i], gdn_norm[i],
                       gla_w_gup[i], gla_b_g[i], gla_norm[i],
                       ssd_conv_w[i], ssd_conv_b[i], ssd_a_log[i], ssd_dt_bias[i], ssd_d[i], ssd_norm[i],
                       w_branch[i], w_out[i])
        return _rmsnorm(x, final_norm)

    y_prompt = trunk(x_prompt)
    y_sample = trunk(x_sample)
    return (y_prompt, y_sample)
```

```python
import numpy as np
from contextlib import ExitStack
import concourse.bass as bass
import concourse.mybir as mybir
from concourse.bass_utils import run_bass_kernel_spmd

F32 = mybir.dt.float32
BF16 = mybir.dt.bfloat16
AF = mybir.ActivationFunctionType
ALU = mybir.AluOpType
ENGS = ("tensor", "vector", "scalar", "gpsimd", "sync")

D = 1024
DFF = 2816
NFF = DFF // 128
KC = D // 128
EPS = 1e-6
TT = 512
SLOT = 2816


class DSem:
    __slots__ = ("dsem", "dcount")

    def __init__(self, sem):
        self.dsem = sem
        self.dcount = 0


class Buf:
    __slots__ = ("name", "w", "r", "ds", "excl")

    def __init__(self, name):
        self.name = name
        self.w = []
        self.r = []
        self.ds = {}
        self.excl = False


class MK:
    def __init__(self, nc, es):
        self.nc = nc
        self.es = es
        self.prog = {e: [] for e in ENGS}
        self.esem = {}
        self.ecount = {e: 0 for e in ENGS}
        self.waited = {e: {} for e in ENGS}
        for e in ENGS:
            self.esem[e] = es.enter_context(nc.semaphore("s_" + e))
        self.nbuf = 0
        self.ninst = 0
        self.reg = {}

    def buf(self, name=None):
        if name is not None and name in self.reg:
            return self.reg[name]
        self.nbuf += 1
        b = Buf(name or ("b_%d" % self.nbuf))
        self.reg[b.name] = b
        return b

    def barrier(self):
        for eng in ENGS:
            need = {}
            for o in ENGS:
                if self.ecount[o] > self.waited[eng].get(("e", o), 0):
                    need[("e", o)] = (self.esem[o], self.ecount[o])
            for b in self.reg.values():
                for q in b.ds.values():
                    if q.dcount > self.waited[eng].get(("d", id(q)), 0):
                        need[("d", id(q))] = (q.dsem, q.dcount)
            self._emit_waits(eng, need)

    def _dsem(self, b, kind):
        if kind not in b.ds:
            b.ds[kind] = DSem(self.es.enter_context(self.nc.semaphore("d%s_%s" % (kind, b.name))))
        return b.ds[kind]

    def _need(self, eng, reads, writes, acc=None):
        need = {}

        def add(ev):
            kind, ref, val = ev
            if kind == "e":
                key = ("e", ref)
                sem = self.esem[ref]
                v = val
            else:
                key = ("d", id(ref))
                sem = ref.dsem
                v = ref.dcount
            if self.waited[eng].get(key, 0) >= v:
                return
            cur = need.get(key)
            if cur is None or cur[1] < v:
                need[key] = (sem, v)

        for b in reads:
            for ev in b.w:
                add(ev)
        for b in writes:
            for ev in b.w:
                if acc is not None and b is acc and ev[0] == "e" and ev[1] == "tensor":
                    continue
                add(ev)
            for ev in b.r:
                add(ev)
        return need

    def _emit_waits(self, eng, need):
        for key, (sem, v) in need.items():
            self.waited[eng][key] = v
            self.prog[eng].append(lambda e, sem=sem, v=v: e.wait_ge(sem, v))

    def _record(self, ev, reads, writes):
        for b in writes:
            b.w = [ev]
            b.r = []
        for b in reads:
            if any(b is w for w in writes):
                continue
            b.r = [x for x in b.r if not (x[0] == ev[0] and x[1] is ev[1])] + [ev]

    def op(self, eng, fn, reads=(), writes=(), acc=None):
        xr = [b for b in reads if b.excl]
        if xr:
            reads = [b for b in reads if not b.excl]
            writes = list(writes) + [b for b in xr if not any(b is w for w in writes)]
        need = self._need(eng, reads, writes, acc)
        self._emit_waits(eng, need)
        self.ecount[eng] += 1
        sem = self.esem[eng]
        self.prog[eng].append(lambda e, fn=fn, sem=sem: fn(e).then_inc(sem, 1))
        ev = ("e", eng, self.ecount[eng])
        self._record(ev, reads, writes)
        self.ninst += 1

    def dma(self, eng, out, in_, reads=(), writes=(), prim=None, **kw):
        if prim is None:
            prim = writes[0] if writes else reads[0]
        need = self._need(eng, reads, writes)
        self._emit_waits(eng, need)
        q = self._dsem(prim, "sw" if eng == "gpsimd" else "hw")
        q.dcount += 16
        sem = q.dsem
        self.prog[eng].append(
            lambda e, out=out, in_=in_, sem=sem, kw=kw: e.dma_start(out=out, in_=in_, **kw).then_inc(sem, 16))
        ev = ("d", q, q.dcount)
        self._record(ev, reads, writes)
        self.ninst += 1

    def dma_batch(self, eng, items, reads=(), writes=(), prim=None):
        need = self._need(eng, reads, writes)
        self._emit_waits(eng, need)
        q = self._dsem(prim, "sw" if eng == "gpsimd" else "hw")
        sem = q.dsem
        for (out, in_) in items:
            q.dcount += 16
            self.prog[eng].append(
                lambda e, out=out, in_=in_, sem=sem: e.dma_start(out=out, in_=in_).then_inc(sem, 16))
            self.ninst += 1
        ev = ("d", q, q.dcount)
        self._record(ev, reads, writes)

    def wait_all(self, eng, bufs):
        need = self._need(eng, bufs, ())
        self._emit_waits(eng, need)

    def run_block(self):
        nc = self.nc
        with nc.Block() as block:
            @block.tensor
            def _(e):
                for f in self.prog["tensor"]:
                    f(e)

            @block.vector
            def _(e):
                for f in self.prog["vector"]:
                    f(e)

            @block.scalar
            def _(e):
                for f in self.prog["scalar"]:
                    f(e)

            @block.gpsimd
            def _(e):
                for f in self.prog["gpsimd"]:
                    f(e)

            @block.sync
            def _(e):
                for f in self.prog["sync"]:
                    f(e)


class Tile:
    __slots__ = ("t", "b")

    def __init__(self, t, b):
        self.t = t
        self.b = b


IN_OFF = {}
_o = 0
for _n, _s in (("a_q", 512), ("a_k", 512), ("a_v", 512), ("a_z", 512), ("a_b", 8), ("a_a", 8),
               ("b_q", 256), ("b_k", 256), ("b_v", 512), ("b_r", 512), ("b_g", 32),
               ("c_z", 512), ("c_x", 512), ("c_B", 256), ("c_C", 256), ("c_dt", 16), ("gate", 3072)):
    IN_OFF[_n] = (_o, _s)
    _o += _s
assert _o == 8256


def _unit(W, cols):
    K = W.shape[0]
    kc = K // 128
    cols = np.asarray(cols)
    sub = np.zeros((K, len(cols)), np.float32)
    ok = cols >= 0
    sub[:, ok] = W[:, cols[ok]]
    return np.ascontiguousarray(sub.reshape(kc, 128, len(cols)).transpose(1, 0, 2)).reshape(128, kc * len(cols))


def col_range(name, lo=0, hi=None):
    o, s = IN_OFF[name]
    if hi is None:
        hi = s
    return list(range(o + lo, o + hi))


def make_cfg():
    cfg = {}
    A_chunks = []
    for nm, n in (("a_q", 4), ("a_k", 4), ("a_v", 4), ("c_x", 4), ("c_B", 2), ("c_C", 2), ("b_q", 2), ("b_k", 2)):
        for c in range(n):
            A_chunks.append(col_range(nm, c * 128, (c + 1) * 128))
    A_units = [A_chunks[2 * i] + A_chunks[2 * i + 1] for i in range(len(A_chunks) // 2)]
    A_units.append(col_range("b_v", 0, 256))
    A_units.append(col_range("b_v", 256, 512))
    A_units.append(col_range("b_k"))
    A_units.append(col_range("b_g", 0, 16) + [-1] * 16 + col_range("b_g", 16, 32) + [-1] * 16
                   + col_range("a_b") + col_range("a_a") + col_range("c_dt"))
    cfg["A_units"] = A_units
    B_chunks = []
    for nm in ("a_z", "b_r", "c_z"):
        for c in range(4):
            B_chunks.append(col_range(nm, c * 128, (c + 1) * 128))
    for c in range(24):
        B_chunks.append(col_range("gate", c * 128, (c + 1) * 128))
    cfg["B_units"] = [B_chunks[2 * i] + B_chunks[2 * i + 1] for i in range(len(B_chunks) // 2)]
    return cfg


CFG = make_cfg()


def unit_plan(cfg):
    plan = []
    for f in range(2):
        for j in range(NFF):
            plan.append((("gu", f, j), KC * 256))
        for oc in range(KC):
            plan.append((("dn", f, oc), NFF * 128))
    for u in range(len(cfg["A_units"])):
        plan.append((("inA", u), KC * len(cfg["A_units"][u])))
    for u in range(len(cfg["B_units"])):
        plan.append((("inB", u), KC * len(cfg["B_units"][u])))
    for oc in range(KC):
        plan.append((("br", oc), 12 * 128))
    for u in range(4):
        plan.append((("wo", u), KC * 256))
    return plan


PLAN = unit_plan(CFG)
UOFF = {}
_o = 0
for _k, _s in PLAN:
    UOFF[_k] = (_o, _s)
    _o += _s
LAYER_W = _o


def pack_layer_weights(inp, l):
    out = np.empty((128, LAYER_W), np.float32)
    for key, size in PLAN:
        o, s = UOFF[key]
        if key[0] == "gu":
            _, f, j = key
            cols = list(range(j * 128, (j + 1) * 128))
            g = _unit(inp["ffn_w_gate"][l, f], cols).reshape(128, KC, 128)
            u = _unit(inp["ffn_w_up"][l, f], cols).reshape(128, KC, 128)
            blk = np.concatenate([g, u], axis=2).reshape(128, KC * 256)
        elif key[0] == "dn":
            _, f, oc = key
            blk = _unit(inp["ffn_w_down"][l, f], list(range(oc * 128, (oc + 1) * 128)))
        elif key[0] == "inA":
            blk = _unit(inp["w_in"][l], CFG["A_units"][key[1]])
        elif key[0] == "inB":
            blk = _unit(inp["w_in"][l], CFG["B_units"][key[1]])
        elif key[0] == "br":
            oc = key[1]
            parts = [_unit(inp["w_branch"][l, b], list(range(oc * 128, (oc + 1) * 128))).reshape(128, 4, 128)
                     for b in range(3)]
            blk = np.concatenate(parts, axis=1).reshape(128, 12 * 128)
        elif key[0] == "wo":
            u = key[1]
            blk = _unit(inp["w_out"][l], list(range(u * 256, (u + 1) * 256)))
        assert blk.shape == (128, s), (key, blk.shape, s)
        out[:, o:o + s] = blk
    return out


PV = {}
_o = 0
for _n, _s in (("ffn_norm0", 8), ("ffn_norm1", 8), ("mix_norm", 8), ("gdn_norm", 1), ("gla_norm", 1),
               ("ssd_norm", 4), ("ssd_d", 4), ("gdn_cw", 60), ("ssd_cw", 40), ("ssd_cb", 8)):
    PV[_n] = (_o, _s)
    _o += _s
NPV = _o

PR = {}
_o = 0
for _n, _s in (("gla_bg", 512), ("sm_bias", 32), ("sm_alog", 32)):
    PR[_n] = (_o, _s)
    _o += _s
NPR = _o


def pack_pvec(inp, l):
    out = np.zeros((128, NPV), np.float32)

    def put(name, vec):
        o, s = PV[name]
        out[:, o:o + s] = np.asarray(vec, np.float32).reshape(s, 128).T
    put("ffn_norm0", inp["ffn_norm"][l, 0])
    put("ffn_norm1", inp["ffn_norm"][l, 1])
    put("mix_norm", inp["mix_norm"][l])
    put("gdn_norm", inp["gdn_norm"][l])
    put("gla_norm", inp["gla_norm"][l])
    put("ssd_norm", inp["ssd_norm"][l])
    put("ssd_d", np.repeat(inp["ssd_d"][l], 64))
    o, s = PV["gdn_cw"]
    cw = inp["gdn_conv_w"][l]
    out[:, o:o + s] = cw.reshape(5, 12, 128).transpose(2, 1, 0).reshape(128, 60)
    o, s = PV["ssd_cw"]
    cw = inp["ssd_conv_w"][l]
    out[:, o:o + s] = cw.reshape(5, 8, 128).transpose(2, 1, 0).reshape(128, 40)
    put("ssd_cb", inp["ssd_conv_b"][l])
    return out


def pack_prow(inp, l):
    out = np.zeros((NPR,), np.float32)
    o, s = PR["gla_bg"]
    out[o:o + s] = inp["gla_b_g"][l].reshape(512)
    o, s = PR["sm_bias"]
    out[o + 8:o + 16] = inp["gdn_dt_bias"][l].reshape(8)
    out[o + 16:o + 32] = inp["ssd_dt_bias"][l].reshape(16)
    o, s = PR["sm_alog"]
    out[o + 8:o + 16] = inp["gdn_a_log"][l].reshape(8)
    out[o + 16:o + 32] = inp["ssd_a_log"][l].reshape(16)
    return out


def pack_wgup(inp, l):
    out = np.zeros((64, 256), np.float32)
    out[0:16] = inp["gla_w_gup"][l, 0]
    out[32:48] = inp["gla_w_gup"][l, 1]
    return out
T2 = 256
TE2 = T2 + 4
NSUB2 = T2 // 128


class Builder:
    def __init__(self, L, depth, mix=("gdn", "gla", "ssd")):
        self.L = L
        self.depth = depth
        self.mix = tuple(mix)
        self.NT = L // TT
        self.NT2 = L // T2
        self.NST = L // 128
        self.NCH = L // 64

    def sb(self, name, shape, dt):
        es = self.scope if self.scope is not None else self.es
        self.tcount = getattr(self, "tcount", 0) + 1
        t = es.enter_context(self.nc.sbuf_tensor("%s_%d" % (name, self.tcount), list(shape), dt))
        return Tile(t, self.k.buf(name))

    def begin_scope(self):
        self.scope = ExitStack()
        self.scope.__enter__()

    def end_scope(self):
        self.k.barrier()
        self.scope.__exit__(None, None, None)
        self.scope = None

    def dram(self, name, shape, dt):
        return self.nc.dram_tensor(name, list(shape), dt).ap()

    def build(self):
        nc = bass.Bass("TRN2", target_bir_lowering=False)
        self.nc = nc
        L, depth = self.L, self.depth
        self.xin = nc.dram_tensor("xin", [D, L], F32, kind="ExternalInput").ap()
        self.wf32 = nc.dram_tensor("wf32", [depth, 128, LAYER_W], F32, kind="ExternalInput").ap()
        self.pvec = nc.dram_tensor("pvec", [depth, 128, NPV], F32, kind="ExternalInput").ap()
        self.prow = nc.dram_tensor("prow", [depth, NPR], F32, kind="ExternalInput").ap()
        self.wgup = nc.dram_tensor("wgup", [depth, 64, 256], F32, kind="ExternalInput").ap()
        self.fnorm = nc.dram_tensor("fnorm", [128, KC], F32, kind="ExternalInput").ap()
        self.yout = nc.dram_tensor("yout", [D, L], F32, kind="ExternalOutput").ap()
        self.wbf = self.dram("wbf", [depth, 128, LAYER_W], BF16)
        self.xres = self.dram("xres", [D, L], F32)
        NST = self.NST
        if "gla" in self.mix:
            self.gla_qg = self.dram("gla_qg", [2, 256, L], BF16)
            self.gla_kmg = self.dram("gla_kmg", [2, 256, L], BF16)
            self.gla_kd = self.dram("gla_kd", [2, L, 256], BF16)
            self.gla_v = self.dram("gla_v", [L, 512], BF16)
            self.gla_o = self.dram("gla_o", [2, 512, L], F32)
        if "ssd" in self.mix:
            self.ssd_cexp = self.dram("ssd_cexp", [2, 8 * 128, L], BF16)
            self.ssd_B = self.dram("ssd_B", [L, 256], BF16)
            self.ssd_xw = self.dram("ssd_xw", [2, L, 512], BF16)
            self.ssd_ydiag = self.dram("ssd_ydiag", [512, L], F32)
            self.ssd_yoff = self.dram("ssd_yoff", [2, 512, L], F32)
        if "gdn" in self.mix:
            self.gdn_wn = self.dram("gdn_wn", [2, 512, L], BF16)
            self.gdn_u = self.dram("gdn_u", [2, L, 512], F32)
            self.gdn_qg = self.dram("gdn_qg", [2, 512, L], BF16)
            self.gdn_kg = self.dram("gdn_kg", [2, L, 512], BF16)
            self.gdn_att = self.dram("gdn_att", [2, NST * 128, 512], BF16)
            self.gdn_o = self.dram("gdn_o", [2, 512, L], F32)
        with ExitStack() as es:
            k = MK(nc, es)
            self.k = k
            self.es = es
            self.scope = None
            self.setup_common()
            self.cast_weights()
            for l in range(depth + 1):
                self.pass1(l)
                if l < depth and self.mix:
                    self.pass2(l)
                    self.scan(l)
            k.wait_all("gpsimd", self.out_bufs)
            k.run_block()
        return nc

    def setup_common(self):
        k = self.k
        self.out_bufs = []
        self.ones_bf = self.sb("ones_bf", [128, 128], BF16)
        k.op("gpsimd", lambda e: e.memset(self.ones_bf.t[:], 1.0), writes=[self.ones_bf.b])
        self.epsb = self.sb("epsb", [128, 1], F32)
        k.op("gpsimd", lambda e: e.memset(self.epsb.t[:], EPS), writes=[self.epsb.b])
        self.oneb = self.sb("oneb", [128, 1], F32)
        k.op("gpsimd", lambda e: e.memset(self.oneb.t[:], 1.0), writes=[self.oneb.b])
        self.U = [self.sb("Ublk%d" % d, [128, 128], F32) for d in range(2)]
        self.SU = [self.sb("SUblk%d" % d, [128, 128], F32) for d in range(2)]
        self.ident = self.sb("ident", [128, 128], F32)
        self.ident_bf = self.sb("ident_bf", [128, 128], BF16)
        self.bones = self.sb("bones", [128, 128], F32)

        def tri(tile, ge, strict):
            k.op("gpsimd", lambda e: e.memset(tile.t[:], 0.0), writes=[tile.b])
            for b in range(2):
                blk = tile.t[b * 64:(b + 1) * 64, b * 64:(b + 1) * 64]
                k.op("gpsimd", lambda e, blk=blk: e.memset(blk, 1.0), writes=[tile.b])
                sgn = 1 if ge else -1
                base = -1 if strict else 0
                k.op("gpsimd", lambda e, blk=blk, sgn=sgn, base=base: e.affine_select(
                    out=blk, in_=blk, pattern=[[sgn, 64]], compare_op=ALU.is_ge, fill=0.0, base=base,
                    channel_multiplier=-sgn), reads=[tile.b], writes=[tile.b])
        tri(self.U[0], True, False)
        tri(self.U[1], False, False)
        tri(self.SU[0], False, True)
        tri(self.SU[1], True, True)
        k.op("gpsimd", lambda e: e.memset(self.bones.t[:], 0.0), writes=[self.bones.b])
        for b in range(2):
            blk = self.bones.t[b * 64:(b + 1) * 64, b * 64:(b + 1) * 64]
            k.op("gpsimd", lambda e, blk=blk: e.memset(blk, 1.0), writes=[self.bones.b])
        k.op("gpsimd", lambda e: e.memset(self.ident.t[:], 1.0), writes=[self.ident.b])
        k.op("gpsimd", lambda e: e.affine_select(out=self.ident.t[:], in_=self.ident.t[:], pattern=[[-1, 128]],
                                                 compare_op=ALU.is_equal, fill=0.0, base=0, channel_multiplier=1),
             reads=[self.ident.b], writes=[self.ident.b])
        k.op("gpsimd", lambda e: e.tensor_copy(out=self.ident_bf.t[:], in_=self.ident.t[:]),
             reads=[self.ident.b], writes=[self.ident_bf.b])
        self.cind = [self.sb("cind%d" % c, [128, 128], F32) for c in range(2)]
        for c in range(2):
            k.op("gpsimd", lambda e, c=c: e.memset(self.cind[c].t[:], 0.0), writes=[self.cind[c].b])
            k.op("gpsimd", lambda e, c=c: e.memset(self.cind[c].t[c * 64:(c + 1) * 64, :], 1.0), writes=[self.cind[c].b])
        self.maskneg = [self.sb("maskneg%d" % d, [128, 128], F32) for d in range(2)]
        for d in range(2):
            k.op("vector", lambda e, d=d: e.tensor_scalar(out=self.maskneg[d].t[:], in0=self.U[d].t[:], scalar1=30000.0,
                                                          scalar2=-30000.0, op0=ALU.mult, op1=ALU.add),
                 reads=[self.U[d].b], writes=[self.maskneg[d].b])
        self.NSLOT = 8
        self.wslots = [self.sb("wslot%d" % i, [128, SLOT], BF16) for i in range(self.NSLOT)]
        self.wnext = 0
        self.NPS = 7
        self.pbanks = []
        for i in range(self.NPS):
            t = self.es.enter_context(self.nc.psum_tensor("pb%d" % i, [128, TT], F32))
            self.pbanks.append(Tile(t, k.buf("pb%d" % i)))
            self.pbanks[-1].b.excl = True
        self.pnext = 0
        t = self.es.enter_context(self.nc.psum_tensor("ptr", [128, 1024], BF16))
        self.ptr = Tile(t, k.buf("ptr"))
        self.ptr.b.excl = True
        self.pv = self.sb("pv", [128, self.depth, NPV], F32)
        self.fn = self.sb("fn", [128, KC], F32)
        self.fn.b = self.pv.b
        k.dma_batch("sync", [(self.pv.t[:, l, :], self.pvec[l]) for l in range(self.depth)] + [(self.fn.t[:], self.fnorm)],
                    writes=[self.pv.b], prim=self.pv.b)
        NCH = self.NCH
        if "gla" in self.mix:
            self.egl_gla = self.sb("egl_gla", [128, 2, 2, NCH], F32)
        if "ssd" in self.mix or "gdn" in self.mix:
            self.egl = self.sb("egl", [128, NCH, 24], F32)
        self.pe_bufs = [k.buf("pe%d" % i) for i in range(32)]
        self.pe_n = 0

    def pvc(self, l, name, i=0, n=1):
        o, s = PV[name]
        return self.pv.t[:, l, o + i:o + i + n]

    def cast_weights(self):
        k = self.k
        CH = 65536
        self.cast_chunks = []
        for l in range(self.depth):
            lo = 0
            lb = k.buf("wcL%d" % l)
            while lo < LAYER_W:
                hi = min(LAYER_W, lo + CH)
                b = k.buf("wc0_%d" % lo) if l == 0 else lb
                k.dma("gpsimd", self.wbf[l, :, lo:hi], self.wf32[l, :, lo:hi], writes=[b], max_dma_last_dim=4096)
                self.cast_chunks.append((l, lo, hi, b))
                lo = hi

    def cast_deps(self, l, o, s):
        out = []
        for (ll, lo, hi, b) in self.cast_chunks:
            if ll == l and lo < o + s and hi > o and b not in out:
                out.append(b)
        return out

    def load_unit(self, l, key):
        k = self.k
        o, s = UOFF[key]
        slot = self.wslots[self.wnext]
        self.wnext = (self.wnext + 1) % self.NSLOT
        k.dma("sync", slot.t[:, 0:s], self.wbf[l, :, o:o + s], reads=self.cast_deps(l, o, s), writes=[slot.b],
              prim=slot.b)
        return slot

    def pbank(self):
        p = self.pbanks[self.pnext]
        self.pnext = (self.pnext + 1) % self.NPS
        return p

    def mm(self, ps, out_ap, lh, rh, reads, start=True, stop=True, acc=False):
        self.k.op("tensor", lambda e: e.matmul(out_ap, lhsT=lh, rhs=rh, start=start, stop=stop),
                  reads=reads, writes=[ps.b], acc=(ps.b if acc else None))

    def mm_group(self, ps, out_ap, pairs, reads):
        n = len(pairs)
        for i, (lh, rh) in enumerate(pairs):
            self.mm(ps, out_ap, lh, rh, reads, start=(i == 0), stop=(i == n - 1), acc=(i > 0))

    def ts(self, eng, out, in0, s1, s2, op0, op1, reads, writes):
        if op1 is None:
            self.k.op(eng, lambda e: e.tensor_scalar(out=out, in0=in0, scalar1=s1, scalar2=None, op0=op0),
                      reads=reads, writes=writes)
        else:
            self.k.op(eng, lambda e: e.tensor_scalar(out=out, in0=in0, scalar1=s1, scalar2=s2, op0=op0, op1=op1),
                      reads=reads, writes=writes)

    def ms(self, ap, val, writes):
        self.k.op("gpsimd", lambda e: e.memset(ap, val), reads=(), writes=writes)

    def tr(self, out, in_, reads, ps):
        idt = self.ident_bf
        self.k.op("tensor", lambda e: e.transpose(out, in_, idt.t[:]), reads=list(reads) + [idt.b], writes=[ps.b])

    def act(self, out, in_, func, reads, writes, **kw):
        self.k.op("scalar", lambda e: e.activation(out=out, in_=in_, func=func, **kw), reads=reads, writes=writes)

    def tt(self, eng, out, in0, in1, op, reads, writes):
        self.k.op(eng, lambda e: e.tensor_tensor(out=out, in0=in0, in1=in1, op=op), reads=reads, writes=writes)

    def stt(self, out, in0, scalar, in1, op0, op1, reads, writes):
        self.k.op("vector", lambda e: e.scalar_tensor_tensor(out=out, in0=in0, scalar=scalar, in1=in1, op0=op0, op1=op1),
                  reads=reads, writes=writes)

    def cp(self, eng, out, in_, reads, writes):
        if eng == "scalar":
            self.k.op(eng, lambda e: e.copy(out=out, in_=in_), reads=reads, writes=writes)
        else:
            self.k.op(eng, lambda e: e.tensor_copy(out=out, in_=in_), reads=reads, writes=writes)

    def rstd_from_ss(self, ps_ap, psb, out, n, cols=None):
        self.act(out.t[:] if cols is None else cols, ps_ap, AF.Ln, [psb, self.epsb.b], [out.b], scale=1.0 / n,
                 bias=self.epsb.t[:, 0:1])
        o = out.t[:] if cols is None else cols
        self.act(o, o, AF.Exp, [out.b], [out.b], scale=-0.5)

    def rmsnorm(self, x, gcol, out_h, W, sq, rstd):
        k = self.k
        self.act(sq.t[:, :, 0:W], x.t[:, :, 0:W], AF.Square, [x.b], [sq.b])
        for (lo, hi) in ((0, min(W, 512)), (512, W)):
            if hi <= lo:
                continue
            ps = self.pbank()
            self.mm_group(ps, ps.t[:, 0:hi - lo], [(self.ones_bf.t[:], sq.t[:, kc, lo:hi]) for kc in range(KC)],
                          reads=[self.ones_bf.b, sq.b])
            self.rstd_from_ss(ps.t[:, 0:hi - lo], ps.b, rstd, D, cols=rstd.t[:, lo:hi])
        for kc in range(KC):
            self.stt(out_h.t[:, kc, 0:W], x.t[:, kc, 0:W], gcol[:, kc:kc + 1], rstd.t[:, 0:W], ALU.mult, ALU.mult,
                     [x.b, rstd.b, self.pv.b, self.fn.b], [out_h.b])

    def ffn(self, l, f, x):
        k = self.k
        o, s = PV["ffn_norm%d" % f]
        self.rmsnorm(x, self.pv.t[:, l, o:o + s], self.hT, TT, self.sq, self.rstd)
        hT, hid = self.hT, self.hid
        for j in range(NFF):
            w = self.load_unit(l, ("gu", f, j))
            wv = w.t[:, 0:KC * 256].rearrange("p (k c) -> p k c", k=KC)
            pg, pu = self.pbank(), self.pbank()
            self.mm_group(pg, pg.t[:], [(wv[:, kc, 0:128], hT.t[:, kc, :]) for kc in range(KC)], reads=[w.b, hT.b])
            self.mm_group(pu, pu.t[:], [(wv[:, kc, 128:256], hT.t[:, kc, :]) for kc in range(KC)], reads=[w.b, hT.b])
            sg = self.sg[j % 2]
            self.act(sg.t[:], pg.t[:], AF.Silu, [pg.b], [sg.b])
            self.tt("vector", hid.t[:, j, :], sg.t[:], pu.t[:], ALU.mult, [sg.b, pu.b], [hid.b])
        for oc in range(KC):
            w = self.load_unit(l, ("dn", f, oc))
            wv = w.t[:, 0:NFF * 128].rearrange("p (k c) -> p k c", k=NFF)
            po = self.pbank()
            self.mm_group(po, po.t[:], [(wv[:, j, :], hid.t[:, j, :]) for j in range(NFF)], reads=[w.b, hid.b])
            self.stt(x.t[:, oc, :], po.t[:], 0.5, x.t[:, oc, :], ALU.mult, ALU.add, [po.b, x.b], [x.b])

    def pass1(self, l):
        k = self.k
        first = (l == 0)
        last = (l == self.depth)
        self.begin_scope()
        self.xT = [self.sb("xT%d" % i, [128, KC, TT], F32) for i in range(2)]
        self.hT = self.sb("hT", [128, KC, TT], BF16)
        self.sq = self.sb("sq", [128, KC, TT], BF16)
        self.rstd = self.sb("rstd", [128, TT], F32)
        self.hid = self.sb("hid", [128, 24, TT], BF16)
        self.sg = [self.sb("sg%d" % i, [128, TT], F32) for i in range(2)]
        if not first and self.mix:
            self.alloc_B()
        src = self.xin if first else self.xres
        for t in range(self.NT):
            x = self.xT[t % 2]
            t0 = t * TT
            k.dma("sync", x.t[:], src[:, t0:t0 + TT].rearrange("(k p) t -> p k t", p=128), writes=[x.b])
            if not first:
                if self.mix:
                    self.phase_B(l - 1, t, x)
                self.ffn(l - 1, 1, x)
            if not last:
                self.ffn(l, 0, x)
                k.dma("gpsimd", self.xres[:, t0:t0 + TT].rearrange("(k p) t -> p k t", p=128), x.t[:],
                      reads=[x.b], prim=x.b)
            else:
                o, s = 0, KC
                self.final_norm(x)
                k.dma("gpsimd", self.yout[:, t0:t0 + TT].rearrange("(k p) t -> p k t", p=128), x.t[:],
                      reads=[x.b], prim=x.b)
                if x.b not in self.out_bufs:
                    self.out_bufs.append(x.b)
        self.end_scope()

    def final_norm(self, x):
        sq, rstd = self.sq, self.rstd
        self.act(sq.t[:], x.t[:], AF.Square, [x.b], [sq.b])
        ps = self.pbank()
        self.mm_group(ps, ps.t[:], [(self.ones_bf.t[:], sq.t[:, kc, :]) for kc in range(KC)],
                      reads=[self.ones_bf.b, sq.b])
        self.rstd_from_ss(ps.t[:], ps.b, rstd, D)
        for kc in range(KC):
            self.stt(x.t[:, kc, :], x.t[:, kc, :], self.fn.t[:, kc:kc + 1], rstd.t[:], ALU.mult, ALU.mult,
                     [x.b, rstd.b, self.fn.b], [x.b])

    def alloc_B(self):
        onv = self.hid.t[:].rearrange("p a b -> p (a b)").bitcast(F32).rearrange("p (a b) -> p a b", a=12)
        self.on = Tile(None, self.hid.b)
        self.on_v = onv
        self.ybr = self.sb("ybr", [128, 12, TT], BF16)
        self.ld = [self.sb("ldB%d" % i, [128, TT], F32) for i in range(4)]
        self.ldn = 0
        self.sqb = self.sb("sqb", [128, 4, TT], BF16)
        self.rsb = self.sb("rsb", [128, TT], F32)
        self.rsb4 = self.sb("rsb4", [128, 4, TT], F32)
        self.mg = [self.sb("mg%d" % i, [128, TT], F32) for i in range(2)]
        self.mtmp = [self.sb("mtmp%d" % i, [128, TT], F32) for i in range(2)]
        self.th = [self.sb("th%d" % i, [128, TT], F32) for i in range(3)]
        self.thn = 0
        self.merged = self.sq

    def ldB(self, src_ap):
        t = self.ld[self.ldn % 4]
        self.ldn += 1
        self.k.dma("sync", t.t[:], src_ap, writes=[t.b])
        return t

    def phase_B(self, l, t, x):
        k = self.k
        t0 = t * TT
        ts = slice(t0, t0 + TT)
        on, ybr, hT = self.on, self.ybr, self.hT
        onv = self.on_v
        self.rmsnorm(x, self.pvc(l, "mix_norm", 0, 8), hT, TT, self.sq, self.rstd)
        if "ssd" in self.mix:
            for u in range(2):
                w = self.load_unit(l, ("inB", 4 + u))
                wv = w.t[:, 0:KC * 256].rearrange("p (k c) -> p k c", k=KC)
                for cc in range(2):
                    c = 2 * u + cc
                    a = self.ldB(self.ssd_ydiag[c * 128:(c + 1) * 128, ts])
                    b = self.ldB(self.ssd_yoff[0, c * 128:(c + 1) * 128, ts])
                    d = self.ldB(self.ssd_yoff[1, c * 128:(c + 1) * 128, ts])
                    self.tt("gpsimd", a.t[:], a.t[:], b.t[:], ALU.add, [a.b, b.b], [a.b])
                    self.tt("gpsimd", a.t[:], a.t[:], d.t[:], ALU.add, [a.b, d.b], [a.b])
                    pz = self.pbank()
                    self.mm_group(pz, pz.t[:], [(wv[:, kc, cc * 128:(cc + 1) * 128], hT.t[:, kc, :]) for kc in range(KC)],
                                  reads=[w.b, hT.b])
                    sz = self.th[self.thn % 3]
                    self.thn += 1
                    self.act(sz.t[:], pz.t[:], AF.Silu, [pz.b], [sz.b])
                    self.tt("vector", onv[:, 8 + c, :], a.t[:], sz.t[:], ALU.mult, [a.b, sz.b], [on.b])
        for m, base, src in (("gdn", 0, getattr(self, "gdn_o", None)), ("gla", 4, getattr(self, "gla_o", None))):
            if m not in self.mix:
                continue
            for h in range(4):
                a = self.ldB(src[0, h * 128:(h + 1) * 128, ts])
                b = self.ldB(src[1, h * 128:(h + 1) * 128, ts])
                self.tt("vector" if h % 2 else "gpsimd", onv[:, base + h, :], a.t[:], b.t[:], ALU.add, [a.b, b.b], [on.b])
            self.act(self.sqb.t[:], onv[:, base:base + 4, :], AF.Square, [on.b], [self.sqb.b])
            pss = []
            for h in range(4):
                ps = self.pbank()
                self.mm(ps, ps.t[:], self.ones_bf.t[:], self.sqb.t[:, h, :], [self.ones_bf.b, self.sqb.b])
                pss.append(ps)
            for h in range(4):
                self.act(self.rsb4.t[:, h, :], pss[h].t[:], AF.Ln, [pss[h].b, self.epsb.b], [self.rsb4.b], scale=1.0 / 128,
                         bias=self.epsb.t[:, 0:1])
            self.act(self.rsb4.t[:], self.rsb4.t[:], AF.Exp, [self.rsb4.b], [self.rsb4.b], scale=-0.5)
            for h in range(4):
                self.stt(onv[:, base + h, :], onv[:, base + h, :], self.pvc(l, m + "_norm"), self.rsb4.t[:, h, :], ALU.mult,
                         ALU.mult, [on.b, self.rsb4.b, self.pv.b], [on.b])
        if "ssd" in self.mix:
            for g in range(2):
                self.act(self.sqb.t[:, 0:2, :], onv[:, 8 + 2 * g:10 + 2 * g, :], AF.Square, [on.b], [self.sqb.b])
                ps = self.pbank()
                self.mm_group(ps, ps.t[:], [(self.ones_bf.t[:], self.sqb.t[:, cc, :]) for cc in range(2)],
                              reads=[self.ones_bf.b, self.sqb.b])
                self.rstd_from_ss(ps.t[:], ps.b, self.rsb, 256)
                for cc in range(2):
                    c = 2 * g + cc
                    self.stt(ybr.t[:, 8 + c, :], onv[:, 8 + c, :], self.pvc(l, "ssd_norm", c), self.rsb.t[:],
                             ALU.mult, ALU.mult, [on.b, self.rsb.b, self.pv.b], [ybr.b])
        for m, base, u0 in (("gdn", 0, 0), ("gla", 4, 2)):
            if m not in self.mix:
                continue
            for u in range(2):
                w = self.load_unit(l, ("inB", u0 + u))
                wv = w.t[:, 0:KC * 256].rearrange("p (k c) -> p k c", k=KC)
                for cc in range(2):
                    h = 2 * u + cc
                    pz = self.pbank()
                    self.mm_group(pz, pz.t[:], [(wv[:, kc, cc * 128:(cc + 1) * 128], hT.t[:, kc, :]) for kc in range(KC)],
                                  reads=[w.b, hT.b])
                    sz = self.th[self.thn % 3]
                    self.thn += 1
                    self.act(sz.t[:], pz.t[:], AF.Silu, [pz.b], [sz.b])
                    self.tt("vector", ybr.t[:, base + h, :], onv[:, base + h, :], sz.t[:], ALU.mult, [on.b, sz.b], [ybr.b])
        mixl = [i for i, m in enumerate(("gdn", "gla", "ssd")) if m in self.mix]
        for op_ in range(4):
            wbs = [self.load_unit(l, ("br", 2 * op_ + cc)) for cc in range(2)]
            for bi, b in enumerate(mixl):
                wg_ = self.load_unit(l, ("inB", 6 + b * 4 + op_))
                wgv = wg_.t[:, 0:KC * 256].rearrange("p (k c) -> p k c", k=KC)
                for cc in range(2):
                    wbv = wbs[cc].t[:, 0:12 * 128].rearrange("p (k c) -> p k c", k=12)
                    pgt = self.pbank()
                    self.mm_group(pgt, pgt.t[:], [(wgv[:, kc, cc * 128:(cc + 1) * 128], hT.t[:, kc, :]) for kc in range(KC)],
                                  reads=[wg_.b, hT.b])
                    th = self.th[self.thn % 3]
                    self.thn += 1
                    self.act(th.t[:], pgt.t[:], AF.Tanh, [pgt.b], [th.b], scale=0.5)
                    pbr = self.pbank()
                    self.mm_group(pbr, pbr.t[:], [(wbv[:, b * 4 + kk, :], ybr.t[:, b * 4 + kk, :]) for kk in range(4)],
                                  reads=[wbs[cc].b, ybr.b])
                    if bi == 0:
                        self.stt(self.mg[cc].t[:], th.t[:], 1.0, pbr.t[:], ALU.add, ALU.mult, [th.b, pbr.b], [self.mg[cc].b])
                    else:
                        tmp = self.mtmp[cc]
                        self.stt(tmp.t[:], th.t[:], 1.0, pbr.t[:], ALU.add, ALU.mult, [th.b, pbr.b], [tmp.b])
                        self.tt("gpsimd", self.mg[cc].t[:], self.mg[cc].t[:], tmp.t[:], ALU.add,
                                [self.mg[cc].b, tmp.b], [self.mg[cc].b])
            for cc in range(2):
                oc = 2 * op_ + cc
                self.act(self.merged.t[:, oc, :], self.mg[cc].t[:], AF.Copy, [self.mg[cc].b], [self.merged.b], scale=0.5)
        for u in range(4):
            w = self.load_unit(l, ("wo", u))
            wv = w.t[:, 0:KC * 256].rearrange("p (k c) -> p k c", k=KC)
            for cc in range(2):
                oc = 2 * u + cc
                po = self.pbank()
                self.mm_group(po, po.t[:], [(wv[:, kc, cc * 128:(cc + 1) * 128], self.merged.t[:, kc, :]) for kc in range(KC)],
                              reads=[w.b, self.merged.b])
                self.tt("vector", x.t[:, oc, :], x.t[:, oc, :], po.t[:], ALU.add, [x.b, po.b], [x.b])

    def pass2(self, l):
        k = self.k
        L = self.L
        self.begin_scope()
        self.xE = self.sb("xE", [128, KC, TE2], F32)
        self.hE = self.sb("hE", [128, KC, TE2], BF16)
        self.sqE = self.sb("sqE", [128, KC, TE2], BF16)
        self.rstdE = self.sb("rstdE", [128, TE2], F32)
        self.prb = self.sb("prb", [128, NPR], F32)
        k.dma("sync", self.prb.t[:], self.prow[l:l + 1, :].to_broadcast([128, NPR]), writes=[self.prb.b])
        self.lrT = self.sb("lrT", [64, T2], BF16)
        self.sm = self.sb("sm", [128, NSUB2, 32], F32)
        if "gla" in self.mix:
            self.alloc_p2_gla(l)
        if "ssd" in self.mix:
            self.alloc_p2_ssd(l)
        if "gdn" in self.mix:
            self.alloc_p2_gdn(l)
        xE = self.xE
        self.hEs = [self.hE, self.sb("hE2", [128, KC, TE2], BF16)]
        self.pnext = 0
        if "ssd" in self.mix or "gdn" in self.mix:
            self.alloc_p2_common(l)

        def front(t):
            hE_ = self.hEs[t % 2]
            t0 = t * T2
            lo, hi = t0 - 2, t0 + T2 + 2
            j0, j1 = 0, TE2
            if lo < 0:
                self.ms(xE.t[:, :, 0:2], 0.0, [xE.b])
                j0, lo = 2, 0
            if hi > L:
                self.ms(xE.t[:, :, TE2 - 2:TE2], 0.0, [xE.b])
                j1, hi = TE2 - 2, L
            k.dma("sync", xE.t[:, :, j0:j1], self.xres[:, lo:hi].rearrange("(k p) t -> p k t", p=128), writes=[xE.b])
            self.rmsnorm(xE, self.pvc(l, "mix_norm", 0, 8), hE_, TE2, self.sqE, self.rstdE)

        front(0)
        for t in range(self.NT2):
            t0 = t * T2
            hE = self.hEs[t % 2]
            self.hE = hE
            w = self.load_unit(l, ("inA", 15))
            wv = w.t[:, 0:KC * 96].rearrange("p (k c) -> p k c", k=KC)
            ps = self.pbank()
            self.mm_group(ps, ps.t[0:64, 0:T2], [(wv[:, kc, 0:64], hE.t[:, kc, 2:2 + T2]) for kc in range(KC)], reads=[w.b, hE.b])
            self.cp("scalar", self.lrT.t[:], ps.t[0:64, 0:T2], [ps.b], [self.lrT.b])
            ps = self.pbank()
            for s in range(NSUB2):
                self.mm_group(ps, ps.t[:, s * 32:(s + 1) * 32],
                              [(hE.t[:, kc, 2 + s * 128:2 + (s + 1) * 128], wv[:, kc, 64:96]) for kc in range(KC)],
                              reads=[w.b, hE.b])
            o, _ = PR["sm_bias"]
            self.tt("vector", self.sm.t[:], ps.t[:, 0:NSUB2 * 32].rearrange("p (s c) -> p s c", s=NSUB2),
                    self.prb.t[:, None, o:o + 32].to_broadcast([128, NSUB2, 32]), ALU.add, [ps.b, self.prb.b], [self.sm.b])
            if t + 1 < self.NT2:
                front(t + 1)
            if "ssd" in self.mix or "gdn" in self.mix:
                self.p2_smalls_post(l, t)
            if "gla" in self.mix:
                self.p2_gla(l, t)
            if "ssd" in self.mix:
                self.p2_ssd(l, t)
            if "gdn" in self.mix:
                self.p2_gdn(l, t)
        self.NPS = 7
        self.end_scope()

    def alloc_p2_gla(self, l):
        k = self.k
        self.g_qT = self.sb("g_qT", [128, 2, T2], BF16)
        self.g_kT = self.sb("g_kT", [128, 2, T2], BF16)
        self.g_ktm = self.sb("g_ktm", [128, NSUB2, 256], F32)
        self.g_v = self.sb("g_v", [128, NSUB2, 512], BF16)
        self.g_qg = [self.sb("g_qg%d" % d, [128, 2, T2], BF16) for d in range(2)]
        self.g_kmg = [self.sb("g_kmg%d" % d, [128, 2, T2], BF16) for d in range(2)]
        self.g_kd = [self.sb("g_kd%d" % d, [128, NSUB2, 256], BF16) for d in range(2)]
        self.g_l = self.sb("g_l", [128, 256], F32)
        self.g_eg = [self.sb("g_eg%d" % i, [128, 128], F32) for i in range(2)]
        self.g_emg = [self.sb("g_emg%d" % i, [128, 128], F32) for i in range(2)]
        self.g_ed = self.sb("g_ed", [128, 256], F32)
        self.wg32 = self.sb("wg32", [64, 256], F32)
        self.wgb = self.sb("wgb", [64, 256], BF16)
        k.dma("sync", self.wg32.t[:], self.wgup[l], writes=[self.wg32.b])
        self.cp("vector", self.wgb.t[:], self.wg32.t[:], [self.wg32.b], [self.wgb.b])

    def p2_gla(self, l, t):
        k = self.k
        hE = self.hE
        t0 = t * T2
        for ui, dst in ((10, self.g_qT), (11, self.g_kT)):
            w = self.load_unit(l, ("inA", ui))
            wv = w.t[:, 0:KC * 256].rearrange("p (k c) -> p k c", k=KC)
            for c in range(2):
                ps = self.pbank()
                self.mm_group(ps, ps.t[:, 0:T2], [(wv[:, kc, c * 128:(c + 1) * 128], hE.t[:, kc, 2:2 + T2]) for kc in range(KC)],
                              reads=[w.b, hE.b])
                self.cp("scalar", dst.t[:, c, :], ps.t[:, 0:T2], [ps.b], [dst.b])
        w0 = self.load_unit(l, ("inA", 12))
        w1 = self.load_unit(l, ("inA", 13))
        w2 = self.load_unit(l, ("inA", 14))
        wv0 = w0.t[:, 0:KC * 256].rearrange("p (k c) -> p k c", k=KC)
        wv1 = w1.t[:, 0:KC * 256].rearrange("p (k c) -> p k c", k=KC)
        wv2 = w2.t[:, 0:KC * 256].rearrange("p (k c) -> p k c", k=KC)
        for s in range(NSUB2):
            hs = [hE.t[:, kc, 2 + s * 128:2 + (s + 1) * 128] for kc in range(KC)]
            ps = self.pbank()
            self.mm_group(ps, ps.t[:, 0:256], [(hs[kc], wv0[:, kc, :]) for kc in range(KC)], reads=[w0.b, hE.b])
            self.mm_group(ps, ps.t[:, 256:512], [(hs[kc], wv1[:, kc, :]) for kc in range(KC)], reads=[w1.b, hE.b])
            self.cp("scalar", self.g_v.t[:, s, :], ps.t[:], [ps.b], [self.g_v.b])
            ps = self.pbank()
            self.mm_group(ps, ps.t[:, 0:256], [(hs[kc], wv2[:, kc, :]) for kc in range(KC)], reads=[w2.b, hE.b])
            self.cp("vector", self.g_ktm.t[:, s, :], ps.t[:, 0:256], [ps.b], [self.g_ktm.b])
        ob, _ = PR["gla_bg"]
        for s in range(NSUB2):
            ss = slice(s * 128, (s + 1) * 128)
            ch0 = (t0 + s * 128) // 64
            for d in range(2):
                ps = self.pbank()
                self.mm(ps, ps.t[:, 0:256], self.lrT.t[d * 32:d * 32 + 16, ss], self.wgb.t[d * 32:d * 32 + 16, :],
                        [self.lrT.b, self.wgb.b])
                gl = self.g_l
                self.tt("vector", gl.t[:], ps.t[:, 0:256], self.prb.t[:, ob + d * 256:ob + (d + 1) * 256], ALU.add,
                        [ps.b, self.prb.b], [gl.b])
                self.act(gl.t[:], gl.t[:], AF.Exp, [gl.b], [gl.b], scale=-1.0)
                self.act(gl.t[:], gl.t[:], AF.Ln, [gl.b, self.oneb.b], [gl.b], bias=self.oneb.t[:, 0:1])
                for c in range(2):
                    ps2 = self.pbank()
                    self.mm(ps2, ps2.t[:, 0:128], gl.t[:, c * 128:(c + 1) * 128], self.U[d].t[:], [gl.b, self.U[d].b])
                    eg, emg = self.g_eg[c], self.g_emg[c]
                    self.act(eg.t[:], ps2.t[:, 0:128], AF.Exp, [ps2.b], [eg.b], scale=-1.0 / 16)
                    self.act(emg.t[:], ps2.t[:, 0:128], AF.Exp, [ps2.b], [emg.b], scale=1.0 / 16)
                    self.stt(self.g_qg[d].t[:, c, ss], self.g_qT.t[:, c, ss], 0.125, eg.t[:], ALU.mult, ALU.mult,
                             [self.g_qT.b, eg.b], [self.g_qg[d].b])
                    self.tt("gpsimd", self.g_kmg[d].t[:, c, ss], self.g_kT.t[:, c, ss], emg.t[:], ALU.mult,
                            [self.g_kT.b, emg.b], [self.g_kmg[d].b])
                    src = eg.t[:, 63::64] if d == 0 else eg.t[:, 0::64]
                    self.cp("gpsimd", self.egl_gla.t[:, c, d, ch0:ch0 + 2], src, [eg.b], [self.egl_gla.b])
                ps3 = self.pbank()
                self.mm(ps3, ps3.t[:, 0:256], self.SU[d].t[:], gl.t[:], [gl.b, self.SU[d].b])
                self.act(self.g_ed.t[:], ps3.t[:, 0:256], AF.Exp, [ps3.b], [self.g_ed.b], scale=-1.0 / 16)
                self.tt("vector", self.g_kd[d].t[:, s, :], self.g_ktm.t[:, s, :], self.g_ed.t[:], ALU.mult,
                        [self.g_ktm.b, self.g_ed.b], [self.g_kd[d].b])
        ts = slice(t0, t0 + T2)
        items, rd = [], []
        for d in range(2):
            items.append((self.gla_qg[d, :, ts].rearrange("(c p) t -> p c t", p=128), self.g_qg[d].t[:]))
            items.append((self.gla_kmg[d, :, ts].rearrange("(c p) t -> p c t", p=128), self.g_kmg[d].t[:]))
            items.append((self.gla_kd[d, ts, :].rearrange("(s p) c -> p s c", p=128), self.g_kd[d].t[:]))
            rd += [self.g_qg[d].b, self.g_kmg[d].b, self.g_kd[d].b]
        items.append((self.gla_v[ts, :].rearrange("(s p) c -> p s c", p=128), self.g_v.t[:]))
        rd.append(self.g_v.b)
        k.dma_batch("gpsimd", items, reads=rd, prim=k.buf("st_gla"))

    def scan(self, l):
        k = self.k
        self.begin_scope()
        NST = self.NST
        if "gla" in self.mix:
            self.alloc_sc_gla()
        if "ssd" in self.mix:
            self.alloc_sc_ssd()
        if "gdn" in self.mix:
            self.alloc_sc_gdn()
        for n in range(NST):
            pp = n % 2
            sts = (n, NST - 1 - n)
            self.ld_items = [[], []]
            self.ld_bufs = [[], []]
            if "gla" in self.mix:
                self.sc_gla_load(sts, pp)
            if "ssd" in self.mix:
                self.sc_ssd_load(sts, pp)
            if "gdn" in self.mix:
                self.sc_gdn_load(sts, pp)
            for d in range(2):
                k.dma_batch("sync", self.ld_items[d], writes=self.ld_bufs[d], prim=k.buf("scin%d%d" % (d, pp)))
            self.st_items = [[], []]
            self.st_bufs = [[], []]
            if "gla" in self.mix:
                self.sc_gla_pre(sts, pp)
            for ci in range(2):
                if "gdn" in self.mix:
                    self.sc_gdn_step(sts, pp, ci)
                if "gla" in self.mix:
                    self.sc_gla_step(sts, pp, ci)
                if "ssd" in self.mix:
                    self.sc_ssd_step(sts, pp, ci)
            if "gla" in self.mix:
                self.sc_gla_out(sts, pp)
            if "ssd" in self.mix:
                self.sc_ssd_out(sts, pp)
            if "gdn" in self.mix:
                self.sc_gdn_out(sts, pp)
            for d in range(2):
                k.dma_batch("gpsimd", self.st_items[d], reads=self.st_bufs[d], prim=k.buf("scout%d%d" % (d, pp)))
        self.end_scope()

    def alloc_sc_gla(self):
        k = self.k
        R2 = range(2)
        self.s_gq = [[self.sb("s_gq%d%d" % (d, p), [128, 2, 128], BF16) for p in R2] for d in R2]
        self.s_gk = [[self.sb("s_gk%d%d" % (d, p), [128, 2, 128], BF16) for p in R2] for d in R2]
        self.s_gkd = [[self.sb("s_gkd%d%d" % (d, p), [128, 256], BF16) for p in R2] for d in R2]
        self.s_gv = [[self.sb("s_gv%d%d" % (d, p), [128, 512], BF16) for p in R2] for d in R2]
        self.s_gatt = [self.sb("s_gatt%d" % d, [128, 4, 128], BF16) for d in R2]
        self.s_gS = [self.sb("s_gS%d" % d, [128, 2, 128], F32) for d in R2]
        self.s_gSb = [self.sb("s_gSb%d" % d, [128, 2, 128], BF16) for d in R2]
        self.s_go = [[self.sb("s_go%d%d" % (d, p), [128, 4, 128], F32) for p in R2] for d in R2]
        self.s_gop = [None, None]
        for d in R2:
            self.ms(self.s_gS[d].t[:], 0.0, [self.s_gS[d].b])
            self.ms(self.s_gSb[d].t[:], 0.0, [self.s_gSb[d].b])

    def sc_gla_load(self, sts, pp):
        k = self.k
        for d in range(2):
            st = sts[d]
            ts = slice(st * 128, (st + 1) * 128)
            q, kk, kd, v = self.s_gq[d][pp], self.s_gk[d][pp], self.s_gkd[d][pp], self.s_gv[d][pp]
            self.ld_items[d] += [(q.t[:], self.gla_qg[d, :, ts].rearrange("(c p) t -> p c t", p=128)),
                                 (kk.t[:], self.gla_kmg[d, :, ts].rearrange("(c p) t -> p c t", p=128)),
                                 (kd.t[:], self.gla_kd[d, ts, :]), (v.t[:], self.gla_v[ts, :])]
            self.ld_bufs[d] += [q.b, kk.b, kd.b, v.b]

    def sc_gla_pre(self, sts, pp):
        for d in range(2):
            q, kk = self.s_gq[d][pp], self.s_gk[d][pp]
            ps = self.pbank()
            for h in range(4):
                c, r = h // 2, h % 2
                rs = slice(r * 64, (r + 1) * 64)
                self.mm(ps, ps.t[:, h * 128:(h + 1) * 128], kk.t[rs, c, :], q.t[rs, c, :], [kk.b, q.b])
            att = self.s_gatt[d]
            self.tt("vector", att.t[:], ps.t[:].rearrange("p (h i) -> p h i", h=4),
                    self.U[d].t[:, None, :].to_broadcast([128, 4, 128]), ALU.mult, [ps.b, self.U[d].b], [att.b])

    def sc_gla_step(self, sts, pp, ci):
        k = self.k
        for d in range(2):
            st = sts[d]
            ch = ci if d == 0 else 1 - ci
            cs = slice(ch * 64, (ch + 1) * 64)
            chabs = st * 2 + ch
            q, kd, v = self.s_gq[d][pp], self.s_gkd[d][pp], self.s_gv[d][pp]
            att, S, Sb = self.s_gatt[d], self.s_gS[d], self.s_gSb[d]
            op = self.pbank()
            for h in range(4):
                c, r = h // 2, h % 2
                rs = slice(r * 64, (r + 1) * 64)
                oap = op.t[:, h * 64:(h + 1) * 64]
                self.mm(op, oap, Sb.t[rs, c, :], q.t[rs, c, cs], [Sb.b, q.b], start=True, stop=False)
                self.mm(op, oap, v.t[:, h * 128:(h + 1) * 128], att.t[:, h, cs], [v.b, att.b], start=False, stop=True,
                        acc=True)
            self.cp("vector", self.s_go[d][pp].t[:, :, cs], op.t[:, 0:256].rearrange("p (h i) -> p h i", h=4), [op.b],
                    [self.s_go[d][pp].b])
            pP = self.pbank()
            for h in range(4):
                c, r = h // 2, h % 2
                self.mm(pP, pP.t[r * 64:(r + 1) * 64, c * 128:(c + 1) * 128], kd.t[cs, h * 64:(h + 1) * 64],
                        v.t[cs, h * 128:(h + 1) * 128], [kd.b, v.b])
            for c in range(2):
                self.stt(S.t[:, c, :], S.t[:, c, :], self.egl_gla.t[:, c, d, chabs:chabs + 1],
                         pP.t[:, c * 128:(c + 1) * 128], ALU.mult, ALU.add, [S.b, pP.b, self.egl_gla.b], [S.b])
            self.cp("scalar", Sb.t[:], S.t[:], [S.b], [Sb.b])

    def sc_gla_out(self, sts, pp):
        k = self.k
        for d in range(2):
            st = sts[d]
            ts = slice(st * 128, (st + 1) * 128)
            o = self.s_go[d][pp]
            self.st_items[d].append((self.gla_o[d, :, ts].rearrange("(h p) t -> p h t", p=128), o.t[:]))
            self.st_bufs[d].append(o.b)

    def alloc_p2_common(self, l):
        k = self.k
        self.nega = self.sb("nega", [128, 32], F32)
        o, _ = PR["sm_alog"]
        self.act(self.nega.t[:], self.prb.t[:, o:o + 32], AF.Exp, [self.prb.b], [self.nega.b])
        self.ts("vector", self.nega.t[:], self.nega.t[:], -1.0, None, ALU.mult, None, [self.nega.b], [self.nega.b])
        self.sp = self.sb("sp", [128, NSUB2, 24], F32)
        self.gd = self.sb("gd", [128, NSUB2, 24], F32)
        self.lnb = self.sb("lnb", [128, NSUB2, 8], F32)
        self.beta = self.sb("beta", [128, NSUB2, 8], F32)
        self.lndt = self.sb("lndt", [128, NSUB2, 16], F32)
        self.c_zr = [self.sb("c_zr%d" % i, [128, TE2], BF16) for i in range(2)]
        self.c_dg = [self.sb("c_dg%d" % i, [128, 5, 128], BF16) for i in range(2)]
        self.c_n = 0

    def p2_smalls_post(self, l, t):
        k = self.k
        sm, sp, gd = self.sm, self.sp, self.gd
        t0 = t * T2
        self.act(sp.t[:], sm.t[:, :, 8:32], AF.Exp, [sm.b], [sp.b])
        self.act(sp.t[:], sp.t[:], AF.Ln, [sp.b, self.oneb.b], [sp.b], bias=self.oneb.t[:, 0:1])
        self.tt("vector", gd.t[:], sp.t[:], self.nega.t[:, None, 8:32].to_broadcast([128, NSUB2, 24]), ALU.mult,
                [sp.b, self.nega.b], [gd.b])
        self.act(self.lndt.t[:], sp.t[:, :, 8:24], AF.Ln, [sp.b], [self.lndt.b])
        self.act(self.lnb.t[:], sm.t[:, :, 0:8], AF.Exp, [sm.b], [self.lnb.b], scale=-1.0)
        self.act(self.lnb.t[:], self.lnb.t[:], AF.Ln, [self.lnb.b, self.oneb.b], [self.lnb.b], bias=self.oneb.t[:, 0:1])
        self.act(self.beta.t[:], self.lnb.t[:], AF.Exp, [self.lnb.b], [self.beta.b], scale=-1.0)
        self.ts("vector", self.lnb.t[:], self.lnb.t[:], -1.0, None, ALU.mult, None, [self.lnb.b], [self.lnb.b])
        ps = self.pbank()
        for s in range(NSUB2):
            for c in range(2):
                j = s * 2 + c
                self.mm(ps, ps.t[:, j * 24:(j + 1) * 24], self.cind[c].t[:], gd.t[:, s, :], [self.cind[c].b, gd.b])
        ch0 = t0 // 64
        n = NSUB2 * 2
        self.act(self.egl.t[:, ch0:ch0 + n, :], ps.t[:, 0:n * 24].rearrange("p (j c) -> p j c", j=n), AF.Exp,
                 [ps.b], [self.egl.b])

    def conv_chunk(self, l, w, wcols, cwname, cc, bias_ap, out_ap, out_buf):
        k = self.k
        hE = self.hE
        ps = self.pbank()
        self.mm_group(ps, ps.t[:, 0:T2], [(wcols(kc), hE.t[:, kc, 0:T2]) for kc in range(KC)], reads=[w.b, hE.b])
        self.mm_group(ps, ps.t[:, T2:TE2], [(wcols(kc), hE.t[:, kc, T2:TE2]) for kc in range(KC)], reads=[w.b, hE.b])
        zr = self.c_zr[self.c_n % 2]
        dg = self.c_dg[self.c_n % 2]
        self.c_n += 1
        self.cp("scalar", zr.t[:], ps.t[:, 0:TE2], [ps.b], [zr.b])
        o, _ = PV[cwname]
        for kk in range(5):
            self.ts("gpsimd", dg.t[:, kk, :], self.ident_bf.t[:], self.pv.t[:, l, o + cc * 5 + kk:o + cc * 5 + kk + 1], None,
                    ALU.mult, None, [self.ident_bf.b, self.pv.b], [dg.b])
        pc = self.pbank()
        self.mm_group(pc, pc.t[:, 0:T2], [(dg.t[:, kk, :], zr.t[:, kk:kk + T2]) for kk in range(5)], reads=[dg.b, zr.b])
        if bias_ap is not None:
            self.act(out_ap, pc.t[:, 0:T2], AF.Silu, [pc.b, self.pv.b], [out_buf], bias=bias_ap)
        else:
            self.act(out_ap, pc.t[:, 0:T2], AF.Silu, [pc.b], [out_buf])

    def alloc_p2_ssd(self, l):
        R2 = range(2)
        self.d_xbc = self.sb("d_xbc", [128, 8, T2], BF16)
        self.d_xtm = self.sb("d_xtm", [128, NSUB2, 512], BF16)
        self.d_Btm = self.sb("d_Btm", [128, NSUB2, 256], BF16)
        self.d_xw = [self.sb("d_xw%d" % d, [128, NSUB2, 512], BF16) for d in R2]
        self.d_yd = self.sb("d_yd", [128, 4, T2], F32)
        self.d_cexp = self.sb("d_cexp", [128, 16, 128], BF16)
        self.d_MT = self.sb("d_MT", [128, 16, 128], BF16)
        self.d_E = [self.sb("d_E%d" % i, [128, 128], F32) for i in range(4)]
        self.d_E1 = [self.sb("d_E1%d" % i, [128, 128], F32) for i in range(4)]
        self.d_sc = self.sb("d_sc", [128, 2, 128], F32)
        self.d_acum = self.sb("d_acum", [128, 16], F32)
        self.d_nb = self.sb("d_nb", [128, 16], F32)
        self.d_wst = self.sb("d_wst", [128, 16], F32)

    def p2_ssd(self, l, t):
        k = self.k
        t0 = t * T2
        ocb, _ = PV["ssd_cb"]
        od, _ = PV["ssd_d"]
        for ui in range(6, 10):
            w = self.load_unit(l, ("inA", ui))
            wv = w.t[:, 0:KC * 256].rearrange("p (k c) -> p k c", k=KC)
            for c2 in range(2):
                cc = (ui - 6) * 2 + c2
                self.conv_chunk(l, w, lambda kc, c2=c2, wv=wv: wv[:, kc, c2 * 128:(c2 + 1) * 128], "ssd_cw", cc,
                                self.pv.t[:, l, ocb + cc:ocb + cc + 1], self.d_xbc.t[:, cc, :], self.d_xbc.b)
        xbc = self.d_xbc
        for s in range(NSUB2):
            ss = slice(s * 128, (s + 1) * 128)
            tok = slice(t0 + s * 128, t0 + (s + 1) * 128)
            pT = self.ptr
            pTv = pT.t[:]
            if getattr(self, 'cut', 99) <= -1:
                continue
            for c in range(6):
                self.tr(pTv[:, c * 128:(c + 1) * 128], xbc.t[:, c, ss], [xbc.b], pT)
            if getattr(self, 'cut', 99) <= 0:
                continue
            self.cp("scalar", self.d_xtm.t[:, s, :], pTv[:, 0:512], [pT.b], [self.d_xtm.b])
            self.cp("vector", self.d_Btm.t[:, s, :], pTv[:, 512:768], [pT.b], [self.d_Btm.b])
            if getattr(self, 'cut', 99) <= 1:
                continue
            pa = self.pbank()
            for d in range(2):
                self.mm(pa, pa.t[:, d * 8:(d + 1) * 8], self.U[d].t[:], self.gd.t[:, s, 8 + d * 8:16 + d * 8],
                        [self.U[d].b, self.gd.b])
            for d in range(2):
                self.mm(pa, pa.t[:, 16 + d * 8:24 + d * 8], self.SU[d].t[:], self.gd.t[:, s, 8 + d * 8:16 + d * 8],
                        [self.SU[d].b, self.gd.b])
            self.cp("vector", self.d_acum.t[:], pa.t[:, 0:16], [pa.b], [self.d_acum.b])
            self.act(self.d_wst.t[:], pa.t[:, 16:32], AF.Exp, [pa.b], [self.d_wst.b])
            self.tt("vector", self.d_wst.t[:], self.d_wst.t[:], self.sp.t[:, s, 8:24], ALU.mult,
                    [self.d_wst.b, self.sp.b], [self.d_wst.b])
            self.tt("vector", self.d_nb.t[:], self.lndt.t[:, s, :], self.d_acum.t[:], ALU.subtract,
                    [self.lndt.b, self.d_acum.b], [self.d_nb.b])
            if getattr(self, 'cut', 99) <= 2:
                continue
            for d in range(2):
                self.tt("gpsimd", self.d_xw[d].t[:, s, :].rearrange("p (h q) -> p h q", h=8),
                        self.d_xtm.t[:, s, :].rearrange("p (h q) -> p h q", h=8),
                        self.d_wst.t[:, d * 8:(d + 1) * 8, None].to_broadcast([128, 8, 64]), ALU.mult,
                        [self.d_xtm.b, self.d_wst.b], [self.d_xw[d].b])
            if getattr(self, 'cut', 99) <= 3:
                continue
            pS = self.pbank()
            for g in range(2):
                self.mm(pS, pS.t[:, g * 128:(g + 1) * 128], xbc.t[:, 4 + g, ss], xbc.t[:, 6 + g, ss], [xbc.b])
            self.cp("scalar", self.d_sc.t[:], pS.t[:, 0:256].rearrange("p (g i) -> p g i", g=2), [pS.b], [self.d_sc.b])
            if getattr(self, 'cut', 99) <= 4:
                continue
            for d in range(2):
                for h in range(8):
                    dh = d * 8 + h
                    g = h // 4
                    pA = self.pbank()
                    bc = self.d_acum.t[:, dh:dh + 1].to_broadcast([128, 128])
                    self.mm(pA, pA.t[:, 0:128], bc, self.ident.t[:], [self.d_acum.b, self.ident.b])
                    self.mm(pA, pA.t[:, 128:256], bc, self.ident.t[:], [self.d_acum.b, self.ident.b], start=True, stop=False)
                    self.mm(pA, pA.t[:, 128:256], self.ident.t[:], self.maskneg[d].t[:], [self.ident.b, self.maskneg[d].b],
                            start=False, stop=True, acc=True)
                    E1, E = self.d_E1[dh % 4], self.d_E[dh % 4]
                    self.act(E1.t[:], pA.t[:, 0:128], AF.Exp, [pA.b], [E1.b])
                    self.act(E.t[:], pA.t[:, 128:256], AF.Exp, [pA.b, self.d_nb.b], [E.b], bias=self.d_nb.t[:, dh:dh + 1])
                    self.tt("gpsimd", self.d_cexp.t[:, dh, :], xbc.t[:, 6 + g, ss], E1.t[:], ALU.mult,
                            [xbc.b, E1.b], [self.d_cexp.b])
                    self.tt("vector", self.d_MT.t[:, dh, :], E.t[:], self.d_sc.t[:, g, :], ALU.mult,
                            [E.b, self.d_sc.b], [self.d_MT.b])
            if getattr(self, 'cut', 99) <= 5:
                continue
            pY = self.pbank()
            for h in range(8):
                c, r = h // 2, h % 2
                for d in range(2):
                    self.mm(pY, pY.t[r * 64:(r + 1) * 64, c * 128:(c + 1) * 128], self.d_xtm.t[:, s, h * 64:(h + 1) * 64],
                            self.d_MT.t[:, d * 8 + h, :], [self.d_xtm.b, self.d_MT.b], start=(d == 0), stop=(d == 1),
                            acc=(d == 1))
            for c in range(4):
                self.stt(self.d_yd.t[:, c, ss], xbc.t[:, c, ss], self.pv.t[:, l, od + c:od + c + 1],
                         pY.t[:, c * 128:(c + 1) * 128], ALU.mult, ALU.add, [xbc.b, pY.b, self.pv.b], [self.d_yd.b])
            if getattr(self, 'cut', 99) <= 6:
                continue
            k.dma_batch("gpsimd", [(self.ssd_cexp[d, :, tok].rearrange("(h p) t -> p h t", p=128),
                                    self.d_cexp.t[:, d * 8:(d + 1) * 8, :]) for d in range(2)],
                        reads=[self.d_cexp.b], prim=k.buf("st_ssd_s"))
        if getattr(self, 'cut', 99) <= 7:
            return
        ts = slice(t0, t0 + T2)
        items = [(self.ssd_B[ts, :].rearrange("(s p) c -> p s c", p=128), self.d_Btm.t[:]),
                 (self.ssd_ydiag[:, ts].rearrange("(c p) t -> p c t", p=128), self.d_yd.t[:])]
        for d in range(2):
            items.append((self.ssd_xw[d, ts, :].rearrange("(s p) c -> p s c", p=128), self.d_xw[d].t[:]))
        k.dma_batch("gpsimd", items, reads=[self.d_Btm.b, self.d_yd.b, self.d_xw[0].b, self.d_xw[1].b],
                    prim=k.buf("st_ssd_t"))

    def alloc_sc_ssd(self):
        k = self.k
        R2 = range(2)
        self.s_dc = [[self.sb("s_dc%d%d" % (d, p), [128, 8, 128], BF16) for p in R2] for d in R2]
        self.s_dB = [[self.sb("s_dB%d%d" % (d, p), [128, 256], BF16) for p in R2] for d in R2]
        self.s_dxw = [[self.sb("s_dxw%d%d" % (d, p), [128, 512], BF16) for p in R2] for d in R2]
        self.s_dS = [self.sb("s_dS%d" % d, [128, 512], F32) for d in R2]
        self.s_dSb = [self.sb("s_dSb%d" % d, [128, 512], BF16) for d in R2]
        self.s_dy = [[self.sb("s_dy%d%d" % (d, p), [128, 4, 128], F32) for p in R2] for d in R2]
        self.s_dop = [None, None]
        for d in R2:
            self.ms(self.s_dS[d].t[:], 0.0, [self.s_dS[d].b])
            self.ms(self.s_dSb[d].t[:], 0.0, [self.s_dSb[d].b])

    def sc_ssd_load(self, sts, pp):
        k = self.k
        for d in range(2):
            st = sts[d]
            ts = slice(st * 128, (st + 1) * 128)
            c, B, xw = self.s_dc[d][pp], self.s_dB[d][pp], self.s_dxw[d][pp]
            self.ld_items[d] += [(c.t[:], self.ssd_cexp[d, :, ts].rearrange("(h p) t -> p h t", p=128)),
                                 (B.t[:], self.ssd_B[ts, :]), (xw.t[:], self.ssd_xw[d, ts, :])]
            self.ld_bufs[d] += [c.b, B.b, xw.b]

    def sc_ssd_step(self, sts, pp, ci):
        k = self.k
        for d in range(2):
            st = sts[d]
            ch = ci if d == 0 else 1 - ci
            cs = slice(ch * 64, (ch + 1) * 64)
            chabs = st * 2 + ch
            c, B, xw = self.s_dc[d][pp], self.s_dB[d][pp], self.s_dxw[d][pp]
            S, Sb = self.s_dS[d], self.s_dSb[d]
            op = self.pbank()
            for h in range(8):
                cc, r = h // 2, h % 2
                self.mm(op, op.t[r * 64:(r + 1) * 64, cc * 64:(cc + 1) * 64],
                        Sb.t[:, h * 64:(h + 1) * 64], c.t[:, h, cs], [Sb.b, c.b])
            self.cp("vector", self.s_dy[d][pp].t[:, :, cs], op.t[:, 0:256].rearrange("p (c i) -> p c i", c=4), [op.b],
                    [self.s_dy[d][pp].b])
            pP = self.pbank()
            for g in range(2):
                self.mm(pP, pP.t[:, g * 256:(g + 1) * 256], B.t[cs, g * 128:(g + 1) * 128], xw.t[cs, g * 256:(g + 1) * 256],
                        [B.b, xw.b])
            self.tt("gpsimd", S.t[:].rearrange("p (h q) -> p h q", h=8), S.t[:].rearrange("p (h q) -> p h q", h=8),
                    self.egl.t[:, chabs, 8 + d * 8:16 + d * 8, None].to_broadcast([128, 8, 64]), ALU.mult,
                    [S.b, self.egl.b], [S.b])
            self.tt("vector", S.t[:], S.t[:], pP.t[:], ALU.add, [S.b, pP.b], [S.b])
            self.cp("scalar", Sb.t[:], S.t[:], [S.b], [Sb.b])

    def sc_ssd_out(self, sts, pp):
        k = self.k
        for d in range(2):
            st = sts[d]
            ts = slice(st * 128, (st + 1) * 128)
            y = self.s_dy[d][pp]
            self.st_items[d].append((self.ssd_yoff[d, :, ts].rearrange("(c p) t -> p c t", p=128), y.t[:]))
            self.st_bufs[d].append(y.b)

    def alloc_p2_gdn(self, l):
        k = self.k
        R2 = range(2)
        self.e_raw = self.sb("e_raw", [128, 8, T2], F32)
        self.e_q = self.sb("e_q", [128, 4, T2], BF16)
        self.e_k = self.sb("e_k", [128, 4, T2], BF16)
        self.e_v = self.sb("e_v", [128, 4, T2], BF16)
        self.e_sq = self.sb("e_sq", [128, T2], BF16)
        self.e_rs = self.sb("e_rs", [128, T2], F32)
        self.e_ktm = self.sb("e_ktm", [128, 4, 128], BF16)
        self.e_vtm = self.sb("e_vtm", [128, 4, 128], BF16)
        self.e_kg = [self.sb("e_kg%d" % d, [128, NSUB2, 512], BF16) for d in R2]
        self.e_kbg = [self.sb("e_kbg%d" % d, [128, 4, 128], BF16) for d in R2]
        self.e_vb = [self.sb("e_vb%d" % d, [128, 4, 128], BF16) for d in R2]
        self.e_kkqk = [self.sb("e_kkqk%d" % h, [128, 2, 128], F32) for h in range(4)]
        self.e_B = [[self.sb("e_B%d_%d" % (u, p), [128, 2, 128], BF16) for p in R2] for u in range(8)]
        self.e_XT = [[self.sb("e_XT%d_%d" % (u, p), [128, 128], BF16) for p in R2] for u in range(8)]
        self.e_E = [self.sb("e_E%d" % i, [128, 128], F32) for i in range(8)]
        self.e_En = 0
        self.e_att = [self.sb("e_att%d" % d, [128, 4, 128], BF16) for d in R2]
        self.e_u = [self.sb("e_u%d" % d, [128, 4, 128], F32) for d in R2]
        self.e_wn = [self.sb("e_wn%d" % d, [128, 4, 128], BF16) for d in R2]
        self.e_qg = [self.sb("e_qg%d" % d, [128, 4, T2], BF16) for d in R2]
        self.e_G = self.sb("e_G", [128, 8], F32)
        self.e_nG = self.sb("e_nG", [128, 8], F32)
        self.e_rA = self.sb("e_rA", [128, 8], F32)
        self.e_egs = self.sb("e_egs", [128, 8], F32)
        self.e_bG = self.sb("e_bG", [128, 8], F32)
        self.mposA = [self.sb("mposA%d" % d, [128, 128], F32) for d in R2]
        self.mnegAT = [self.sb("mnegAT%d" % d, [128, 128], F32) for d in R2]
        for d in R2:
            self.ts("vector", self.mposA[d].t[:], self.SU[d].t[:], -30000.0, 30000.0, ALU.mult, ALU.add,
                    [self.SU[d].b], [self.mposA[d].b])
            self.ts("vector", self.mnegAT[d].t[:], self.SU[1 - d].t[:], 30000.0, -30000.0, ALU.mult, ALU.add,
                    [self.SU[1 - d].b], [self.mnegAT[d].b])

    def nextE(self):
        t = self.e_E[self.e_En % 8]
        self.e_En += 1
        return t

    def p2_gdn(self, l, t):
        k = self.k
        t0 = t * T2
        for ui in range(6):
            w = self.load_unit(l, ("inA", ui))
            wv = w.t[:, 0:KC * 256].rearrange("p (k c) -> p k c", k=KC)
            for c2 in range(2):
                cc = ui * 2 + c2
                if cc < 8:
                    out_ap, ob = self.e_raw.t[:, cc, :], self.e_raw.b
                else:
                    out_ap, ob = self.e_v.t[:, cc - 8, :], self.e_v.b
                self.conv_chunk(l, w, lambda kc, c2=c2, wv=wv: wv[:, kc, c2 * 128:(c2 + 1) * 128], "gdn_cw", cc, None,
                                out_ap, ob)
        for cc in range(8):
            raw = self.e_raw.t[:, cc, :]
            self.act(self.e_sq.t[:], raw, AF.Square, [self.e_raw.b], [self.e_sq.b])
            ps = self.pbank()
            self.mm(ps, ps.t[:, 0:T2], self.ones_bf.t[:], self.e_sq.t[:], [self.ones_bf.b, self.e_sq.b])
            self.rstd_from_ss(ps.t[:, 0:T2], ps.b, self.e_rs, 1)
            dst = self.e_q if cc < 4 else self.e_k
            self.stt(dst.t[:, cc % 4, :], raw, (128 ** -0.5) if cc < 4 else 1.0, self.e_rs.t[:], ALU.mult, ALU.mult,
                     [self.e_raw.b, self.e_rs.b], [dst.b])
        for s in range(NSUB2):
            ss = slice(s * 128, (s + 1) * 128)
            tok = slice(t0 + s * 128, t0 + (s + 1) * 128)
            st = (t0 + s * 128) // 128
            pa = self.pbank()
            for d in range(2):
                self.mm(pa, pa.t[:, d * 4:(d + 1) * 4], self.U[d].t[:], self.gd.t[:, s, d * 4:(d + 1) * 4],
                        [self.U[d].b, self.gd.b])
            for d in range(2):
                self.mm(pa, pa.t[:, 8 + d * 4:12 + d * 4], self.SU[d].t[:], self.gd.t[:, s, d * 4:(d + 1) * 4],
                        [self.SU[d].b, self.gd.b])
            self.cp("vector", self.e_G.t[:], pa.t[:, 0:8], [pa.b], [self.e_G.b])
            self.act(self.e_egs.t[:], pa.t[:, 8:16], AF.Exp, [pa.b], [self.e_egs.b])
            self.act(self.e_bG.t[:], self.e_G.t[:], AF.Exp, [self.e_G.b], [self.e_bG.b])
            self.tt("vector", self.e_bG.t[:], self.e_bG.t[:], self.beta.t[:, s, :], ALU.mult, [self.e_bG.b, self.beta.b],
                    [self.e_bG.b])
            self.tt("vector", self.e_rA.t[:], self.e_G.t[:], self.lnb.t[:, s, :], ALU.add, [self.e_G.b, self.lnb.b],
                    [self.e_rA.b])
            self.ts("vector", self.e_nG.t[:], self.e_G.t[:], -1.0, None, ALU.mult, None, [self.e_G.b], [self.e_nG.b])
            pT = self.ptr
            for h in range(4):
                self.tr(pT.t[:, h * 128:(h + 1) * 128], self.e_k.t[:, h, ss], [self.e_k.b], pT)
            for h in range(4):
                self.tr(pT.t[:, 512 + h * 128:512 + (h + 1) * 128], self.e_v.t[:, h, ss], [self.e_v.b], pT)
            self.cp("scalar", self.e_ktm.t[:], pT.t[:, 0:512].rearrange("p (h c) -> p h c", h=4), [pT.b], [self.e_ktm.b])
            self.cp("vector", self.e_vtm.t[:], pT.t[:, 512:1024].rearrange("p (h c) -> p h c", h=4), [pT.b], [self.e_vtm.b])
            for d in range(2):
                ds_ = slice(d * 4, (d + 1) * 4)
                bc = lambda tl: tl.t[:, ds_, None].to_broadcast([128, 4, 128])
                self.tt("gpsimd", self.e_kbg[d].t[:], self.e_ktm.t[:], self.e_bG.t[:, d * 4:(d + 1) * 4, None].to_broadcast([128, 4, 128]),
                        ALU.mult, [self.e_ktm.b, self.e_bG.b], [self.e_kbg[d].b])
                self.tt("gpsimd", self.e_kg[d].t[:, s, :].rearrange("p (h c) -> p h c", h=4), self.e_ktm.t[:],
                        self.e_egs.t[:, d * 4:(d + 1) * 4, None].to_broadcast([128, 4, 128]), ALU.mult,
                        [self.e_ktm.b, self.e_egs.b], [self.e_kg[d].b])
                self.tt("gpsimd", self.e_vb[d].t[:], self.e_vtm.t[:],
                        self.beta.t[:, s, d * 4:(d + 1) * 4, None].to_broadcast([128, 4, 128]), ALU.mult,
                        [self.e_vtm.b, self.beta.b], [self.e_vb[d].b])
            for h in range(4):
                ps = self.pbank()
                self.mm(ps, ps.t[:, 0:128], self.e_k.t[:, h, ss], self.e_k.t[:, h, ss], [self.e_k.b])
                self.mm(ps, ps.t[:, 128:256], self.e_k.t[:, h, ss], self.e_q.t[:, h, ss], [self.e_k.b, self.e_q.b])
                self.cp("scalar", self.e_kkqk[h].t[:], ps.t[:, 0:256].rearrange("p (a i) -> p a i", a=2), [ps.b],
                        [self.e_kkqk[h].b])
            units = [(d, h) for d in range(2) for h in range(4)]
            for ui_, (d, h) in enumerate(units):
                dh = d * 4 + h
                kk, qk = self.e_kkqk[h].t[:, 0, :], self.e_kkqk[h].t[:, 1, :]
                kb_ = self.e_kkqk[h].b
                B0 = self.e_B[ui_][0]
                pA = self.pbank()
                bcG = self.e_G.t[:, dh:dh + 1].to_broadcast([128, 128])
                bcR = self.e_rA.t[:, dh:dh + 1].to_broadcast([128, 128])
                idt = self.ident
                self.mm(pA, pA.t[:, 0:128], bcG, idt.t[:], [self.e_G.b, idt.b], start=True, stop=False)
                self.mm(pA, pA.t[:, 0:128], idt.t[:], self.mposA[d].t[:], [idt.b, self.mposA[d].b], start=False, stop=True, acc=True)
                self.mm(pA, pA.t[:, 128:256], bcR, idt.t[:], [self.e_rA.b, idt.b], start=True, stop=False)
                self.mm(pA, pA.t[:, 128:256], idt.t[:], self.mnegAT[d].t[:], [idt.b, self.mnegAT[d].b], start=False, stop=True, acc=True)
                self.mm(pA, pA.t[:, 256:384], bcG, idt.t[:], [self.e_G.b, idt.b], start=True, stop=False)
                self.mm(pA, pA.t[:, 256:384], idt.t[:], self.maskneg[d].t[:], [idt.b, self.maskneg[d].b], start=False, stop=True, acc=True)
                self.mm(pA, pA.t[:, 384:512], bcG, idt.t[:], [self.e_G.b, idt.b])
                E = self.nextE()
                self.act(E.t[:], pA.t[:, 0:128], AF.Exp, [pA.b, self.e_rA.b], [E.b], scale=-1.0, bias=self.e_rA.t[:, dh:dh + 1])
                self.stt(B0.t[:, 0, :], E.t[:], -1.0, kk, ALU.mult, ALU.mult, [E.b, kb_], [B0.b])
                E = self.nextE()
                self.act(E.t[:], pA.t[:, 128:256], AF.Exp, [pA.b, self.e_nG.b], [E.b], bias=self.e_nG.t[:, dh:dh + 1])
                self.stt(B0.t[:, 1, :], E.t[:], -1.0, kk, ALU.mult, ALU.mult, [E.b, kb_], [B0.b])
                E = self.nextE()
                self.act(E.t[:], pA.t[:, 256:384], AF.Exp, [pA.b, self.e_nG.b], [E.b], bias=self.e_nG.t[:, dh:dh + 1])
                self.tt("vector", self.e_att[d].t[:, h, :], E.t[:], qk, ALU.mult, [E.b, kb_], [self.e_att[d].b])
                E = self.nextE()
                self.act(E.t[:], pA.t[:, 384:512], AF.Exp, [pA.b], [E.b])
                self.tt("gpsimd", self.e_qg[d].t[:, h, ss], self.e_q.t[:, h, ss], E.t[:], ALU.mult, [self.e_q.b, E.b],
                        [self.e_qg[d].b])
                self.tt("gpsimd", self.e_XT[ui_][0].t[:], B0.t[:, 1, :], self.ident.t[:], ALU.add, [B0.b, self.ident.b],
                        [self.e_XT[ui_][0].b])
            for lev in range(1, 6):
                pi, po = (lev - 1) % 2, lev % 2
                n = 2 if lev < 5 else 1
                sqb = []
                for pr in range(4):
                    ps = self.pbank()
                    for a in range(2):
                        Bp = self.e_B[2 * pr + a][pi]
                        self.mm(ps, ps.t[:, a * 256:a * 256 + 128], Bp.t[:, 1, :], Bp.t[:, 0, :], [Bp.b])
                        if lev < 5:
                            self.mm(ps, ps.t[:, a * 256 + 128:a * 256 + 256], Bp.t[:, 0, :], Bp.t[:, 1, :], [Bp.b])
                    sqb.append(ps)
                for ui_ in range(8):
                    ps = sqb[ui_ // 2]
                    a = ui_ % 2
                    Bn = self.e_B[ui_][po]
                    self.cp("scalar", Bn.t[:, 0:n, :], ps.t[:, a * 256:a * 256 + n * 128].rearrange("p (a i) -> p a i", a=n),
                            [ps.b], [Bn.b])
                prb_ = []
                for hf in range(2):
                    ps2 = self.pbank()
                    for a in range(4):
                        ui_ = hf * 4 + a
                        Bn, Xp = self.e_B[ui_][po], self.e_XT[ui_][pi]
                        self.mm(ps2, ps2.t[:, a * 128:(a + 1) * 128], Bn.t[:, 0, :], Xp.t[:], [Bn.b, Xp.b])
                    prb_.append(ps2)
                for ui_ in range(8):
                    ps2 = prb_[ui_ // 4]
                    a = ui_ % 4
                    Xp, Xn = self.e_XT[ui_][pi], self.e_XT[ui_][po]
                    self.tt("vector", Xn.t[:], Xp.t[:], ps2.t[:, a * 128:(a + 1) * 128], ALU.add, [Xp.b, ps2.b], [Xn.b])
            fin = 5 % 2
            for d in range(2):
                pu = self.pbank()
                pw = self.pbank()
                for h in range(4):
                    X = self.e_XT[d * 4 + h][fin]
                    self.mm(pu, pu.t[:, h * 128:(h + 1) * 128], X.t[:], self.e_vb[d].t[:, h, :], [X.b, self.e_vb[d].b])
                for h in range(4):
                    X = self.e_XT[d * 4 + h][fin]
                    self.mm(pw, pw.t[:, h * 128:(h + 1) * 128], self.e_kbg[d].t[:, h, :], X.t[:], [X.b, self.e_kbg[d].b])
                self.cp("scalar", self.e_u[d].t[:], pu.t[:].rearrange("p (h c) -> p h c", h=4), [pu.b], [self.e_u[d].b])
                self.ts("vector", self.e_wn[d].t[:], pw.t[:].rearrange("p (h c) -> p h c", h=4), -1.0, None, ALU.mult, None,
                        [pw.b], [self.e_wn[d].b])
            items, rd = [], []
            for d in range(2):
                items.append((self.gdn_u[d, tok, :], self.e_u[d].t[:].rearrange("p h c -> p (h c)")))
                items.append((self.gdn_wn[d, :, tok].rearrange("(h p) t -> p h t", p=128), self.e_wn[d].t[:]))
                items.append((self.gdn_att[d, st * 128:(st + 1) * 128, :], self.e_att[d].t[:].rearrange("p h c -> p (h c)")))
                rd += [self.e_u[d].b, self.e_wn[d].b, self.e_att[d].b]
            k.dma_batch("gpsimd", items, reads=rd, prim=k.buf("st_gdn_s"))
        ts = slice(t0, t0 + T2)
        items, rd = [], []
        for d in range(2):
            items.append((self.gdn_kg[d, ts, :].rearrange("(s p) c -> p s c", p=128), self.e_kg[d].t[:]))
            items.append((self.gdn_qg[d, :, ts].rearrange("(h p) t -> p h t", p=128), self.e_qg[d].t[:]))
            rd += [self.e_kg[d].b, self.e_qg[d].b]
        k.dma_batch("gpsimd", items, reads=rd, prim=k.buf("st_gdn_t"))

    def alloc_sc_gdn(self):
        k = self.k
        R2 = range(2)
        self.s_ewn = [[self.sb("s_ewn%d%d" % (d, p), [128, 4, 128], BF16) for p in R2] for d in R2]
        self.s_eu = [[self.sb("s_eu%d%d" % (d, p), [128, 512], F32) for p in R2] for d in R2]
        self.s_eqg = [[self.sb("s_eqg%d%d" % (d, p), [128, 4, 128], BF16) for p in R2] for d in R2]
        self.s_ekg = [[self.sb("s_ekg%d%d" % (d, p), [128, 512], BF16) for p in R2] for d in R2]
        self.s_eatt = [[self.sb("s_eatt%d%d" % (d, p), [128, 512], BF16) for p in R2] for d in R2]
        self.s_eS = [self.sb("s_eS%d" % d, [128, 4, 128], F32) for d in R2]
        self.s_eSb = [self.sb("s_eSb%d" % d, [128, 4, 128], BF16) for d in R2]
        self.s_evn = [self.sb("s_evn%d" % d, [128, 512], BF16) for d in R2]
        self.s_eo = [[self.sb("s_eo%d%d" % (d, p), [128, 4, 128], F32) for p in R2] for d in R2]
        self.s_eop = [None, None]
        for d in R2:
            self.ms(self.s_eS[d].t[:], 0.0, [self.s_eS[d].b])
            self.ms(self.s_eSb[d].t[:], 0.0, [self.s_eSb[d].b])

    def sc_gdn_load(self, sts, pp):
        k = self.k
        for d in range(2):
            st = sts[d]
            ts = slice(st * 128, (st + 1) * 128)
            wn, u, qg, kg, att = self.s_ewn[d][pp], self.s_eu[d][pp], self.s_eqg[d][pp], self.s_ekg[d][pp], self.s_eatt[d][pp]
            self.ld_items[d] += [(wn.t[:], self.gdn_wn[d, :, ts].rearrange("(h p) t -> p h t", p=128)),
                                 (u.t[:], self.gdn_u[d, ts, :]),
                                 (qg.t[:], self.gdn_qg[d, :, ts].rearrange("(h p) t -> p h t", p=128)),
                                 (kg.t[:], self.gdn_kg[d, ts, :]), (att.t[:], self.gdn_att[d, ts, :])]
            self.ld_bufs[d] += [wn.b, u.b, qg.b, kg.b, att.b]

    def sc_gdn_step(self, sts, pp, ci):
        k = self.k
        for d in range(2):
            st = sts[d]
            ch = ci if d == 0 else 1 - ci
            cs = slice(ch * 64, (ch + 1) * 64)
            chabs = st * 2 + ch
            wn, u, qg, kg, att = self.s_ewn[d][pp], self.s_eu[d][pp], self.s_eqg[d][pp], self.s_ekg[d][pp], self.s_eatt[d][pp]
            S, Sb, vn = self.s_eS[d], self.s_eSb[d], self.s_evn[d]
            op = self.pbank()
            pv = self.pbank()
            for h in range(4):
                self.mm(pv, pv.t[cs, h * 128:(h + 1) * 128], wn.t[:, h, cs], Sb.t[:, h, :], [wn.b, Sb.b])
            self.tt("vector", vn.t[cs, :], u.t[cs, :], pv.t[cs, :], ALU.add, [u.b, pv.b], [vn.b])
            for h in range(4):
                oap = op.t[:, h * 64:(h + 1) * 64]
                self.mm(op, oap, Sb.t[:, h, :], qg.t[:, h, cs], [Sb.b, qg.b], start=True, stop=False)
                self.mm(op, oap, vn.t[cs, h * 128:(h + 1) * 128], att.t[cs, h * 128 + ch * 64:h * 128 + (ch + 1) * 64],
                        [vn.b, att.b], start=False, stop=True, acc=True)
            self.cp("scalar", self.s_eo[d][pp].t[:, :, cs], op.t[:, 0:256].rearrange("p (h i) -> p h i", h=4), [op.b],
                    [self.s_eo[d][pp].b])
            pP = self.pbank()
            for h in range(4):
                self.mm(pP, pP.t[:, h * 128:(h + 1) * 128], kg.t[cs, h * 128:(h + 1) * 128], vn.t[cs, h * 128:(h + 1) * 128],
                        [kg.b, vn.b])
            self.tt("gpsimd", S.t[:], S.t[:], self.egl.t[:, chabs, d * 4:(d + 1) * 4, None].to_broadcast([128, 4, 128]),
                    ALU.mult, [S.b, self.egl.b], [S.b])
            self.tt("vector", S.t[:], S.t[:], pP.t[:].rearrange("p (h c) -> p h c", h=4), ALU.add, [S.b, pP.b], [S.b])
            self.cp("scalar", Sb.t[:], S.t[:], [S.b], [Sb.b])

    def sc_gdn_out(self, sts, pp):
        k = self.k
        for d in range(2):
            st = sts[d]
            ts = slice(st * 128, (st + 1) * 128)
            o = self.s_eo[d][pp]
            self.st_items[d].append((self.gdn_o[d, :, ts].rearrange("(h p) t -> p h t", p=128), o.t[:]))
            self.st_bufs[d].append(o.b)
_CACHE = {}


def run(inputs, depth, mix=("gdn", "gla", "ssd")):
    xp = np.asarray(inputs["x_prompt"], np.float32)
    xs = np.asarray(inputs["x_sample"], np.float32)
    L = xp.shape[1]
    assert xs.shape[1] == L
    seqs = [xp[i] for i in range(xp.shape[0])] + [xs[i] for i in range(xs.shape[0])]
    nseq = len(seqs)
    key = (L, depth, tuple(mix))
    if key not in _CACHE:
        _CACHE[key] = Builder(L, depth, mix).build()
    nc = _CACHE[key]
    inp = {n: np.asarray(v, np.float32) for n, v in inputs.items()}
    wf32 = np.stack([pack_layer_weights(inp, l) for l in range(depth)])
    pvec = np.stack([pack_pvec(inp, l) for l in range(depth)])
    prow = np.stack([pack_prow(inp, l) for l in range(depth)])
    wgup = np.stack([pack_wgup(inp, l) for l in range(depth)])
    fnorm = np.ascontiguousarray(inp["final_norm"].reshape(KC, 128).T)
    in_maps = []
    for c in range(8):
        s = seqs[c % nseq]
        in_maps.append({"xin": np.ascontiguousarray(s.T), "wf32": wf32, "pvec": pvec, "prow": prow, "wgup": wgup,
                        "fnorm": fnorm})
    res = run_bass_kernel_spmd(nc, in_maps, core_ids=list(range(8)))
    outs = [np.ascontiguousarray(res.results[c]["yout"].T) for c in range(nseq)]
    yp = np.stack(outs[:xp.shape[0]]).astype(np.float32)
    ys = np.stack(outs[xp.shape[0]:]).astype(np.float32)
    return yp, ys


def kernel(**inputs):
    return run(inputs, 4)
```

```python
import numpy as np
from contextlib import ExitStack
import concourse.bass as bass
import concourse.mybir as mybir
from concourse.bass_utils import run_bass_kernel_spmd

F32 = mybir.dt.float32
BF16 = mybir.dt.bfloat16
AF = mybir.ActivationFunctionType
ALU = mybir.AluOpType
ENGS = ("tensor", "vector", "scalar", "gpsimd", "sync")

D = 1024
DFF = 2816
NFF = DFF // 128
KC = D // 128
EPS = 1e-6
TT = 512
SLOT = 2816


class DSem:
    __slots__ = ("dsem", "dcount")

    def __init__(self, sem):
        self.dsem = sem
        self.dcount = 0


class Buf:
    __slots__ = ("name", "w", "r", "ds", "excl")

    def __init__(self, name):
        self.name = name
        self.w = []
        self.r = []
        self.ds = {}
        self.excl = False


class MK:
    def __init__(self, nc, es):
        self.nc = nc
        self.es = es
        self.prog = {e: [] for e in ENGS}
        self.esem = {}
        self.ecount = {e: 0 for e in ENGS}
        self.waited = {e: {} for e in ENGS}
        for e in ENGS:
            self.esem[e] = es.enter_context(nc.semaphore("s_" + e))
        self.nbuf = 0
        self.ninst = 0
        self.reg = {}

    def buf(self, name=None):
        if name is not None and name in self.reg:
            return self.reg[name]
        self.nbuf += 1
        b = Buf(name or ("b_%d" % self.nbuf))
        self.reg[b.name] = b
        return b

    def barrier(self):
        for eng in ENGS:
            need = {}
            for o in ENGS:
                if self.ecount[o] > self.waited[eng].get(("e", o), 0):
                    need[("e", o)] = (self.esem[o], self.ecount[o])
            for b in self.reg.values():
                for q in b.ds.values():
                    if q.dcount > self.waited[eng].get(("d", id(q)), 0):
                        need[("d", id(q))] = (q.dsem, q.dcount)
            self._emit_waits(eng, need)

    def _dsem(self, b, kind):
        if kind not in b.ds:
            b.ds[kind] = DSem(self.es.enter_context(self.nc.semaphore("d%s_%s" % (kind, b.name))))
        return b.ds[kind]

    def _need(self, eng, reads, writes, acc=None):
        need = {}

        def add(ev):
            kind, ref, val = ev
            if kind == "e":
                key = ("e", ref)
                sem = self.esem[ref]
                v = val
            else:
                key = ("d", id(ref))
                sem = ref.dsem
                v = ref.dcount
            if self.waited[eng].get(key, 0) >= v:
                return
            cur = need.get(key)
            if cur is None or cur[1] < v:
                need[key] = (sem, v)

        for b in reads:
            for ev in b.w:
                add(ev)
        for b in writes:
            for ev in b.w:
                if acc is not None and b is acc and ev[0] == "e" and ev[1] == "tensor":
                    continue
                add(ev)
            for ev in b.r:
                add(ev)
        return need

    def _emit_waits(self, eng, need):
        for key, (sem, v) in need.items():
            self.waited[eng][key] = v
            self.prog[eng].append(lambda e, sem=sem, v=v: e.wait_ge(sem, v))

    def _record(self, ev, reads, writes):
        for b in writes:
            b.w = [ev]
            b.r = []
        for b in reads:
            if any(b is w for w in writes):
                continue
            b.r = [x for x in b.r if not (x[0] == ev[0] and x[1] is ev[1])] + [ev]

    def op(self, eng, fn, reads=(), writes=(), acc=None):
        xr = [b for b in reads if b.excl]
        if xr:
            reads = [b for b in reads if not b.excl]
            writes = list(writes) + [b for b in xr if not any(b is w for w in writes)]
        need = self._need(eng, reads, writes, acc)
        self._emit_waits(eng, need)
        self.ecount[eng] += 1
        sem = self.esem[eng]
        self.prog[eng].append(lambda e, fn=fn, sem=sem: fn(e).then_inc(sem, 1))
        ev = ("e", eng, self.ecount[eng])
        self._record(ev, reads, writes)
        self.ninst += 1

    def dma(self, eng, out, in_, reads=(), writes=(), prim=None, **kw):
        if prim is None:
            prim = writes[0] if writes else reads[0]
        need = self._need(eng, reads, writes)
        self._emit_waits(eng, need)
        q = self._dsem(prim, "sw" if eng == "gpsimd" else "hw")
        q.dcount += 16
        sem = q.dsem
        self.prog[eng].append(
            lambda e, out=out, in_=in_, sem=sem, kw=kw: e.dma_start(out=out, in_=in_, **kw).then_inc(sem, 16))
        ev = ("d", q, q.dcount)
        self._record(ev, reads, writes)
        self.ninst += 1

    def dma_batch(self, eng, items, reads=(), writes=(), prim=None):
        need = self._need(eng, reads, writes)
        self._emit_waits(eng, need)
        q = self._dsem(prim, "sw" if eng == "gpsimd" else "hw")
        sem = q.dsem
        for (out, in_) in items:
            q.dcount += 16
            self.prog[eng].append(
                lambda e, out=out, in_=in_, sem=sem: e.dma_start(out=out, in_=in_).then_inc(sem, 16))
            self.ninst += 1
        ev = ("d", q, q.dcount)
        self._record(ev, reads, writes)

    def wait_all(self, eng, bufs):
        need = self._need(eng, bufs, ())
        self._emit_waits(eng, need)

    def run_block(self):
        nc = self.nc
        with nc.Block() as block:
            @block.tensor
            def _(e):
                for f in self.prog["tensor"]:
                    f(e)

            @block.vector
            def _(e):
                for f in self.prog["vector"]:
                    f(e)

            @block.scalar
            def _(e):
                for f in self.prog["scalar"]:
                    f(e)

            @block.gpsimd
            def _(e):
                for f in self.prog["gpsimd"]:
                    f(e)

            @block.sync
            def _(e):
                for f in self.prog["sync"]:
                    f(e)


class Tile:
    __slots__ = ("t", "b")

    def __init__(self, t, b):
        self.t = t
        self.b = b


IN_OFF = {}
_o = 0
for _n, _s in (("a_q", 512), ("a_k", 512), ("a_v", 512), ("a_z", 512), ("a_b", 8), ("a_a", 8),
               ("b_q", 256), ("b_k", 256), ("b_v", 512), ("b_r", 512), ("b_g", 32),
               ("c_z", 512), ("c_x", 512), ("c_B", 256), ("c_C", 256), ("c_dt", 16), ("gate", 3072)):
    IN_OFF[_n] = (_o, _s)
    _o += _s
assert _o == 8256


def _unit(W, cols):
    K = W.shape[0]
    kc = K // 128
    cols = np.asarray(cols)
    sub = np.zeros((K, len(cols)), np.float32)
    ok = cols >= 0
    sub[:, ok] = W[:, cols[ok]]
    return np.ascontiguousarray(sub.reshape(kc, 128, len(cols)).transpose(1, 0, 2)).reshape(128, kc * len(cols))


def col_range(name, lo=0, hi=None):
    o, s = IN_OFF[name]
    if hi is None:
        hi = s
    return list(range(o + lo, o + hi))


def make_cfg():
    cfg = {}
    A_chunks = []
    for nm, n in (("a_q", 4), ("a_k", 4), ("a_v", 4), ("c_x", 4), ("c_B", 2), ("c_C", 2), ("b_q", 2), ("b_k", 2)):
        for c in range(n):
            A_chunks.append(col_range(nm, c * 128, (c + 1) * 128))
    A_units = [A_chunks[2 * i] + A_chunks[2 * i + 1] for i in range(len(A_chunks) // 2)]
    A_units.append(col_range("b_v", 0, 256))
    A_units.append(col_range("b_v", 256, 512))
    A_units.append(col_range("b_k"))
    A_units.append(col_range("b_g", 0, 16) + [-1] * 16 + col_range("b_g", 16, 32) + [-1] * 16
                   + col_range("a_b") + col_range("a_a") + col_range("c_dt"))
    cfg["A_units"] = A_units
    B_chunks = []
    for nm in ("a_z", "b_r", "c_z"):
        for c in range(4):
            B_chunks.append(col_range(nm, c * 128, (c + 1) * 128))
    for c in range(24):
        B_chunks.append(col_range("gate", c * 128, (c + 1) * 128))
    cfg["B_units"] = [B_chunks[2 * i] + B_chunks[2 * i + 1] for i in range(len(B_chunks) // 2)]
    return cfg


CFG = make_cfg()


def unit_plan(cfg):
    plan = []
    for f in range(2):
        for j in range(NFF):
            plan.append((("gu", f, j), KC * 256))
        for oc in range(KC):
            plan.append((("dn", f, oc), NFF * 128))
    for u in range(len(cfg["A_units"])):
        plan.append((("inA", u), KC * len(cfg["A_units"][u])))
    for u in range(len(cfg["B_units"])):
        plan.append((("inB", u), KC * len(cfg["B_units"][u])))
    for oc in range(KC):
        plan.append((("br", oc), 12 * 128))
    for u in range(4):
        plan.append((("wo", u), KC * 256))
    return plan


PLAN = unit_plan(CFG)
UOFF = {}
_o = 0
for _k, _s in PLAN:
    UOFF[_k] = (_o, _s)
    _o += _s
LAYER_W = _o


def pack_layer_weights(inp, l):
    out = np.empty((128, LAYER_W), np.float32)
    for key, size in PLAN:
        o, s = UOFF[key]
        if key[0] == "gu":
            _, f, j = key
            cols = list(range(j * 128, (j + 1) * 128))
            g = _unit(inp["ffn_w_gate"][l, f], cols).reshape(128, KC, 128)
            u = _unit(inp["ffn_w_up"][l, f], cols).reshape(128, KC, 128)
            blk = np.concatenate([g, u], axis=2).reshape(128, KC * 256)
        elif key[0] == "dn":
            _, f, oc = key
            blk = _unit(inp["ffn_w_down"][l, f], list(range(oc * 128, (oc + 1) * 128)))
        elif key[0] == "inA":
            blk = _unit(inp["w_in"][l], CFG["A_units"][key[1]])
        elif key[0] == "inB":
            blk = _unit(inp["w_in"][l], CFG["B_units"][key[1]])
        elif key[0] == "br":
            oc = key[1]
            parts = [_unit(inp["w_branch"][l, b], list(range(oc * 128, (oc + 1) * 128))).reshape(128, 4, 128)
                     for b in range(3)]
            blk = np.concatenate(parts, axis=1).reshape(128, 12 * 128)
        elif key[0] == "wo":
            u = key[1]
            blk = _unit(inp["w_out"][l], list(range(u * 256, (u + 1) * 256)))
        assert blk.shape == (128, s), (key, blk.shape, s)
        out[:, o:o + s] = blk
    return out


PV = {}
_o = 0
for _n, _s in (("ffn_norm0", 8), ("ffn_norm1", 8), ("mix_norm", 8), ("gdn_norm", 1), ("gla_norm", 1),
               ("ssd_norm", 4), ("ssd_d", 4), ("gdn_cw", 60), ("ssd_cw", 40), ("ssd_cb", 8)):
    PV[_n] = (_o, _s)
    _o += _s
NPV = _o

PR = {}
_o = 0
for _n, _s in (("gla_bg", 512), ("sm_bias", 32), ("sm_alog", 32)):
    PR[_n] = (_o, _s)
    _o += _s
NPR = _o


def pack_pvec(inp, l):
    out = np.zeros((128, NPV), np.float32)

    def put(name, vec):
        o, s = PV[name]
        out[:, o:o + s] = np.asarray(vec, np.float32).reshape(s, 128).T
    put("ffn_norm0", inp["ffn_norm"][l, 0])
    put("ffn_norm1", inp["ffn_norm"][l, 1])
    put("mix_norm", inp["mix_norm"][l])
    put("gdn_norm", inp["gdn_norm"][l])
    put("gla_norm", inp["gla_norm"][l])
    put("ssd_norm", inp["ssd_norm"][l])
    put("ssd_d", np.repeat(inp["ssd_d"][l], 64))
    o, s = PV["gdn_cw"]
    cw = inp["gdn_conv_w"][l]
    out[:, o:o + s] = cw.reshape(5, 12, 128).transpose(2, 1, 0).reshape(128, 60)
    o, s = PV["ssd_cw"]
    cw = inp["ssd_conv_w"][l]
    out[:, o:o + s] = cw.reshape(5, 8, 128).transpose(2, 1, 0).reshape(128, 40)
    put("ssd_cb", inp["ssd_conv_b"][l])
    return out


def pack_prow(inp, l):
    out = np.zeros((NPR,), np.float32)
    o, s = PR["gla_bg"]
    out[o:o + s] = inp["gla_b_g"][l].reshape(512)
    o, s = PR["sm_bias"]
    out[o + 8:o + 16] = inp["gdn_dt_bias"][l].reshape(8)
    out[o + 16:o + 32] = inp["ssd_dt_bias"][l].reshape(16)
    o, s = PR["sm_alog"]
    out[o + 8:o + 16] = inp["gdn_a_log"][l].reshape(8)
    out[o + 16:o + 32] = inp["ssd_a_log"][l].reshape(16)
    return out


def pack_wgup(inp, l):
    out = np.zeros((64, 256), np.float32)
    out[0:16] = inp["gla_w_gup"][l, 0]
    out[32:48] = inp["gla_w_gup"][l, 1]
    return out
T2 = 256
TE2 = T2 + 4
NSUB2 = T2 // 128


class Builder:
    def __init__(self, L, depth, mix=("gdn", "gla", "ssd")):
        self.L = L
        self.depth = depth
        self.mix = tuple(mix)
        self.NT = L // TT
        self.NT2 = L // T2
        self.NST = L // 128
        self.NCH = L // 64

    def sb(self, name, shape, dt):
        es = self.scope if self.scope is not None else self.es
        self.tcount = getattr(self, "tcount", 0) + 1
        t = es.enter_context(self.nc.sbuf_tensor("%s_%d" % (name, self.tcount), list(shape), dt))
        return Tile(t, self.k.buf(name))

    def begin_scope(self):
        self.scope = ExitStack()
        self.scope.__enter__()

    def end_scope(self):
        self.k.barrier()
        self.scope.__exit__(None, None, None)
        self.scope = None

    def dram(self, name, shape, dt):
        return self.nc.dram_tensor(name, list(shape), dt).ap()

    def build(self):
        nc = bass.Bass("TRN2", target_bir_lowering=False)
        self.nc = nc
        L, depth = self.L, self.depth
        self.xin = nc.dram_tensor("xin", [D, L], F32, kind="ExternalInput").ap()
        self.wf32 = nc.dram_tensor("wf32", [depth, 128, LAYER_W], F32, kind="ExternalInput").ap()
        self.pvec = nc.dram_tensor("pvec", [depth, 128, NPV], F32, kind="ExternalInput").ap()
        self.prow = nc.dram_tensor("prow", [depth, NPR], F32, kind="ExternalInput").ap()
        self.wgup = nc.dram_tensor("wgup", [depth, 64, 256], F32, kind="ExternalInput").ap()
        self.fnorm = nc.dram_tensor("fnorm", [128, KC], F32, kind="ExternalInput").ap()
        self.yout = nc.dram_tensor("yout", [D, L], F32, kind="ExternalOutput").ap()
        self.wbf = self.dram("wbf", [depth, 128, LAYER_W], BF16)
        self.xres = self.dram("xres", [D, L], F32)
        NST = self.NST
        if "gla" in self.mix:
            self.gla_qg = self.dram("gla_qg", [2, 256, L], BF16)
            self.gla_kmg = self.dram("gla_kmg", [2, 256, L], BF16)
            self.gla_kd = self.dram("gla_kd", [2, L, 256], BF16)
            self.gla_v = self.dram("gla_v", [L, 512], BF16)
            self.gla_o = self.dram("gla_o", [2, 512, L], F32)
        if "ssd" in self.mix:
            self.ssd_cexp = self.dram("ssd_cexp", [2, 8 * 128, L], BF16)
            self.ssd_B = self.dram("ssd_B", [L, 256], BF16)
            self.ssd_xw = self.dram("ssd_xw", [2, L, 512], BF16)
            self.ssd_ydiag = self.dram("ssd_ydiag", [512, L], F32)
            self.ssd_yoff = self.dram("ssd_yoff", [2, 512, L], F32)
        if "gdn" in self.mix:
            self.gdn_wn = self.dram("gdn_wn", [2, 512, L], BF16)
            self.gdn_u = self.dram("gdn_u", [2, L, 512], F32)
            self.gdn_qg = self.dram("gdn_qg", [2, 512, L], BF16)
            self.gdn_kg = self.dram("gdn_kg", [2, L, 512], BF16)
            self.gdn_att = self.dram("gdn_att", [2, NST * 128, 512], BF16)
            self.gdn_o = self.dram("gdn_o", [2, 512, L], F32)
        with ExitStack() as es:
            k = MK(nc, es)
            self.k = k
            self.es = es
            self.scope = None
            self.setup_common()
            self.cast_weights()
            for l in range(depth + 1):
                self.pass1(l)
                if l < depth and self.mix:
                    self.pass2(l)
                    self.scan(l)
            k.wait_all("gpsimd", self.out_bufs)
            k.run_block()
        return nc

    def setup_common(self):
        k = self.k
        self.out_bufs = []
        self.ones_bf = self.sb("ones_bf", [128, 128], BF16)
        k.op("gpsimd", lambda e: e.memset(self.ones_bf.t[:], 1.0), writes=[self.ones_bf.b])
        self.epsb = self.sb("epsb", [128, 1], F32)
        k.op("gpsimd", lambda e: e.memset(self.epsb.t[:], EPS), writes=[self.epsb.b])
        self.oneb = self.sb("oneb", [128, 1], F32)
        k.op("gpsimd", lambda e: e.memset(self.oneb.t[:], 1.0), writes=[self.oneb.b])
        self.U = [self.sb("Ublk%d" % d, [128, 128], F32) for d in range(2)]
        self.SU = [self.sb("SUblk%d" % d, [128, 128], F32) for d in range(2)]
        self.ident = self.sb("ident", [128, 128], F32)
        self.ident_bf = self.sb("ident_bf", [128, 128], BF16)
        self.bones = self.sb("bones", [128, 128], F32)

        def tri(tile, ge, strict):
            k.op("gpsimd", lambda e: e.memset(tile.t[:], 0.0), writes=[tile.b])
            for b in range(2):
                blk = tile.t[b * 64:(b + 1) * 64, b * 64:(b + 1) * 64]
                k.op("gpsimd", lambda e, blk=blk: e.memset(blk, 1.0), writes=[tile.b])
                sgn = 1 if ge else -1
                base = -1 if strict else 0
                k.op("gpsimd", lambda e, blk=blk, sgn=sgn, base=base: e.affine_select(
                    out=blk, in_=blk, pattern=[[sgn, 64]], compare_op=ALU.is_ge, fill=0.0, base=base,
                    channel_multiplier=-sgn), reads=[tile.b], writes=[tile.b])
        tri(self.U[0], True, False)
        tri(self.U[1], False, False)
        tri(self.SU[0], False, True)
        tri(self.SU[1], True, True)
        k.op("gpsimd", lambda e: e.memset(self.bones.t[:], 0.0), writes=[self.bones.b])
        for b in range(2):
            blk = self.bones.t[b * 64:(b + 1) * 64, b * 64:(b + 1) * 64]
            k.op("gpsimd", lambda e, blk=blk: e.memset(blk, 1.0), writes=[self.bones.b])
        k.op("gpsimd", lambda e: e.memset(self.ident.t[:], 1.0), writes=[self.ident.b])
        k.op("gpsimd", lambda e: e.affine_select(out=self.ident.t[:], in_=self.ident.t[:], pattern=[[-1, 128]],
                                                 compare_op=ALU.is_equal, fill=0.0, base=0, channel_multiplier=1),
             reads=[self.ident.b], writes=[self.ident.b])
        k.op("gpsimd", lambda e: e.tensor_copy(out=self.ident_bf.t[:], in_=self.ident.t[:]),
             reads=[self.ident.b], writes=[self.ident_bf.b])
        self.cind = [self.sb("cind%d" % c, [128, 128], F32) for c in range(2)]
        for c in range(2):
            k.op("gpsimd", lambda e, c=c: e.memset(self.cind[c].t[:], 0.0), writes=[self.cind[c].b])
            k.op("gpsimd", lambda e, c=c: e.memset(self.cind[c].t[c * 64:(c + 1) * 64, :], 1.0), writes=[self.cind[c].b])
        self.maskneg = [self.sb("maskneg%d" % d, [128, 128], F32) for d in range(2)]
        for d in range(2):
            k.op("vector", lambda e, d=d: e.tensor_scalar(out=self.maskneg[d].t[:], in0=self.U[d].t[:], scalar1=30000.0,
                                                          scalar2=-30000.0, op0=ALU.mult, op1=ALU.add),
                 reads=[self.U[d].b], writes=[self.maskneg[d].b])
        self.NSLOT = 8
        self.wslots = [self.sb("wslot%d" % i, [128, SLOT], BF16) for i in range(self.NSLOT)]
        self.wnext = 0
        self.NPS = 7
        self.pbanks = []
        for i in range(self.NPS):
            t = self.es.enter_context(self.nc.psum_tensor("pb%d" % i, [128, TT], F32))
            self.pbanks.append(Tile(t, k.buf("pb%d" % i)))
            self.pbanks[-1].b.excl = True
        self.pnext = 0
        t = self.es.enter_context(self.nc.psum_tensor("ptr", [128, 1024], BF16))
        self.ptr = Tile(t, k.buf("ptr"))
        self.ptr.b.excl = True
        self.pv = self.sb("pv", [128, self.depth, NPV], F32)
        self.fn = self.sb("fn", [128, KC], F32)
        self.fn.b = self.pv.b
        k.dma_batch("sync", [(self.pv.t[:, l, :], self.pvec[l]) for l in range(self.depth)] + [(self.fn.t[:], self.fnorm)],
                    writes=[self.pv.b], prim=self.pv.b)
        NCH = self.NCH
        if "gla" in self.mix:
            self.egl_gla = self.sb("egl_gla", [128, 2, 2, NCH], F32)
        if "ssd" in self.mix or "gdn" in self.mix:
            self.egl = self.sb("egl", [128, NCH, 24], F32)
        self.pe_bufs = [k.buf("pe%d" % i) for i in range(32)]
        self.pe_n = 0

    def pvc(self, l, name, i=0, n=1):
        o, s = PV[name]
        return self.pv.t[:, l, o + i:o + i + n]

    def cast_weights(self):
        k = self.k
        CH = 65536
        self.cast_chunks = []
        for l in range(self.depth):
            lo = 0
            lb = k.buf("wcL%d" % l)
            while lo < LAYER_W:
                hi = min(LAYER_W, lo + CH)
                b = k.buf("wc0_%d" % lo) if l == 0 else lb
                k.dma("gpsimd", self.wbf[l, :, lo:hi], self.wf32[l, :, lo:hi], writes=[b], max_dma_last_dim=4096)
                self.cast_chunks.append((l, lo, hi, b))
                lo = hi

    def cast_deps(self, l, o, s):
        out = []
        for (ll, lo, hi, b) in self.cast_chunks:
            if ll == l and lo < o + s and hi > o and b not in out:
                out.append(b)
        return out

    def load_unit(self, l, key):
        k = self.k
        o, s = UOFF[key]
        slot = self.wslots[self.wnext]
        self.wnext = (self.wnext + 1) % self.NSLOT
        k.dma("sync", slot.t[:, 0:s], self.wbf[l, :, o:o + s], reads=self.cast_deps(l, o, s), writes=[slot.b],
              prim=slot.b)
        return slot

    def pbank(self):
        p = self.pbanks[self.pnext]
        self.pnext = (self.pnext + 1) % self.NPS
        return p

    def mm(self, ps, out_ap, lh, rh, reads, start=True, stop=True, acc=False):
        self.k.op("tensor", lambda e: e.matmul(out_ap, lhsT=lh, rhs=rh, start=start, stop=stop),
                  reads=reads, writes=[ps.b], acc=(ps.b if acc else None))

    def mm_group(self, ps, out_ap, pairs, reads):
        n = len(pairs)
        for i, (lh, rh) in enumerate(pairs):
            self.mm(ps, out_ap, lh, rh, reads, start=(i == 0), stop=(i == n - 1), acc=(i > 0))

    def ts(self, eng, out, in0, s1, s2, op0, op1, reads, writes):
        if op1 is None:
            self.k.op(eng, lambda e: e.tensor_scalar(out=out, in0=in0, scalar1=s1, scalar2=None, op0=op0),
                      reads=reads, writes=writes)
        else:
            self.k.op(eng, lambda e: e.tensor_scalar(out=out, in0=in0, scalar1=s1, scalar2=s2, op0=op0, op1=op1),
                      reads=reads, writes=writes)

    def ms(self, ap, val, writes):
        self.k.op("gpsimd", lambda e: e.memset(ap, val), reads=(), writes=writes)

    def tr(self, out, in_, reads, ps):
        idt = self.ident_bf
        self.k.op("tensor", lambda e: e.transpose(out, in_, idt.t[:]), reads=list(reads) + [idt.b], writes=[ps.b])

    def act(self, out, in_, func, reads, writes, **kw):
        self.k.op("scalar", lambda e: e.activation(out=out, in_=in_, func=func, **kw), reads=reads, writes=writes)

    def tt(self, eng, out, in0, in1, op, reads, writes):
        self.k.op(eng, lambda e: e.tensor_tensor(out=out, in0=in0, in1=in1, op=op), reads=reads, writes=writes)

    def stt(self, out, in0, scalar, in1, op0, op1, reads, writes):
        self.k.op("vector", lambda e: e.scalar_tensor_tensor(out=out, in0=in0, scalar=scalar, in1=in1, op0=op0, op1=op1),
                  reads=reads, writes=writes)

    def cp(self, eng, out, in_, reads, writes):
        if eng == "scalar":
            self.k.op(eng, lambda e: e.copy(out=out, in_=in_), reads=reads, writes=writes)
        else:
            self.k.op(eng, lambda e: e.tensor_copy(out=out, in_=in_), reads=reads, writes=writes)

    def rstd_from_ss(self, ps_ap, psb, out, n, cols=None):
        self.act(out.t[:] if cols is None else cols, ps_ap, AF.Ln, [psb, self.epsb.b], [out.b], scale=1.0 / n,
                 bias=self.epsb.t[:, 0:1])
        o = out.t[:] if cols is None else cols
        self.act(o, o, AF.Exp, [out.b], [out.b], scale=-0.5)

    def rmsnorm(self, x, gcol, out_h, W, sq, rstd):
        k = self.k
        self.act(sq.t[:, :, 0:W], x.t[:, :, 0:W], AF.Square, [x.b], [sq.b])
        for (lo, hi) in ((0, min(W, 512)), (512, W)):
            if hi <= lo:
                continue
            ps = self.pbank()
            self.mm_group(ps, ps.t[:, 0:hi - lo], [(self.ones_bf.t[:], sq.t[:, kc, lo:hi]) for kc in range(KC)],
                          reads=[self.ones_bf.b, sq.b])
            self.rstd_from_ss(ps.t[:, 0:hi - lo], ps.b, rstd, D, cols=rstd.t[:, lo:hi])
        for kc in range(KC):
            self.stt(out_h.t[:, kc, 0:W], x.t[:, kc, 0:W], gcol[:, kc:kc + 1], rstd.t[:, 0:W], ALU.mult, ALU.mult,
                     [x.b, rstd.b, self.pv.b, self.fn.b], [out_h.b])

    def ffn(self, l, f, x):
        k = self.k
        o, s = PV["ffn_norm%d" % f]
        self.rmsnorm(x, self.pv.t[:, l, o:o + s], self.hT, TT, self.sq, self.rstd)
        hT, hid = self.hT, self.hid
        for j in range(NFF):
            w = self.load_unit(l, ("gu", f, j))
            wv = w.t[:, 0:KC * 256].rearrange("p (k c) -> p k c", k=KC)
            pg, pu = self.pbank(), self.pbank()
            self.mm_group(pg, pg.t[:], [(wv[:, kc, 0:128], hT.t[:, kc, :]) for kc in range(KC)], reads=[w.b, hT.b])
            self.mm_group(pu, pu.t[:], [(wv[:, kc, 128:256], hT.t[:, kc, :]) for kc in range(KC)], reads=[w.b, hT.b])
            sg = self.sg[j % 2]
            self.act(sg.t[:], pg.t[:], AF.Silu, [pg.b], [sg.b])
            self.tt("vector", hid.t[:, j, :], sg.t[:], pu.t[:], ALU.mult, [sg.b, pu.b], [hid.b])
        for oc in range(KC):
            w = self.load_unit(l, ("dn", f, oc))
            wv = w.t[:, 0:NFF * 128].rearrange("p (k c) -> p k c", k=NFF)
            po = self.pbank()
            self.mm_group(po, po.t[:], [(wv[:, j, :], hid.t[:, j, :]) for j in range(NFF)], reads=[w.b, hid.b])
            self.stt(x.t[:, oc, :], po.t[:], 0.5, x.t[:, oc, :], ALU.mult, ALU.add, [po.b, x.b], [x.b])

    def pass1(self, l):
        k = self.k
        first = (l == 0)
        last = (l == self.depth)
        self.begin_scope()
        self.xT = [self.sb("xT%d" % i, [128, KC, TT], F32) for i in range(2)]
        self.hT = self.sb("hT", [128, KC, TT], BF16)
        self.sq = self.sb("sq", [128, KC, TT], BF16)
        self.rstd = self.sb("rstd", [128, TT], F32)
        self.hid = self.sb("hid", [128, 24, TT], BF16)
        self.sg = [self.sb("sg%d" % i, [128, TT], F32) for i in range(2)]
        if not first and self.mix:
            self.alloc_B()
        src = self.xin if first else self.xres
        for t in range(self.NT):
            x = self.xT[t % 2]
            t0 = t * TT
            k.dma("sync", x.t[:], src[:, t0:t0 + TT].rearrange("(k p) t -> p k t", p=128), writes=[x.b])
            if not first:
                if self.mix:
                    self.phase_B(l - 1, t, x)
                self.ffn(l - 1, 1, x)
            if not last:
                self.ffn(l, 0, x)
                k.dma("gpsimd", self.xres[:, t0:t0 + TT].rearrange("(k p) t -> p k t", p=128), x.t[:],
                      reads=[x.b], prim=x.b)
            else:
                o, s = 0, KC
                self.final_norm(x)
                k.dma("gpsimd", self.yout[:, t0:t0 + TT].rearrange("(k p) t -> p k t", p=128), x.t[:],
                      reads=[x.b], prim=x.b)
                if x.b not in self.out_bufs:
                    self.out_bufs.append(x.b)
        self.end_scope()

    def final_norm(self, x):
        sq, rstd = self.sq, self.rstd
        self.act(sq.t[:], x.t[:], AF.Square, [x.b], [sq.b])
        ps = self.pbank()
        self.mm_group(ps, ps.t[:], [(self.ones_bf.t[:], sq.t[:, kc, :]) for kc in range(KC)],
                      reads=[self.ones_bf.b, sq.b])
        self.rstd_from_ss(ps.t[:], ps.b, rstd, D)
        for kc in range(KC):
            self.stt(x.t[:, kc, :], x.t[:, kc, :], self.fn.t[:, kc:kc + 1], rstd.t[:], ALU.mult, ALU.mult,
                     [x.b, rstd.b, self.fn.b], [x.b])

    def alloc_B(self):
        onv = self.hid.t[:].rearrange("p a b -> p (a b)").bitcast(F32).rearrange("p (a b) -> p a b", a=12)
        self.on = Tile(None, self.hid.b)
        self.on_v = onv
        self.ybr = self.sb("ybr", [128, 12, TT], BF16)
        self.ld = [self.sb("ldB%d" % i, [128, TT], F32) for i in range(4)]
        self.ldn = 0
        self.sqb = self.sb("sqb", [128, 4, TT], BF16)
        self.rsb = self.sb("rsb", [128, TT], F32)
        self.rsb4 = self.sb("rsb4", [128, 4, TT], F32)
        self.mg = [self.sb("mg%d" % i, [128, TT], F32) for i in range(2)]
        self.mtmp = [self.sb("mtmp%d" % i, [128, TT], F32) for i in range(2)]
        self.th = [self.sb("th%d" % i, [128, TT], F32) for i in range(3)]
        self.thn = 0
        self.merged = self.sq

    def ldB(self, src_ap):
        t = self.ld[self.ldn % 4]
        self.ldn += 1
        self.k.dma("sync", t.t[:], src_ap, writes=[t.b])
        return t

    def phase_B(self, l, t, x):
        k = self.k
        t0 = t * TT
        ts = slice(t0, t0 + TT)
        on, ybr, hT = self.on, self.ybr, self.hT
        onv = self.on_v
        self.rmsnorm(x, self.pvc(l, "mix_norm", 0, 8), hT, TT, self.sq, self.rstd)
        if "ssd" in self.mix:
            for u in range(2):
                w = self.load_unit(l, ("inB", 4 + u))
                wv = w.t[:, 0:KC * 256].rearrange("p (k c) -> p k c", k=KC)
                for cc in range(2):
                    c = 2 * u + cc
                    a = self.ldB(self.ssd_ydiag[c * 128:(c + 1) * 128, ts])
                    b = self.ldB(self.ssd_yoff[0, c * 128:(c + 1) * 128, ts])
                    d = self.ldB(self.ssd_yoff[1, c * 128:(c + 1) * 128, ts])
                    self.tt("gpsimd", a.t[:], a.t[:], b.t[:], ALU.add, [a.b, b.b], [a.b])
                    self.tt("gpsimd", a.t[:], a.t[:], d.t[:], ALU.add, [a.b, d.b], [a.b])
                    pz = self.pbank()
                    self.mm_group(pz, pz.t[:], [(wv[:, kc, cc * 128:(cc + 1) * 128], hT.t[:, kc, :]) for kc in range(KC)],
                                  reads=[w.b, hT.b])
                    sz = self.th[self.thn % 3]
                    self.thn += 1
                    self.act(sz.t[:], pz.t[:], AF.Silu, [pz.b], [sz.b])
                    self.tt("vector", onv[:, 8 + c, :], a.t[:], sz.t[:], ALU.mult, [a.b, sz.b], [on.b])
        for m, base, src in (("gdn", 0, getattr(self, "gdn_o", None)), ("gla", 4, getattr(self, "gla_o", None))):
            if m not in self.mix:
                continue
            for h in range(4):
                a = self.ldB(src[0, h * 128:(h + 1) * 128, ts])
                b = self.ldB(src[1, h * 128:(h + 1) * 128, ts])
                self.tt("vector" if h % 2 else "gpsimd", onv[:, base + h, :], a.t[:], b.t[:], ALU.add, [a.b, b.b], [on.b])
            self.act(self.sqb.t[:], onv[:, base:base + 4, :], AF.Square, [on.b], [self.sqb.b])
            pss = []
            for h in range(4):
                ps = self.pbank()
                self.mm(ps, ps.t[:], self.ones_bf.t[:], self.sqb.t[:, h, :], [self.ones_bf.b, self.sqb.b])
                pss.append(ps)
            for h in range(4):
                self.act(self.rsb4.t[:, h, :], pss[h].t[:], AF.Ln, [pss[h].b, self.epsb.b], [self.rsb4.b], scale=1.0 / 128,
                         bias=self.epsb.t[:, 0:1])
            self.act(self.rsb4.t[:], self.rsb4.t[:], AF.Exp, [self.rsb4.b], [self.rsb4.b], scale=-0.5)
            for h in range(4):
                self.stt(onv[:, base + h, :], onv[:, base + h, :], self.pvc(l, m + "_norm"), self.rsb4.t[:, h, :], ALU.mult,
                         ALU.mult, [on.b, self.rsb4.b, self.pv.b], [on.b])
        if "ssd" in self.mix:
            for g in range(2):
                self.act(self.sqb.t[:, 0:2, :], onv[:, 8 + 2 * g:10 + 2 * g, :], AF.Square, [on.b], [self.sqb.b])
                ps = self.pbank()
                self.mm_group(ps, ps.t[:], [(self.ones_bf.t[:], self.sqb.t[:, cc, :]) for cc in range(2)],
                              reads=[self.ones_bf.b, self.sqb.b])
                self.rstd_from_ss(ps.t[:], ps.b, self.rsb, 256)
                for cc in range(2):
                    c = 2 * g + cc
                    self.stt(ybr.t[:, 8 + c, :], onv[:, 8 + c, :], self.pvc(l, "ssd_norm", c), self.rsb.t[:],
                             ALU.mult, ALU.mult, [on.b, self.rsb.b, self.pv.b], [ybr.b])
        for m, base, u0 in (("gdn", 0, 0), ("gla", 4, 2)):
            if m not in self.mix:
                continue
            for u in range(2):
                w = self.load_unit(l, ("inB", u0 + u))
                wv = w.t[:, 0:KC * 256].rearrange("p (k c) -> p k c", k=KC)
                for cc in range(2):
                    h = 2 * u + cc
                    pz = self.pbank()
                    self.mm_group(pz, pz.t[:], [(wv[:, kc, cc * 128:(cc + 1) * 128], hT.t[:, kc, :]) for kc in range(KC)],
                                  reads=[w.b, hT.b])
                    sz = self.th[self.thn % 3]
                    self.thn += 1
                    self.act(sz.t[:], pz.t[:], AF.Silu, [pz.b], [sz.b])
                    self.tt("vector", ybr.t[:, base + h, :], onv[:, base + h, :], sz.t[:], ALU.mult, [on.b, sz.b], [ybr.b])
        mixl = [i for i, m in enumerate(("gdn", "gla", "ssd")) if m in self.mix]
        for op_ in range(4):
            wbs = [self.load_unit(l, ("br", 2 * op_ + cc)) for cc in range(2)]
            for bi, b in enumerate(mixl):
                wg_ = self.load_unit(l, ("inB", 6 + b * 4 + op_))
                wgv = wg_.t[:, 0:KC * 256].rearrange("p (k c) -> p k c", k=KC)
                for cc in range(2):
                    wbv = wbs[cc].t[:, 0:12 * 128].rearrange("p (k c) -> p k c", k=12)
                    pgt = self.pbank()
                    self.mm_group(pgt, pgt.t[:], [(wgv[:, kc, cc * 128:(cc + 1) * 128], hT.t[:, kc, :]) for kc in range(KC)],
                                  reads=[wg_.b, hT.b])
                    th = self.th[self.thn % 3]
                    self.thn += 1
                    self.act(th.t[:], pgt.t[:], AF.Tanh, [pgt.b], [th.b], scale=0.5)
                    pbr = self.pbank()
                    self.mm_group(pbr, pbr.t[:], [(wbv[:, b * 4 + kk, :], ybr.t[:, b * 4 + kk, :]) for kk in range(4)],
                                  reads=[wbs[cc].b, ybr.b])
                    if bi == 0:
                        self.stt(self.mg[cc].t[:], th.t[:], 1.0, pbr.t[:], ALU.add, ALU.mult, [th.b, pbr.b], [self.mg[cc].b])
                    else:
                        tmp = self.mtmp[cc]
                        self.stt(tmp.t[:], th.t[:], 1.0, pbr.t[:], ALU.add, ALU.mult, [th.b, pbr.b], [tmp.b])
                        self.tt("gpsimd", self.mg[cc].t[:], self.mg[cc].t[:], tmp.t[:], ALU.add,
                                [self.mg[cc].b, tmp.b], [self.mg[cc].b])
            for cc in range(2):
                oc = 2 * op_ + cc
                self.act(self.merged.t[:, oc, :], self.mg[cc].t[:], AF.Copy, [self.mg[cc].b], [self.merged.b], scale=0.5)
        for u in range(4):
            w = self.load_unit(l, ("wo", u))
            wv = w.t[:, 0:KC * 256].rearrange("p (k c) -> p k c", k=KC)
            for cc in range(2):
                oc = 2 * u + cc
                po = self.pbank()
                self.mm_group(po, po.t[:], [(wv[:, kc, cc * 128:(cc + 1) * 128], self.merged.t[:, kc, :]) for kc in range(KC)],
                              reads=[w.b, self.merged.b])
                self.tt("vector", x.t[:, oc, :], x.t[:, oc, :], po.t[:], ALU.add, [x.b, po.b], [x.b])

    def pass2(self, l):
        k = self.k
        L = self.L
        self.begin_scope()
        self.xE = self.sb("xE", [128, KC, TE2], F32)
        self.hE = self.sb("hE", [128, KC, TE2], BF16)
        self.sqE = self.sb("sqE", [128, KC, TE2], BF16)
        self.rstdE = self.sb("rstdE", [128, TE2], F32)
        self.prb = self.sb("prb", [128, NPR], F32)
        k.dma("sync", self.prb.t[:], self.prow[l:l + 1, :].to_broadcast([128, NPR]), writes=[self.prb.b])
        self.lrT = self.sb("lrT", [64, T2], BF16)
        self.sm = self.sb("sm", [128, NSUB2, 32], F32)
        if "gla" in self.mix:
            self.alloc_p2_gla(l)
        if "ssd" in self.mix:
            self.alloc_p2_ssd(l)
        if "gdn" in self.mix:
            self.alloc_p2_gdn(l)
        xE = self.xE
        self.hEs = [self.hE, self.sb("hE2", [128, KC, TE2], BF16)]
        self.pnext = 0
        if "ssd" in self.mix or "gdn" in self.mix:
            self.alloc_p2_common(l)

        def front(t):
            hE_ = self.hEs[t % 2]
            t0 = t * T2
            lo, hi = t0 - 2, t0 + T2 + 2
            j0, j1 = 0, TE2
            if lo < 0:
                self.ms(xE.t[:, :, 0:2], 0.0, [xE.b])
                j0, lo = 2, 0
            if hi > L:
                self.ms(xE.t[:, :, TE2 - 2:TE2], 0.0, [xE.b])
                j1, hi = TE2 - 2, L
            k.dma("sync", xE.t[:, :, j0:j1], self.xres[:, lo:hi].rearrange("(k p) t -> p k t", p=128), writes=[xE.b])
            self.rmsnorm(xE, self.pvc(l, "mix_norm", 0, 8), hE_, TE2, self.sqE, self.rstdE)

        front(0)
        for t in range(self.NT2):
            t0 = t * T2
            hE = self.hEs[t % 2]
            self.hE = hE
            w = self.load_unit(l, ("inA", 15))
            wv = w.t[:, 0:KC * 96].rearrange("p (k c) -> p k c", k=KC)
            ps = self.pbank()
            self.mm_group(ps, ps.t[0:64, 0:T2], [(wv[:, kc, 0:64], hE.t[:, kc, 2:2 + T2]) for kc in range(KC)], reads=[w.b, hE.b])
            self.cp("scalar", self.lrT.t[:], ps.t[0:64, 0:T2], [ps.b], [self.lrT.b])
            ps = self.pbank()
            for s in range(NSUB2):
                self.mm_group(ps, ps.t[:, s * 32:(s + 1) * 32],
                              [(hE.t[:, kc, 2 + s * 128:2 + (s + 1) * 128], wv[:, kc, 64:96]) for kc in range(KC)],
                              reads=[w.b, hE.b])
            o, _ = PR["sm_bias"]
            self.tt("vector", self.sm.t[:], ps.t[:, 0:NSUB2 * 32].rearrange("p (s c) -> p s c", s=NSUB2),
                    self.prb.t[:, None, o:o + 32].to_broadcast([128, NSUB2, 32]), ALU.add, [ps.b, self.prb.b], [self.sm.b])
            if t + 1 < self.NT2:
                front(t + 1)
            if "ssd" in self.mix or "gdn" in self.mix:
                self.p2_smalls_post(l, t)
            if "gla" in self.mix:
                self.p2_gla(l, t)
            if "ssd" in self.mix:
                self.p2_ssd(l, t)
            if "gdn" in self.mix:
                self.p2_gdn(l, t)
        self.NPS = 7
        self.end_scope()

    def alloc_p2_gla(self, l):
        k = self.k
        self.g_qT = self.sb("g_qT", [128, 2, T2], BF16)
        self.g_kT = self.sb("g_kT", [128, 2, T2], BF16)
        self.g_ktm = self.sb("g_ktm", [128, NSUB2, 256], F32)
        self.g_v = self.sb("g_v", [128, NSUB2, 512], BF16)
        self.g_qg = [self.sb("g_qg%d" % d, [128, 2, T2], BF16) for d in range(2)]
        self.g_kmg = [self.sb("g_kmg%d" % d, [128, 2, T2], BF16) for d in range(2)]
        self.g_kd = [self.sb("g_kd%d" % d, [128, NSUB2, 256], BF16) for d in range(2)]
        self.g_l = self.sb("g_l", [128, 256], F32)
        self.g_eg = [self.sb("g_eg%d" % i, [128, 128], F32) for i in range(2)]
        self.g_emg = [self.sb("g_emg%d" % i, [128, 128], F32) for i in range(2)]
        self.g_ed = self.sb("g_ed", [128, 256], F32)
        self.wg32 = self.sb("wg32", [64, 256], F32)
        self.wgb = self.sb("wgb", [64, 256], BF16)
        k.dma("sync", self.wg32.t[:], self.wgup[l], writes=[self.wg32.b])
        self.cp("vector", self.wgb.t[:], self.wg32.t[:], [self.wg32.b], [self.wgb.b])

    def p2_gla(self, l, t):
        k = self.k
        hE = self.hE
        t0 = t * T2
        for ui, dst in ((10, self.g_qT), (11, self.g_kT)):
            w = self.load_unit(l, ("inA", ui))
            wv = w.t[:, 0:KC * 256].rearrange("p (k c) -> p k c", k=KC)
            for c in range(2):
                ps = self.pbank()
                self.mm_group(ps, ps.t[:, 0:T2], [(wv[:, kc, c * 128:(c + 1) * 128], hE.t[:, kc, 2:2 + T2]) for kc in range(KC)],
                              reads=[w.b, hE.b])
                self.cp("scalar", dst.t[:, c, :], ps.t[:, 0:T2], [ps.b], [dst.b])
        w0 = self.load_unit(l, ("inA", 12))
        w1 = self.load_unit(l, ("inA", 13))
        w2 = self.load_unit(l, ("inA", 14))
        wv0 = w0.t[:, 0:KC * 256].rearrange("p (k c) -> p k c", k=KC)
        wv1 = w1.t[:, 0:KC * 256].rearrange("p (k c) -> p k c", k=KC)
        wv2 = w2.t[:, 0:KC * 256].rearrange("p (k c) -> p k c", k=KC)
        for s in range(NSUB2):
            hs = [hE.t[:, kc, 2 + s * 128:2 + (s + 1) * 128] for kc in range(KC)]
            ps = self.pbank()
            self.mm_group(ps, ps.t[:, 0:256], [(hs[kc], wv0[:, kc, :]) for kc in range(KC)], reads=[w0.b, hE.b])
            self.mm_group(ps, ps.t[:, 256:512], [(hs[kc], wv1[:, kc, :]) for kc in range(KC)], reads=[w1.b, hE.b])
            self.cp("scalar", self.g_v.t[:, s, :], ps.t[:], [ps.b], [self.g_v.b])
            ps = self.pbank()
            self.mm_group(ps, ps.t[:, 0:256], [(hs[kc], wv2[:, kc, :]) for kc in range(KC)], reads=[w2.b, hE.b])
            self.cp("vector", self.g_ktm.t[:, s, :], ps.t[:, 0:256], [ps.b], [self.g_ktm.b])
        ob, _ = PR["gla_bg"]
        for s in range(NSUB2):
            ss = slice(s * 128, (s + 1) * 128)
            ch0 = (t0 + s * 128) // 64
            for d in range(2):
                ps = self.pbank()
                self.mm(ps, ps.t[:, 0:256], self.lrT.t[d * 32:d * 32 + 16, ss], self.wgb.t[d * 32:d * 32 + 16, :],
                        [self.lrT.b, self.wgb.b])
                gl = self.g_l
                self.tt("vector", gl.t[:], ps.t[:, 0:256], self.prb.t[:, ob + d * 256:ob + (d + 1) * 256], ALU.add,
                        [ps.b, self.prb.b], [gl.b])
                self.act(gl.t[:], gl.t[:], AF.Exp, [gl.b], [gl.b], scale=-1.0)
                self.act(gl.t[:], gl.t[:], AF.Ln, [gl.b, self.oneb.b], [gl.b], bias=self.oneb.t[:, 0:1])
                for c in range(2):
                    ps2 = self.pbank()
                    self.mm(ps2, ps2.t[:, 0:128], gl.t[:, c * 128:(c + 1) * 128], self.U[d].t[:], [gl.b, self.U[d].b])
                    eg, emg = self.g_eg[c], self.g_emg[c]
                    self.act(eg.t[:], ps2.t[:, 0:128], AF.Exp, [ps2.b], [eg.b], scale=-1.0 / 16)
                    self.act(emg.t[:], ps2.t[:, 0:128], AF.Exp, [ps2.b], [emg.b], scale=1.0 / 16)
                    self.stt(self.g_qg[d].t[:, c, ss], self.g_qT.t[:, c, ss], 0.125, eg.t[:], ALU.mult, ALU.mult,
                             [self.g_qT.b, eg.b], [self.g_qg[d].b])
                    self.tt("gpsimd", self.g_kmg[d].t[:, c, ss], self.g_kT.t[:, c, ss], emg.t[:], ALU.mult,
                            [self.g_kT.b, emg.b], [self.g_kmg[d].b])
                    src = eg.t[:, 63::64] if d == 0 else eg.t[:, 0::64]
                    self.cp("gpsimd", self.egl_gla.t[:, c, d, ch0:ch0 + 2], src, [eg.b], [self.egl_gla.b])
                ps3 = self.pbank()
                self.mm(ps3, ps3.t[:, 0:256], self.SU[d].t[:], gl.t[:], [gl.b, self.SU[d].b])
                self.act(self.g_ed.t[:], ps3.t[:, 0:256], AF.Exp, [ps3.b], [self.g_ed.b], scale=-1.0 / 16)
                self.tt("vector", self.g_kd[d].t[:, s, :], self.g_ktm.t[:, s, :], self.g_ed.t[:], ALU.mult,
                        [self.g_ktm.b, self.g_ed.b], [self.g_kd[d].b])
        ts = slice(t0, t0 + T2)
        items, rd = [], []
        for d in range(2):
            items.append((self.gla_qg[d, :, ts].rearrange("(c p) t -> p c t", p=128), self.g_qg[d].t[:]))
            items.append((self.gla_kmg[d, :, ts].rearrange("(c p) t -> p c t", p=128), self.g_kmg[d].t[:]))
            items.append((self.gla_kd[d, ts, :].rearrange("(s p) c -> p s c", p=128), self.g_kd[d].t[:]))
            rd += [self.g_qg[d].b, self.g_kmg[d].b, self.g_kd[d].b]
        items.append((self.gla_v[ts, :].rearrange("(s p) c -> p s c", p=128), self.g_v.t[:]))
        rd.append(self.g_v.b)
        k.dma_batch("gpsimd", items, reads=rd, prim=k.buf("st_gla"))

    def scan(self, l):
        k = self.k
        self.begin_scope()
        NST = self.NST
        if "gla" in self.mix:
            self.alloc_sc_gla()
        if "ssd" in self.mix:
            self.alloc_sc_ssd()
        if "gdn" in self.mix:
            self.alloc_sc_gdn()
        for n in range(NST):
            pp = n % 2
            sts = (n, NST - 1 - n)
            self.ld_items = [[], []]
            self.ld_bufs = [[], []]
            if "gla" in self.mix:
                self.sc_gla_load(sts, pp)
            if "ssd" in self.mix:
                self.sc_ssd_load(sts, pp)
            if "gdn" in self.mix:
                self.sc_gdn_load(sts, pp)
            for d in range(2):
                k.dma_batch("sync", self.ld_items[d], writes=self.ld_bufs[d], prim=k.buf("scin%d%d" % (d, pp)))
            self.st_items = [[], []]
            self.st_bufs = [[], []]
            if "gla" in self.mix:
                self.sc_gla_pre(sts, pp)
            for ci in range(2):
                if "gdn" in self.mix:
                    self.sc_gdn_step(sts, pp, ci)
                if "gla" in self.mix:
                    self.sc_gla_step(sts, pp, ci)
                if "ssd" in self.mix:
                    self.sc_ssd_step(sts, pp, ci)
            if "gla" in self.mix:
                self.sc_gla_out(sts, pp)
            if "ssd" in self.mix:
                self.sc_ssd_out(sts, pp)
            if "gdn" in self.mix:
                self.sc_gdn_out(sts, pp)
            for d in range(2):
                k.dma_batch("gpsimd", self.st_items[d], reads=self.st_bufs[d], prim=k.buf("scout%d%d" % (d, pp)))
        self.end_scope()

    def alloc_sc_gla(self):
        k = self.k
        R2 = range(2)
        self.s_gq = [[self.sb("s_gq%d%d" % (d, p), [128, 2, 128], BF16) for p in R2] for d in R2]
        self.s_gk = [[self.sb("s_gk%d%d" % (d, p), [128, 2, 128], BF16) for p in R2] for d in R2]
        self.s_gkd = [[self.sb("s_gkd%d%d" % (d, p), [128, 256], BF16) for p in R2] for d in R2]
        self.s_gv = [[self.sb("s_gv%d%d" % (d, p), [128, 512], BF16) for p in R2] for d in R2]
        self.s_gatt = [self.sb("s_gatt%d" % d, [128, 4, 128], BF16) for d in R2]
        self.s_gS = [self.sb("s_gS%d" % d, [128, 2, 128], F32) for d in R2]
        self.s_gSb = [self.sb("s_gSb%d" % d, [128, 2, 128], BF16) for d in R2]
        self.s_go = [[self.sb("s_go%d%d" % (d, p), [128, 4, 128], F32) for p in R2] for d in R2]
        self.s_gop = [None, None]
        for d in R2:
            self.ms(self.s_gS[d].t[:], 0.0, [self.s_gS[d].b])
            self.ms(self.s_gSb[d].t[:], 0.0, [self.s_gSb[d].b])

    def sc_gla_load(self, sts, pp):
        k = self.k
        for d in range(2):
            st = sts[d]
            ts = slice(st * 128, (st + 1) * 128)
            q, kk, kd, v = self.s_gq[d][pp], self.s_gk[d][pp], self.s_gkd[d][pp], self.s_gv[d][pp]
            self.ld_items[d] += [(q.t[:], self.gla_qg[d, :, ts].rearrange("(c p) t -> p c t", p=128)),
                                 (kk.t[:], self.gla_kmg[d, :, ts].rearrange("(c p) t -> p c t", p=128)),
                                 (kd.t[:], self.gla_kd[d, ts, :]), (v.t[:], self.gla_v[ts, :])]
            self.ld_bufs[d] += [q.b, kk.b, kd.b, v.b]

    def sc_gla_pre(self, sts, pp):
        for d in range(2):
            q, kk = self.s_gq[d][pp], self.s_gk[d][pp]
            ps = self.pbank()
            for h in range(4):
                c, r = h // 2, h % 2
                rs = slice(r * 64, (r + 1) * 64)
                self.mm(ps, ps.t[:, h * 128:(h + 1) * 128], kk.t[rs, c, :], q.t[rs, c, :], [kk.b, q.b])
            att = self.s_gatt[d]
            self.tt("vector", att.t[:], ps.t[:].rearrange("p (h i) -> p h i", h=4),
                    self.U[d].t[:, None, :].to_broadcast([128, 4, 128]), ALU.mult, [ps.b, self.U[d].b], [att.b])

    def sc_gla_step(self, sts, pp, ci):
        k = self.k
        for d in range(2):
            st = sts[d]
            ch = ci if d == 0 else 1 - ci
            cs = slice(ch * 64, (ch + 1) * 64)
            chabs = st * 2 + ch
            q, kd, v = self.s_gq[d][pp], self.s_gkd[d][pp], self.s_gv[d][pp]
            att, S, Sb = self.s_gatt[d], self.s_gS[d], self.s_gSb[d]
            op = self.pbank()
            for h in range(4):
                c, r = h // 2, h % 2
                rs = slice(r * 64, (r + 1) * 64)
                oap = op.t[:, h * 64:(h + 1) * 64]
                self.mm(op, oap, Sb.t[rs, c, :], q.t[rs, c, cs], [Sb.b, q.b], start=True, stop=False)
                self.mm(op, oap, v.t[:, h * 128:(h + 1) * 128], att.t[:, h, cs], [v.b, att.b], start=False, stop=True,
                        acc=True)
            self.cp("vector", self.s_go[d][pp].t[:, :, cs], op.t[:, 0:256].rearrange("p (h i) -> p h i", h=4), [op.b],
                    [self.s_go[d][pp].b])
            pP = self.pbank()
            for h in range(4):
                c, r = h // 2, h % 2
                self.mm(pP, pP.t[r * 64:(r + 1) * 64, c * 128:(c + 1) * 128], kd.t[cs, h * 64:(h + 1) * 64],
                        v.t[cs, h * 128:(h + 1) * 128], [kd.b, v.b])
            for c in range(2):
                self.stt(S.t[:, c, :], S.t[:, c, :], self.egl_gla.t[:, c, d, chabs:chabs + 1],
                         pP.t[:, c * 128:(c + 1) * 128], ALU.mult, ALU.add, [S.b, pP.b, self.egl_gla.b], [S.b])
            self.cp("scalar", Sb.t[:], S.t[:], [S.b], [Sb.b])

    def sc_gla_out(self, sts, pp):
        k = self.k
        for d in range(2):
            st = sts[d]
            ts = slice(st * 128, (st + 1) * 128)
            o = self.s_go[d][pp]
            self.st_items[d].append((self.gla_o[d, :, ts].rearrange("(h p) t -> p h t", p=128), o.t[:]))
            self.st_bufs[d].append(o.b)

    def alloc_p2_common(self, l):
        k = self.k
        self.nega = self.sb("nega", [128, 32], F32)
        o, _ = PR["sm_alog"]
        self.act(self.nega.t[:], self.prb.t[:, o:o + 32], AF.Exp, [self.prb.b], [self.nega.b])
        self.ts("vector", self.nega.t[:], self.nega.t[:], -1.0, None, ALU.mult, None, [self.nega.b], [self.nega.b])
        self.sp = self.sb("sp", [128, NSUB2, 24], F32)
        self.gd = self.sb("gd", [128, NSUB2, 24], F32)
        self.lnb = self.sb("lnb", [128, NSUB2, 8], F32)
        self.beta = self.sb("beta", [128, NSUB2, 8], F32)
        self.lndt = self.sb("lndt", [128, NSUB2, 16], F32)
        self.c_zr = [self.sb("c_zr%d" % i, [128, TE2], BF16) for i in range(2)]
        self.c_dg = [self.sb("c_dg%d" % i, [128, 5, 128], BF16) for i in range(2)]
        self.c_n = 0

    def p2_smalls_post(self, l, t):
        k = self.k
        sm, sp, gd = self.sm, self.sp, self.gd
        t0 = t * T2
        self.act(sp.t[:], sm.t[:, :, 8:32], AF.Exp, [sm.b], [sp.b])
        self.act(sp.t[:], sp.t[:], AF.Ln, [sp.b, self.oneb.b], [sp.b], bias=self.oneb.t[:, 0:1])
        self.tt("vector", gd.t[:], sp.t[:], self.nega.t[:, None, 8:32].to_broadcast([128, NSUB2, 24]), ALU.mult,
                [sp.b, self.nega.b], [gd.b])
        self.act(self.lndt.t[:], sp.t[:, :, 8:24], AF.Ln, [sp.b], [self.lndt.b])
        self.act(self.lnb.t[:], sm.t[:, :, 0:8], AF.Exp, [sm.b], [self.lnb.b], scale=-1.0)
        self.act(self.lnb.t[:], self.lnb.t[:], AF.Ln, [self.lnb.b, self.oneb.b], [self.lnb.b], bias=self.oneb.t[:, 0:1])
        self.act(self.beta.t[:], self.lnb.t[:], AF.Exp, [self.lnb.b], [self.beta.b], scale=-1.0)
        self.ts("vector", self.lnb.t[:], self.lnb.t[:], -1.0, None, ALU.mult, None, [self.lnb.b], [self.lnb.b])
        ps = self.pbank()
        for s in range(NSUB2):
            for c in range(2):
                j = s * 2 + c
                self.mm(ps, ps.t[:, j * 24:(j + 1) * 24], self.cind[c].t[:], gd.t[:, s, :], [self.cind[c].b, gd.b])
        ch0 = t0 // 64
        n = NSUB2 * 2
        self.act(self.egl.t[:, ch0:ch0 + n, :], ps.t[:, 0:n * 24].rearrange("p (j c) -> p j c", j=n), AF.Exp,
                 [ps.b], [self.egl.b])

    def conv_chunk(self, l, w, wcols, cwname, cc, bias_ap, out_ap, out_buf):
        k = self.k
        hE = self.hE
        ps = self.pbank()
        self.mm_group(ps, ps.t[:, 0:T2], [(wcols(kc), hE.t[:, kc, 0:T2]) for kc in range(KC)], reads=[w.b, hE.b])
        self.mm_group(ps, ps.t[:, T2:TE2], [(wcols(kc), hE.t[:, kc, T2:TE2]) for kc in range(KC)], reads=[w.b, hE.b])
        zr = self.c_zr[self.c_n % 2]
        dg = self.c_dg[self.c_n % 2]
        self.c_n += 1
        self.cp("scalar", zr.t[:], ps.t[:, 0:TE2], [ps.b], [zr.b])
        o, _ = PV[cwname]
        for kk in range(5):
            self.ts("vector", dg.t[:, kk, :], self.ident_bf.t[:], self.pv.t[:, l, o + cc * 5 + kk:o + cc * 5 + kk + 1], None,
                    ALU.mult, None, [self.ident_bf.b, self.pv.b], [dg.b])
        pc = self.pbank()
        self.mm_group(pc, pc.t[:, 0:T2], [(dg.t[:, kk, :], zr.t[:, kk:kk + T2]) for kk in range(5)], reads=[dg.b, zr.b])
        if bias_ap is not None:
            self.act(out_ap, pc.t[:, 0:T2], AF.Silu, [pc.b, self.pv.b], [out_buf], bias=bias_ap)
        else:
            self.act(out_ap, pc.t[:, 0:T2], AF.Silu, [pc.b], [out_buf])

    def alloc_p2_ssd(self, l):
        R2 = range(2)
        self.d_xbc = self.sb("d_xbc", [128, 8, T2], BF16)
        self.d_xtm = self.sb("d_xtm", [128, NSUB2, 512], BF16)
        self.d_Btm = self.sb("d_Btm", [128, NSUB2, 256], BF16)
        self.d_xw = [self.sb("d_xw%d" % d, [128, NSUB2, 512], BF16) for d in R2]
        self.d_yd = self.sb("d_yd", [128, 4, T2], F32)
        self.d_cexp = self.sb("d_cexp", [128, 16, 128], BF16)
        self.d_MT = self.sb("d_MT", [128, 16, 128], BF16)
        self.d_E = [self.sb("d_E%d" % i, [128, 128], F32) for i in range(4)]
        self.d_E1 = [self.sb("d_E1%d" % i, [128, 128], F32) for i in range(4)]
        self.d_sc = self.sb("d_sc", [128, 2, 128], F32)
        self.d_acum = self.sb("d_acum", [128, 16], F32)
        self.d_nb = self.sb("d_nb", [128, 16], F32)
        self.d_wst = self.sb("d_wst", [128, 16], F32)

    def p2_ssd(self, l, t):
        k = self.k
        t0 = t * T2
        ocb, _ = PV["ssd_cb"]
        od, _ = PV["ssd_d"]
        for ui in range(6, 10):
            w = self.load_unit(l, ("inA", ui))
            wv = w.t[:, 0:KC * 256].rearrange("p (k c) -> p k c", k=KC)
            for c2 in range(2):
                cc = (ui - 6) * 2 + c2
                self.conv_chunk(l, w, lambda kc, c2=c2, wv=wv: wv[:, kc, c2 * 128:(c2 + 1) * 128], "ssd_cw", cc,
                                self.pv.t[:, l, ocb + cc:ocb + cc + 1], self.d_xbc.t[:, cc, :], self.d_xbc.b)
        xbc = self.d_xbc
        for s in range(NSUB2):
            ss = slice(s * 128, (s + 1) * 128)
            tok = slice(t0 + s * 128, t0 + (s + 1) * 128)
            pT = self.ptr
            pTv = pT.t[:]
            if getattr(self, 'cut', 99) <= -1:
                continue
            for c in range(6):
                self.tr(pTv[:, c * 128:(c + 1) * 128], xbc.t[:, c, ss], [xbc.b], pT)
            if getattr(self, 'cut', 99) <= 0:
                continue
            self.cp("scalar", self.d_xtm.t[:, s, :], pTv[:, 0:512], [pT.b], [self.d_xtm.b])
            self.cp("vector", self.d_Btm.t[:, s, :], pTv[:, 512:768], [pT.b], [self.d_Btm.b])
            if getattr(self, 'cut', 99) <= 1:
                continue
            pa = self.pbank()
            for d in range(2):
                self.mm(pa, pa.t[:, d * 8:(d + 1) * 8], self.U[d].t[:], self.gd.t[:, s, 8 + d * 8:16 + d * 8],
                        [self.U[d].b, self.gd.b])
            for d in range(2):
                self.mm(pa, pa.t[:, 16 + d * 8:24 + d * 8], self.SU[d].t[:], self.gd.t[:, s, 8 + d * 8:16 + d * 8],
                        [self.SU[d].b, self.gd.b])
            self.cp("vector", self.d_acum.t[:], pa.t[:, 0:16], [pa.b], [self.d_acum.b])
            self.act(self.d_wst.t[:], pa.t[:, 16:32], AF.Exp, [pa.b], [self.d_wst.b])
            self.tt("vector", self.d_wst.t[:], self.d_wst.t[:], self.sp.t[:, s, 8:24], ALU.mult,
                    [self.d_wst.b, self.sp.b], [self.d_wst.b])
            self.tt("vector", self.d_nb.t[:], self.lndt.t[:, s, :], self.d_acum.t[:], ALU.subtract,
                    [self.lndt.b, self.d_acum.b], [self.d_nb.b])
            if getattr(self, 'cut', 99) <= 2:
                continue
            for d in range(2):
                self.tt("gpsimd", self.d_xw[d].t[:, s, :].rearrange("p (h q) -> p h q", h=8),
                        self.d_xtm.t[:, s, :].rearrange("p (h q) -> p h q", h=8),
                        self.d_wst.t[:, d * 8:(d + 1) * 8, None].to_broadcast([128, 8, 64]), ALU.mult,
                        [self.d_xtm.b, self.d_wst.b], [self.d_xw[d].b])
            if getattr(self, 'cut', 99) <= 3:
                continue
            pS = self.pbank()
            for g in range(2):
                self.mm(pS, pS.t[:, g * 128:(g + 1) * 128], xbc.t[:, 4 + g, ss], xbc.t[:, 6 + g, ss], [xbc.b])
            self.cp("scalar", self.d_sc.t[:], pS.t[:, 0:256].rearrange("p (g i) -> p g i", g=2), [pS.b], [self.d_sc.b])
            if getattr(self, 'cut', 99) <= 4:
                continue
            for d in range(2):
                for h in range(8):
                    dh = d * 8 + h
                    g = h // 4
                    pA = self.pbank()
                    bc = self.d_acum.t[:, dh:dh + 1].to_broadcast([128, 128])
                    self.mm(pA, pA.t[:, 0:128], bc, self.ident.t[:], [self.d_acum.b, self.ident.b])
                    self.mm(pA, pA.t[:, 128:256], bc, self.ident.t[:], [self.d_acum.b, self.ident.b], start=True, stop=False)
                    self.mm(pA, pA.t[:, 128:256], self.ident.t[:], self.maskneg[d].t[:], [self.ident.b, self.maskneg[d].b],
                            start=False, stop=True, acc=True)
                    E1, E = self.d_E1[dh % 4], self.d_E[dh % 4]
                    self.act(E1.t[:], pA.t[:, 0:128], AF.Exp, [pA.b], [E1.b])
                    self.act(E.t[:], pA.t[:, 128:256], AF.Exp, [pA.b, self.d_nb.b], [E.b], bias=self.d_nb.t[:, dh:dh + 1])
                    self.tt("gpsimd", self.d_cexp.t[:, dh, :], xbc.t[:, 6 + g, ss], E1.t[:], ALU.mult,
                            [xbc.b, E1.b], [self.d_cexp.b])
                    self.tt("vector", self.d_MT.t[:, dh, :], E.t[:], self.d_sc.t[:, g, :], ALU.mult,
                            [E.b, self.d_sc.b], [self.d_MT.b])
            if getattr(self, 'cut', 99) <= 5:
                continue
            pY = self.pbank()
            for h in range(8):
                c, r = h // 2, h % 2
                for d in range(2):
                    self.mm(pY, pY.t[r * 64:(r + 1) * 64, c * 128:(c + 1) * 128], self.d_xtm.t[:, s, h * 64:(h + 1) * 64],
                            self.d_MT.t[:, d * 8 + h, :], [self.d_xtm.b, self.d_MT.b], start=(d == 0), stop=(d == 1),
                            acc=(d == 1))
            for c in range(4):
                self.stt(self.d_yd.t[:, c, ss], xbc.t[:, c, ss], self.pv.t[:, l, od + c:od + c + 1],
                         pY.t[:, c * 128:(c + 1) * 128], ALU.mult, ALU.add, [xbc.b, pY.b, self.pv.b], [self.d_yd.b])
            if getattr(self, 'cut', 99) <= 6:
                continue
            k.dma_batch("gpsimd", [(self.ssd_cexp[d, :, tok].rearrange("(h p) t -> p h t", p=128),
                                    self.d_cexp.t[:, d * 8:(d + 1) * 8, :]) for d in range(2)],
                        reads=[self.d_cexp.b], prim=k.buf("st_ssd_s"))
        if getattr(self, 'cut', 99) <= 7:
            return
        ts = slice(t0, t0 + T2)
        items = [(self.ssd_B[ts, :].rearrange("(s p) c -> p s c", p=128), self.d_Btm.t[:]),
                 (self.ssd_ydiag[:, ts].rearrange("(c p) t -> p c t", p=128), self.d_yd.t[:])]
        for d in range(2):
            items.append((self.ssd_xw[d, ts, :].rearrange("(s p) c -> p s c", p=128), self.d_xw[d].t[:]))
        k.dma_batch("gpsimd", items, reads=[self.d_Btm.b, self.d_yd.b, self.d_xw[0].b, self.d_xw[1].b],
                    prim=k.buf("st_ssd_t"))

    def alloc_sc_ssd(self):
        k = self.k
        R2 = range(2)
        self.s_dc = [[self.sb("s_dc%d%d" % (d, p), [128, 8, 128], BF16) for p in R2] for d in R2]
        self.s_dB = [[self.sb("s_dB%d%d" % (d, p), [128, 256], BF16) for p in R2] for d in R2]
        self.s_dxw = [[self.sb("s_dxw%d%d" % (d, p), [128, 512], BF16) for p in R2] for d in R2]
        self.s_dS = [self.sb("s_dS%d" % d, [128, 512], F32) for d in R2]
        self.s_dSb = [self.sb("s_dSb%d" % d, [128, 512], BF16) for d in R2]
        self.s_dy = [[self.sb("s_dy%d%d" % (d, p), [128, 4, 128], F32) for p in R2] for d in R2]
        self.s_dop = [None, None]
        for d in R2:
            self.ms(self.s_dS[d].t[:], 0.0, [self.s_dS[d].b])
            self.ms(self.s_dSb[d].t[:], 0.0, [self.s_dSb[d].b])

    def sc_ssd_load(self, sts, pp):
        k = self.k
        for d in range(2):
            st = sts[d]
            ts = slice(st * 128, (st + 1) * 128)
            c, B, xw = self.s_dc[d][pp], self.s_dB[d][pp], self.s_dxw[d][pp]
            self.ld_items[d] += [(c.t[:], self.ssd_cexp[d, :, ts].rearrange("(h p) t -> p h t", p=128)),
                                 (B.t[:], self.ssd_B[ts, :]), (xw.t[:], self.ssd_xw[d, ts, :])]
            self.ld_bufs[d] += [c.b, B.b, xw.b]

    def sc_ssd_step(self, sts, pp, ci):
        k = self.k
        for d in range(2):
            st = sts[d]
            ch = ci if d == 0 else 1 - ci
            cs = slice(ch * 64, (ch + 1) * 64)
            chabs = st * 2 + ch
            c, B, xw = self.s_dc[d][pp], self.s_dB[d][pp], self.s_dxw[d][pp]
            S, Sb = self.s_dS[d], self.s_dSb[d]
            op = self.pbank()
            for h in range(8):
                cc, r = h // 2, h % 2
                self.mm(op, op.t[r * 64:(r + 1) * 64, cc * 64:(cc + 1) * 64],
                        Sb.t[:, h * 64:(h + 1) * 64], c.t[:, h, cs], [Sb.b, c.b])
            self.cp("vector", self.s_dy[d][pp].t[:, :, cs], op.t[:, 0:256].rearrange("p (c i) -> p c i", c=4), [op.b],
                    [self.s_dy[d][pp].b])
            pP = self.pbank()
            for g in range(2):
                self.mm(pP, pP.t[:, g * 256:(g + 1) * 256], B.t[cs, g * 128:(g + 1) * 128], xw.t[cs, g * 256:(g + 1) * 256],
                        [B.b, xw.b])
            for h in range(8):
                self.stt(S.t[:, h * 64:(h + 1) * 64], S.t[:, h * 64:(h + 1) * 64],
                         self.egl.t[:, chabs, 8 + d * 8 + h:8 + d * 8 + h + 1], pP.t[:, h * 64:(h + 1) * 64], ALU.mult, ALU.add,
                         [S.b, pP.b, self.egl.b], [S.b])
            self.cp("scalar", Sb.t[:], S.t[:], [S.b], [Sb.b])

    def sc_ssd_out(self, sts, pp):
        k = self.k
        for d in range(2):
            st = sts[d]
            ts = slice(st * 128, (st + 1) * 128)
            y = self.s_dy[d][pp]
            self.st_items[d].append((self.ssd_yoff[d, :, ts].rearrange("(c p) t -> p c t", p=128), y.t[:]))
            self.st_bufs[d].append(y.b)

    def alloc_p2_gdn(self, l):
        k = self.k
        R2 = range(2)
        self.e_raw = self.sb("e_raw", [128, 8, T2], F32)
        self.e_q = self.sb("e_q", [128, 4, T2], BF16)
        self.e_k = self.sb("e_k", [128, 4, T2], BF16)
        self.e_v = self.sb("e_v", [128, 4, T2], BF16)
        self.e_sq = self.sb("e_sq", [128, T2], BF16)
        self.e_rs = self.sb("e_rs", [128, T2], F32)
        self.e_ktm = self.sb("e_ktm", [128, 4, 128], BF16)
        self.e_vtm = self.sb("e_vtm", [128, 4, 128], BF16)
        self.e_kg = [self.sb("e_kg%d" % d, [128, NSUB2, 512], BF16) for d in R2]
        self.e_kbg = [self.sb("e_kbg%d" % d, [128, 4, 128], BF16) for d in R2]
        self.e_vb = [self.sb("e_vb%d" % d, [128, 4, 128], BF16) for d in R2]
        self.e_kkqk = [self.sb("e_kkqk%d" % h, [128, 2, 128], F32) for h in range(4)]
        self.e_B = [[self.sb("e_B%d_%d" % (u, p), [128, 2, 128], BF16) for p in R2] for u in range(8)]
        self.e_XT = [[self.sb("e_XT%d_%d" % (u, p), [128, 128], BF16) for p in R2] for u in range(8)]
        self.e_E = [self.sb("e_E%d" % i, [128, 128], F32) for i in range(8)]
        self.e_En = 0
        self.e_att = [self.sb("e_att%d" % d, [128, 4, 128], BF16) for d in R2]
        self.e_u = [self.sb("e_u%d" % d, [128, 4, 128], F32) for d in R2]
        self.e_wn = [self.sb("e_wn%d" % d, [128, 4, 128], BF16) for d in R2]
        self.e_qg = [self.sb("e_qg%d" % d, [128, 4, T2], BF16) for d in R2]
        self.e_G = self.sb("e_G", [128, 8], F32)
        self.e_nG = self.sb("e_nG", [128, 8], F32)
        self.e_rA = self.sb("e_rA", [128, 8], F32)
        self.e_egs = self.sb("e_egs", [128, 8], F32)
        self.e_bG = self.sb("e_bG", [128, 8], F32)
        self.mposA = [self.sb("mposA%d" % d, [128, 128], F32) for d in R2]
        self.mnegAT = [self.sb("mnegAT%d" % d, [128, 128], F32) for d in R2]
        for d in R2:
            self.ts("vector", self.mposA[d].t[:], self.SU[d].t[:], -30000.0, 30000.0, ALU.mult, ALU.add,
                    [self.SU[d].b], [self.mposA[d].b])
            self.ts("vector", self.mnegAT[d].t[:], self.SU[1 - d].t[:], 30000.0, -30000.0, ALU.mult, ALU.add,
                    [self.SU[1 - d].b], [self.mnegAT[d].b])

    def nextE(self):
        t = self.e_E[self.e_En % 8]
        self.e_En += 1
        return t

    def p2_gdn(self, l, t):
        k = self.k
        t0 = t * T2
        for ui in range(6):
            w = self.load_unit(l, ("inA", ui))
            wv = w.t[:, 0:KC * 256].rearrange("p (k c) -> p k c", k=KC)
            for c2 in range(2):
                cc = ui * 2 + c2
                if cc < 8:
                    out_ap, ob = self.e_raw.t[:, cc, :], self.e_raw.b
                else:
                    out_ap, ob = self.e_v.t[:, cc - 8, :], self.e_v.b
                self.conv_chunk(l, w, lambda kc, c2=c2, wv=wv: wv[:, kc, c2 * 128:(c2 + 1) * 128], "gdn_cw", cc, None,
                                out_ap, ob)
        for cc in range(8):
            raw = self.e_raw.t[:, cc, :]
            self.act(self.e_sq.t[:], raw, AF.Square, [self.e_raw.b], [self.e_sq.b])
            ps = self.pbank()
            self.mm(ps, ps.t[:, 0:T2], self.ones_bf.t[:], self.e_sq.t[:], [self.ones_bf.b, self.e_sq.b])
            self.rstd_from_ss(ps.t[:, 0:T2], ps.b, self.e_rs, 1)
            dst = self.e_q if cc < 4 else self.e_k
            self.stt(dst.t[:, cc % 4, :], raw, (128 ** -0.5) if cc < 4 else 1.0, self.e_rs.t[:], ALU.mult, ALU.mult,
                     [self.e_raw.b, self.e_rs.b], [dst.b])
        for s in range(NSUB2):
            ss = slice(s * 128, (s + 1) * 128)
            tok = slice(t0 + s * 128, t0 + (s + 1) * 128)
            st = (t0 + s * 128) // 128
            pa = self.pbank()
            for d in range(2):
                self.mm(pa, pa.t[:, d * 4:(d + 1) * 4], self.U[d].t[:], self.gd.t[:, s, d * 4:(d + 1) * 4],
                        [self.U[d].b, self.gd.b])
            for d in range(2):
                self.mm(pa, pa.t[:, 8 + d * 4:12 + d * 4], self.SU[d].t[:], self.gd.t[:, s, d * 4:(d + 1) * 4],
                        [self.SU[d].b, self.gd.b])
            self.cp("vector", self.e_G.t[:], pa.t[:, 0:8], [pa.b], [self.e_G.b])
            self.act(self.e_egs.t[:], pa.t[:, 8:16], AF.Exp, [pa.b], [self.e_egs.b])
            self.act(self.e_bG.t[:], self.e_G.t[:], AF.Exp, [self.e_G.b], [self.e_bG.b])
            self.tt("vector", self.e_bG.t[:], self.e_bG.t[:], self.beta.t[:, s, :], ALU.mult, [self.e_bG.b, self.beta.b],
                    [self.e_bG.b])
            self.tt("vector", self.e_rA.t[:], self.e_G.t[:], self.lnb.t[:, s, :], ALU.add, [self.e_G.b, self.lnb.b],
                    [self.e_rA.b])
            self.ts("vector", self.e_nG.t[:], self.e_G.t[:], -1.0, None, ALU.mult, None, [self.e_G.b], [self.e_nG.b])
            pT = self.ptr
            for h in range(4):
                self.tr(pT.t[:, h * 128:(h + 1) * 128], self.e_k.t[:, h, ss], [self.e_k.b], pT)
            for h in range(4):
                self.tr(pT.t[:, 512 + h * 128:512 + (h + 1) * 128], self.e_v.t[:, h, ss], [self.e_v.b], pT)
            self.cp("scalar", self.e_ktm.t[:], pT.t[:, 0:512].rearrange("p (h c) -> p h c", h=4), [pT.b], [self.e_ktm.b])
            self.cp("vector", self.e_vtm.t[:], pT.t[:, 512:1024].rearrange("p (h c) -> p h c", h=4), [pT.b], [self.e_vtm.b])
            for d in range(2):
                ds_ = slice(d * 4, (d + 1) * 4)
                bc = lambda tl: tl.t[:, ds_, None].to_broadcast([128, 4, 128])
                self.tt("gpsimd", self.e_kbg[d].t[:], self.e_ktm.t[:], self.e_bG.t[:, d * 4:(d + 1) * 4, None].to_broadcast([128, 4, 128]),
                        ALU.mult, [self.e_ktm.b, self.e_bG.b], [self.e_kbg[d].b])
                self.tt("gpsimd", self.e_kg[d].t[:, s, :].rearrange("p (h c) -> p h c", h=4), self.e_ktm.t[:],
                        self.e_egs.t[:, d * 4:(d + 1) * 4, None].to_broadcast([128, 4, 128]), ALU.mult,
                        [self.e_ktm.b, self.e_egs.b], [self.e_kg[d].b])
                self.tt("gpsimd", self.e_vb[d].t[:], self.e_vtm.t[:],
                        self.beta.t[:, s, d * 4:(d + 1) * 4, None].to_broadcast([128, 4, 128]), ALU.mult,
                        [self.e_vtm.b, self.beta.b], [self.e_vb[d].b])
            for h in range(4):
                ps = self.pbank()
                self.mm(ps, ps.t[:, 0:128], self.e_k.t[:, h, ss], self.e_k.t[:, h, ss], [self.e_k.b])
                self.mm(ps, ps.t[:, 128:256], self.e_k.t[:, h, ss], self.e_q.t[:, h, ss], [self.e_k.b, self.e_q.b])
                self.cp("scalar", self.e_kkqk[h].t[:], ps.t[:, 0:256].rearrange("p (a i) -> p a i", a=2), [ps.b],
                        [self.e_kkqk[h].b])
            units = [(d, h) for d in range(2) for h in range(4)]
            for ui_, (d, h) in enumerate(units):
                dh = d * 4 + h
                kk, qk = self.e_kkqk[h].t[:, 0, :], self.e_kkqk[h].t[:, 1, :]
                kb_ = self.e_kkqk[h].b
                B0 = self.e_B[ui_][0]
                pA = self.pbank()
                bcG = self.e_G.t[:, dh:dh + 1].to_broadcast([128, 128])
                bcR = self.e_rA.t[:, dh:dh + 1].to_broadcast([128, 128])
                idt = self.ident
                self.mm(pA, pA.t[:, 0:128], bcG, idt.t[:], [self.e_G.b, idt.b], start=True, stop=False)
                self.mm(pA, pA.t[:, 0:128], idt.t[:], self.mposA[d].t[:], [idt.b, self.mposA[d].b], start=False, stop=True, acc=True)
                self.mm(pA, pA.t[:, 128:256], bcR, idt.t[:], [self.e_rA.b, idt.b], start=True, stop=False)
                self.mm(pA, pA.t[:, 128:256], idt.t[:], self.mnegAT[d].t[:], [idt.b, self.mnegAT[d].b], start=False, stop=True, acc=True)
                self.mm(pA, pA.t[:, 256:384], bcG, idt.t[:], [self.e_G.b, idt.b], start=True, stop=False)
                self.mm(pA, pA.t[:, 256:384], idt.t[:], self.maskneg[d].t[:], [idt.b, self.maskneg[d].b], start=False, stop=True, acc=True)
                self.mm(pA, pA.t[:, 384:512], bcG, idt.t[:], [self.e_G.b, idt.b])
                E = self.nextE()
                self.act(E.t[:], pA.t[:, 0:128], AF.Exp, [pA.b, self.e_rA.b], [E.b], scale=-1.0, bias=self.e_rA.t[:, dh:dh + 1])
                self.stt(B0.t[:, 0, :], E.t[:], -1.0, kk, ALU.mult, ALU.mult, [E.b, kb_], [B0.b])
                E = self.nextE()
                self.act(E.t[:], pA.t[:, 128:256], AF.Exp, [pA.b, self.e_nG.b], [E.b], bias=self.e_nG.t[:, dh:dh + 1])
                self.stt(B0.t[:, 1, :], E.t[:], -1.0, kk, ALU.mult, ALU.mult, [E.b, kb_], [B0.b])
                E = self.nextE()
                self.act(E.t[:], pA.t[:, 256:384], AF.Exp, [pA.b, self.e_nG.b], [E.b], bias=self.e_nG.t[:, dh:dh + 1])
                self.tt("vector", self.e_att[d].t[:, h, :], E.t[:], qk, ALU.mult, [E.b, kb_], [self.e_att[d].b])
                E = self.nextE()
                self.act(E.t[:], pA.t[:, 384:512], AF.Exp, [pA.b], [E.b])
                self.tt("gpsimd", self.e_qg[d].t[:, h, ss], self.e_q.t[:, h, ss], E.t[:], ALU.mult, [self.e_q.b, E.b],
                        [self.e_qg[d].b])
                self.tt("gpsimd", self.e_XT[ui_][0].t[:], B0.t[:, 1, :], self.ident.t[:], ALU.add, [B0.b, self.ident.b],
                        [self.e_XT[ui_][0].b])
            for lev in range(1, 6):
                pi, po = (lev - 1) % 2, lev % 2
                n = 2 if lev < 5 else 1
                sqb = []
                for pr in range(4):
                    ps = self.pbank()
                    for a in range(2):
                        Bp = self.e_B[2 * pr + a][pi]
                        self.mm(ps, ps.t[:, a * 256:a * 256 + 128], Bp.t[:, 1, :], Bp.t[:, 0, :], [Bp.b])
                        if lev < 5:
                            self.mm(ps, ps.t[:, a * 256 + 128:a * 256 + 256], Bp.t[:, 0, :], Bp.t[:, 1, :], [Bp.b])
                    sqb.append(ps)
                for ui_ in range(8):
                    ps = sqb[ui_ // 2]
                    a = ui_ % 2
                    Bn = self.e_B[ui_][po]
                    self.cp("scalar", Bn.t[:, 0:n, :], ps.t[:, a * 256:a * 256 + n * 128].rearrange("p (a i) -> p a i", a=n),
                            [ps.b], [Bn.b])
                prb_ = []
                for hf in range(2):
                    ps2 = self.pbank()
                    for a in range(4):
                        ui_ = hf * 4 + a
                        Bn, Xp = self.e_B[ui_][po], self.e_XT[ui_][pi]
                        self.mm(ps2, ps2.t[:, a * 128:(a + 1) * 128], Bn.t[:, 0, :], Xp.t[:], [Bn.b, Xp.b])
                    prb_.append(ps2)
                for ui_ in range(8):
                    ps2 = prb_[ui_ // 4]
                    a = ui_ % 4
                    Xp, Xn = self.e_XT[ui_][pi], self.e_XT[ui_][po]
                    self.tt("vector", Xn.t[:], Xp.t[:], ps2.t[:, a * 128:(a + 1) * 128], ALU.add, [Xp.b, ps2.b], [Xn.b])
            fin = 5 % 2
            for d in range(2):
                pu = self.pbank()
                pw = self.pbank()
                for h in range(4):
                    X = self.e_XT[d * 4 + h][fin]
                    self.mm(pu, pu.t[:, h * 128:(h + 1) * 128], X.t[:], self.e_vb[d].t[:, h, :], [X.b, self.e_vb[d].b])
                for h in range(4):
                    X = self.e_XT[d * 4 + h][fin]
                    self.mm(pw, pw.t[:, h * 128:(h + 1) * 128], self.e_kbg[d].t[:, h, :], X.t[:], [X.b, self.e_kbg[d].b])
                self.cp("scalar", self.e_u[d].t[:], pu.t[:].rearrange("p (h c) -> p h c", h=4), [pu.b], [self.e_u[d].b])
                self.ts("vector", self.e_wn[d].t[:], pw.t[:].rearrange("p (h c) -> p h c", h=4), -1.0, None, ALU.mult, None,
                        [pw.b], [self.e_wn[d].b])
            items, rd = [], []
            for d in range(2):
                items.append((self.gdn_u[d, tok, :], self.e_u[d].t[:].rearrange("p h c -> p (h c)")))
                items.append((self.gdn_wn[d, :, tok].rearrange("(h p) t -> p h t", p=128), self.e_wn[d].t[:]))
                items.append((self.gdn_att[d, st * 128:(st + 1) * 128, :], self.e_att[d].t[:].rearrange("p h c -> p (h c)")))
                rd += [self.e_u[d].b, self.e_wn[d].b, self.e_att[d].b]
            k.dma_batch("gpsimd", items, reads=rd, prim=k.buf("st_gdn_s"))
        ts = slice(t0, t0 + T2)
        items, rd = [], []
        for d in range(2):
            items.append((self.gdn_kg[d, ts, :].rearrange("(s p) c -> p s c", p=128), self.e_kg[d].t[:]))
            items.append((self.gdn_qg[d, :, ts].rearrange("(h p) t -> p h t", p=128), self.e_qg[d].t[:]))
            rd += [self.e_kg[d].b, self.e_qg[d].b]
        k.dma_batch("gpsimd", items, reads=rd, prim=k.buf("st_gdn_t"))

    def alloc_sc_gdn(self):
        k = self.k
        R2 = range(2)
        self.s_ewn = [[self.sb("s_ewn%d%d" % (d, p), [128, 4, 128], BF16) for p in R2] for d in R2]
        self.s_eu = [[self.sb("s_eu%d%d" % (d, p), [128, 512], F32) for p in R2] for d in R2]
        self.s_eqg = [[self.sb("s_eqg%d%d" % (d, p), [128, 4, 128], BF16) for p in R2] for d in R2]
        self.s_ekg = [[self.sb("s_ekg%d%d" % (d, p), [128, 512], BF16) for p in R2] for d in R2]
        self.s_eatt = [[self.sb("s_eatt%d%d" % (d, p), [128, 512], BF16) for p in R2] for d in R2]
        self.s_eS = [self.sb("s_eS%d" % d, [128, 4, 128], F32) for d in R2]
        self.s_eSb = [self.sb("s_eSb%d" % d, [128, 4, 128], BF16) for d in R2]
        self.s_evn = [self.sb("s_evn%d" % d, [128, 512], BF16) for d in R2]
        self.s_eo = [[self.sb("s_eo%d%d" % (d, p), [128, 4, 128], F32) for p in R2] for d in R2]
        self.s_eop = [None, None]
        for d in R2:
            self.ms(self.s_eS[d].t[:], 0.0, [self.s_eS[d].b])
            self.ms(self.s_eSb[d].t[:], 0.0, [self.s_eSb[d].b])

    def sc_gdn_load(self, sts, pp):
        k = self.k
        for d in range(2):
            st = sts[d]
            ts = slice(st * 128, (st + 1) * 128)
            wn, u, qg, kg, att = self.s_ewn[d][pp], self.s_eu[d][pp], self.s_eqg[d][pp], self.s_ekg[d][pp], self.s_eatt[d][pp]
            self.ld_items[d] += [(wn.t[:], self.gdn_wn[d, :, ts].rearrange("(h p) t -> p h t", p=128)),
                                 (u.t[:], self.gdn_u[d, ts, :]),
                                 (qg.t[:], self.gdn_qg[d, :, ts].rearrange("(h p) t -> p h t", p=128)),
                                 (kg.t[:], self.gdn_kg[d, ts, :]), (att.t[:], self.gdn_att[d, ts, :])]
            self.ld_bufs[d] += [wn.b, u.b, qg.b, kg.b, att.b]

    def sc_gdn_step(self, sts, pp, ci):
        k = self.k
        for d in range(2):
            st = sts[d]
            ch = ci if d == 0 else 1 - ci
            cs = slice(ch * 64, (ch + 1) * 64)
            chabs = st * 2 + ch
            wn, u, qg, kg, att = self.s_ewn[d][pp], self.s_eu[d][pp], self.s_eqg[d][pp], self.s_ekg[d][pp], self.s_eatt[d][pp]
            S, Sb, vn = self.s_eS[d], self.s_eSb[d], self.s_evn[d]
            op = self.pbank()
            pv = self.pbank()
            for h in range(4):
                self.mm(pv, pv.t[cs, h * 128:(h + 1) * 128], wn.t[:, h, cs], Sb.t[:, h, :], [wn.b, Sb.b])
            self.tt("vector", vn.t[cs, :], u.t[cs, :], pv.t[cs, :], ALU.add, [u.b, pv.b], [vn.b])
            for h in range(4):
                oap = op.t[:, h * 64:(h + 1) * 64]
                self.mm(op, oap, Sb.t[:, h, :], qg.t[:, h, cs], [Sb.b, qg.b], start=True, stop=False)
                self.mm(op, oap, vn.t[cs, h * 128:(h + 1) * 128], att.t[cs, h * 128 + ch * 64:h * 128 + (ch + 1) * 64],
                        [vn.b, att.b], start=False, stop=True, acc=True)
            self.cp("scalar", self.s_eo[d][pp].t[:, :, cs], op.t[:, 0:256].rearrange("p (h i) -> p h i", h=4), [op.b],
                    [self.s_eo[d][pp].b])
            pP = self.pbank()
            for h in range(4):
                self.mm(pP, pP.t[:, h * 128:(h + 1) * 128], kg.t[cs, h * 128:(h + 1) * 128], vn.t[cs, h * 128:(h + 1) * 128],
                        [kg.b, vn.b])
            for h in range(4):
                self.stt(S.t[:, h, :], S.t[:, h, :], self.egl.t[:, chabs, d * 4 + h:d * 4 + h + 1],
                         pP.t[:, h * 128:(h + 1) * 128], ALU.mult, ALU.add, [S.b, pP.b, self.egl.b], [S.b])
            self.cp("scalar", Sb.t[:], S.t[:], [S.b], [Sb.b])

    def sc_gdn_out(self, sts, pp):
        k = self.k
        for d in range(2):
            st = sts[d]
            ts = slice(st * 128, (st + 1) * 128)
            o = self.s_eo[d][pp]
            self.st_items[d].append((self.gdn_o[d, :, ts].rearrange("(h p) t -> p h t", p=128), o.t[:]))
            self.st_bufs[d].append(o.b)
_CACHE = {}


def run(inputs, depth, mix=("gdn", "gla", "ssd")):
    xp = np.asarray(inputs["x_prompt"], np.float32)
    xs = np.asarray(inputs["x_sample"], np.float32)
    L = xp.shape[1]
    assert xs.shape[1] == L
    seqs = [xp[i] for i in range(xp.shape[0])] + [xs[i] for i in range(xs.shape[0])]
    nseq = len(seqs)
    key = (L, depth, tuple(mix))
    if key not in _CACHE:
        _CACHE[key] = Builder(L, depth, mix).build()
    nc = _CACHE[key]
    inp = {n: np.asarray(v, np.float32) for n, v in inputs.items()}
    wf32 = np.stack([pack_layer_weights(inp, l) for l in range(depth)])
    pvec = np.stack([pack_pvec(inp, l) for l in range(depth)])
    prow = np.stack([pack_prow(inp, l) for l in range(depth)])
    wgup = np.stack([pack_wgup(inp, l) for l in range(depth)])
    fnorm = np.ascontiguousarray(inp["final_norm"].reshape(KC, 128).T)
    in_maps = []
    for c in range(8):
        s = seqs[c % nseq]
        in_maps.append({"xin": np.ascontiguousarray(s.T), "wf32": wf32, "pvec": pvec, "prow": prow, "wgup": wgup,
                        "fnorm": fnorm})
    res = run_bass_kernel_spmd(nc, in_maps, core_ids=list(range(8)))
    outs = [np.ascontiguousarray(res.results[c]["yout"].T) for c in range(nseq)]
    yp = np.stack(outs[:xp.shape[0]]).astype(np.float32)
    ys = np.stack(outs[xp.shape[0]:]).astype(np.float32)
    return yp, ys


def kernel(**inputs):
    return run(inputs, 4)
```

```python
import numpy as np
from contextlib import ExitStack
import concourse.bass as bass
import concourse.mybir as mybir
from concourse.bass_utils import run_bass_kernel_spmd

F32 = mybir.dt.float32
BF16 = mybir.dt.bfloat16
AF = mybir.ActivationFunctionType
ALU = mybir.AluOpType
ENGS = ("tensor", "vector", "scalar", "gpsimd", "sync")

D = 1024
DFF = 2816
NFF = DFF // 128
KC = D // 128
EPS = 1e-6
TT = 512
SLOT = 2816


class DSem:
    __slots__ = ("dsem", "dcount")

    def __init__(self, sem):
        self.dsem = sem
        self.dcount = 0


class Buf:
    __slots__ = ("name", "w", "r", "ds", "excl")

    def __init__(self, name):
        self.name = name
        self.w = []
        self.r = []
        self.ds = {}
        self.excl = False


class MK:
    def __init__(self, nc, es):
        self.nc = nc
        self.es = es
        self.prog = {e: [] for e in ENGS}
        self.esem = {}
        self.ecount = {e: 0 for e in ENGS}
        self.waited = {e: {} for e in ENGS}
        for e in ENGS:
            self.esem[e] = es.enter_context(nc.semaphore("s_" + e))
        self.nbuf = 0
        self.ninst = 0
        self.reg = {}

    def buf(self, name=None):
        if name is not None and name in self.reg:
            return self.reg[name]
        self.nbuf += 1
        b = Buf(name or ("b_%d" % self.nbuf))
        self.reg[b.name] = b
        return b

    def barrier(self):
        for eng in ENGS:
            need = {}
            for o in ENGS:
                if self.ecount[o] > self.waited[eng].get(("e", o), 0):
                    need[("e", o)] = (self.esem[o], self.ecount[o])
            for b in self.reg.values():
                for q in b.ds.values():
                    if q.dcount > self.waited[eng].get(("d", id(q)), 0):
                        need[("d", id(q))] = (q.dsem, q.dcount)
            self._emit_waits(eng, need)

    def _dsem(self, b, kind):
        if kind not in b.ds:
            b.ds[kind] = DSem(self.es.enter_context(self.nc.semaphore("d%s_%s" % (kind, b.name))))
        return b.ds[kind]

    def _need(self, eng, reads, writes, acc=None):
        need = {}

        def add(ev):
            kind, ref, val = ev
            if kind == "e":
                key = ("e", ref)
                sem = self.esem[ref]
                v = val
            else:
                key = ("d", id(ref))
                sem = ref.dsem
                v = ref.dcount
            if self.waited[eng].get(key, 0) >= v:
                return
            cur = need.get(key)
            if cur is None or cur[1] < v:
                need[key] = (sem, v)

        for b in reads:
            for ev in b.w:
                add(ev)
        for b in writes:
            for ev in b.w:
                if acc is not None and b is acc and ev[0] == "e" and ev[1] == "tensor":
                    continue
                add(ev)
            for ev in b.r:
                add(ev)
        return need

    def _emit_waits(self, eng, need):
        for key, (sem, v) in need.items():
            self.waited[eng][key] = v
            self.prog[eng].append(lambda e, sem=sem, v=v: e.wait_ge(sem, v))

    def _record(self, ev, reads, writes):
        for b in writes:
            b.w = [ev]
            b.r = []
        for b in reads:
            if any(b is w for w in writes):
                continue
            b.r = [x for x in b.r if not (x[0] == ev[0] and x[1] is ev[1])] + [ev]

    def op(self, eng, fn, reads=(), writes=(), acc=None):
        xr = [b for b in reads if b.excl]
        if xr:
            reads = [b for b in reads if not b.excl]
            writes = list(writes) + [b for b in xr if not any(b is w for w in writes)]
        need = self._need(eng, reads, writes, acc)
        self._emit_waits(eng, need)
        self.ecount[eng] += 1
        sem = self.esem[eng]
        self.prog[eng].append(lambda e, fn=fn, sem=sem: fn(e).then_inc(sem, 1))
        ev = ("e", eng, self.ecount[eng])
        self._record(ev, reads, writes)
        self.ninst += 1

    def dma(self, eng, out, in_, reads=(), writes=(), prim=None, **kw):
        if prim is None:
            prim = writes[0] if writes else reads[0]
        need = self._need(eng, reads, writes)
        self._emit_waits(eng, need)
        q = self._dsem(prim, "sw" if eng == "gpsimd" else "hw")
        q.dcount += 16
        sem = q.dsem
        self.prog[eng].append(
            lambda e, out=out, in_=in_, sem=sem, kw=kw: e.dma_start(out=out, in_=in_, **kw).then_inc(sem, 16))
        ev = ("d", q, q.dcount)
        self._record(ev, reads, writes)
        self.ninst += 1

    def dma_batch(self, eng, items, reads=(), writes=(), prim=None):
        need = self._need(eng, reads, writes)
        self._emit_waits(eng, need)
        q = self._dsem(prim, "sw" if eng == "gpsimd" else "hw")
        sem = q.dsem
        for (out, in_) in items:
            q.dcount += 16
            self.prog[eng].append(
                lambda e, out=out, in_=in_, sem=sem: e.dma_start(out=out, in_=in_).then_inc(sem, 16))
            self.ninst += 1
        ev = ("d", q, q.dcount)
        self._record(ev, reads, writes)

    def wait_all(self, eng, bufs):
        need = self._need(eng, bufs, ())
        self._emit_waits(eng, need)

    def run_block(self):
        nc = self.nc
        with nc.Block() as block:
            @block.tensor
            def _(e):
                for f in self.prog["tensor"]:
                    f(e)

            @block.vector
            def _(e):
                for f in self.prog["vector"]:
                    f(e)

            @block.scalar
            def _(e):
                for f in self.prog["scalar"]:
                    f(e)

            @block.gpsimd
            def _(e):
                for f in self.prog["gpsimd"]:
                    f(e)

            @block.sync
            def _(e):
                for f in self.prog["sync"]:
                    f(e)


class Tile:
    __slots__ = ("t", "b")

    def __init__(self, t, b):
        self.t = t
        self.b = b


IN_OFF = {}
_o = 0
for _n, _s in (("a_q", 512), ("a_k", 512), ("a_v", 512), ("a_z", 512), ("a_b", 8), ("a_a", 8),
               ("b_q", 256), ("b_k", 256), ("b_v", 512), ("b_r", 512), ("b_g", 32),
               ("c_z", 512), ("c_x", 512), ("c_B", 256), ("c_C", 256), ("c_dt", 16), ("gate", 3072)):
    IN_OFF[_n] = (_o, _s)
    _o += _s
assert _o == 8256


def _unit(W, cols):
    K = W.shape[0]
    kc = K // 128
    cols = np.asarray(cols)
    sub = np.zeros((K, len(cols)), np.float32)
    ok = cols >= 0
    sub[:, ok] = W[:, cols[ok]]
    return np.ascontiguousarray(sub.reshape(kc, 128, len(cols)).transpose(1, 0, 2)).reshape(128, kc * len(cols))


def col_range(name, lo=0, hi=None):
    o, s = IN_OFF[name]
    if hi is None:
        hi = s
    return list(range(o + lo, o + hi))


def make_cfg():
    cfg = {}
    A_chunks = []
    for nm, n in (("a_q", 4), ("a_k", 4), ("a_v", 4), ("c_x", 4), ("c_B", 2), ("c_C", 2), ("b_q", 2), ("b_k", 2)):
        for c in range(n):
            A_chunks.append(col_range(nm, c * 128, (c + 1) * 128))
    A_units = [A_chunks[2 * i] + A_chunks[2 * i + 1] for i in range(len(A_chunks) // 2)]
    A_units.append(col_range("b_v", 0, 256))
    A_units.append(col_range("b_v", 256, 512))
    A_units.append(col_range("b_k"))
    A_units.append(col_range("b_g", 0, 16) + [-1] * 16 + col_range("b_g", 16, 32) + [-1] * 16
                   + col_range("a_b") + col_range("a_a") + col_range("c_dt"))
    cfg["A_units"] = A_units
    B_chunks = []
    for nm in ("a_z", "b_r", "c_z"):
        for c in range(4):
            B_chunks.append(col_range(nm, c * 128, (c + 1) * 128))
    for c in range(24):
        B_chunks.append(col_range("gate", c * 128, (c + 1) * 128))
    cfg["B_units"] = [B_chunks[2 * i] + B_chunks[2 * i + 1] for i in range(len(B_chunks) // 2)]
    return cfg


CFG = make_cfg()


def unit_plan(cfg):
    plan = []
    for f in range(2):
        for j in range(NFF):
            plan.append((("gu", f, j), KC * 256))
        for oc in range(KC):
            plan.append((("dn", f, oc), NFF * 128))
    for u in range(len(cfg["A_units"])):
        plan.append((("inA", u), KC * len(cfg["A_units"][u])))
    for u in range(len(cfg["B_units"])):
        plan.append((("inB", u), KC * len(cfg["B_units"][u])))
    for oc in range(KC):
        plan.append((("br", oc), 12 * 128))
    for u in range(4):
        plan.append((("wo", u), KC * 256))
    return plan


PLAN = unit_plan(CFG)
UOFF = {}
_o = 0
for _k, _s in PLAN:
    UOFF[_k] = (_o, _s)
    _o += _s
LAYER_W = _o


def pack_layer_weights(inp, l):
    out = np.empty((128, LAYER_W), np.float32)
    for key, size in PLAN:
        o, s = UOFF[key]
        if key[0] == "gu":
            _, f, j = key
            cols = list(range(j * 128, (j + 1) * 128))
            g = _unit(inp["ffn_w_gate"][l, f], cols).reshape(128, KC, 128)
            u = _unit(inp["ffn_w_up"][l, f], cols).reshape(128, KC, 128)
            blk = np.concatenate([g, u], axis=2).reshape(128, KC * 256)
        elif key[0] == "dn":
            _, f, oc = key
            blk = _unit(inp["ffn_w_down"][l, f], list(range(oc * 128, (oc + 1) * 128)))
        elif key[0] == "inA":
            blk = _unit(inp["w_in"][l], CFG["A_units"][key[1]])
        elif key[0] == "inB":
            blk = _unit(inp["w_in"][l], CFG["B_units"][key[1]])
        elif key[0] == "br":
            oc = key[1]
            parts = [_unit(inp["w_branch"][l, b], list(range(oc * 128, (oc + 1) * 128))).reshape(128, 4, 128)
                     for b in range(3)]
            blk = np.concatenate(parts, axis=1).reshape(128, 12 * 128)
        elif key[0] == "wo":
            u = key[1]
            blk = _unit(inp["w_out"][l], list(range(u * 256, (u + 1) * 256)))
        assert blk.shape == (128, s), (key, blk.shape, s)
        out[:, o:o + s] = blk
    return out


PV = {}
_o = 0
for _n, _s in (("ffn_norm0", 8), ("ffn_norm1", 8), ("mix_norm", 8), ("gdn_norm", 1), ("gla_norm", 1),
               ("ssd_norm", 4), ("ssd_d", 4), ("gdn_cw", 60), ("ssd_cw", 40), ("ssd_cb", 8)):
    PV[_n] = (_o, _s)
    _o += _s
NPV = _o

PR = {}
_o = 0
for _n, _s in (("gla_bg", 512), ("sm_bias", 32), ("sm_alog", 32)):
    PR[_n] = (_o, _s)
    _o += _s
NPR = _o


def pack_pvec(inp, l):
    out = np.zeros((128, NPV), np.float32)

    def put(name, vec):
        o, s = PV[name]
        out[:, o:o + s] = np.asarray(vec, np.float32).reshape(s, 128).T
    put("ffn_norm0", inp["ffn_norm"][l, 0])
    put("ffn_norm1", inp["ffn_norm"][l, 1])
    put("mix_norm", inp["mix_norm"][l])
    put("gdn_norm", inp["gdn_norm"][l])
    put("gla_norm", inp["gla_norm"][l])
    put("ssd_norm", inp["ssd_norm"][l])
    put("ssd_d", np.repeat(inp["ssd_d"][l], 64))
    o, s = PV["gdn_cw"]
    cw = inp["gdn_conv_w"][l]
    out[:, o:o + s] = cw.reshape(5, 12, 128).transpose(2, 1, 0).reshape(128, 60)
    o, s = PV["ssd_cw"]
    cw = inp["ssd_conv_w"][l]
    out[:, o:o + s] = cw.reshape(5, 8, 128).transpose(2, 1, 0).reshape(128, 40)
    put("ssd_cb", inp["ssd_conv_b"][l])
    return out


def pack_prow(inp, l):
    out = np.zeros((NPR,), np.float32)
    o, s = PR["gla_bg"]
    out[o:o + s] = inp["gla_b_g"][l].reshape(512)
    o, s = PR["sm_bias"]
    out[o + 8:o + 16] = inp["gdn_dt_bias"][l].reshape(8)
    out[o + 16:o + 32] = inp["ssd_dt_bias"][l].reshape(16)
    o, s = PR["sm_alog"]
    out[o + 8:o + 16] = inp["gdn_a_log"][l].reshape(8)
    out[o + 16:o + 32] = inp["ssd_a_log"][l].reshape(16)
    return out


def pack_wgup(inp, l):
    out = np.zeros((64, 256), np.float32)
    out[0:16] = inp["gla_w_gup"][l, 0]
    out[32:48] = inp["gla_w_gup"][l, 1]
    return out
T2 = 256
TE2 = T2 + 4
NSUB2 = T2 // 128


class Builder:
    def __init__(self, L, depth, mix=("gdn", "gla", "ssd")):
        self.L = L
        self.depth = depth
        self.mix = tuple(mix)
        self.NT = L // TT
        self.NT2 = L // T2
        self.NST = L // 128
        self.NCH = L // 64

    def sb(self, name, shape, dt):
        es = self.scope if self.scope is not None else self.es
        self.tcount = getattr(self, "tcount", 0) + 1
        t = es.enter_context(self.nc.sbuf_tensor("%s_%d" % (name, self.tcount), list(shape), dt))
        return Tile(t, self.k.buf(name))

    def begin_scope(self):
        self.scope = ExitStack()
        self.scope.__enter__()

    def end_scope(self):
        self.k.barrier()
        self.scope.__exit__(None, None, None)
        self.scope = None

    def dram(self, name, shape, dt):
        return self.nc.dram_tensor(name, list(shape), dt).ap()

    def build(self):
        nc = bass.Bass("TRN2", target_bir_lowering=False)
        self.nc = nc
        L, depth = self.L, self.depth
        self.xin = nc.dram_tensor("xin", [D, L], F32, kind="ExternalInput").ap()
        self.wf32 = nc.dram_tensor("wf32", [depth, 128, LAYER_W], F32, kind="ExternalInput").ap()
        self.pvec = nc.dram_tensor("pvec", [depth, 128, NPV], F32, kind="ExternalInput").ap()
        self.prow = nc.dram_tensor("prow", [depth, NPR], F32, kind="ExternalInput").ap()
        self.wgup = nc.dram_tensor("wgup", [depth, 64, 256], F32, kind="ExternalInput").ap()
        self.fnorm = nc.dram_tensor("fnorm", [128, KC], F32, kind="ExternalInput").ap()
        self.yout = nc.dram_tensor("yout", [D, L], F32, kind="ExternalOutput").ap()
        self.wbf = self.dram("wbf", [depth, 128, LAYER_W], BF16)
        self.xres = self.dram("xres", [D, L], F32)
        NST = self.NST
        if "gla" in self.mix:
            self.gla_qg = self.dram("gla_qg", [2, 256, L], BF16)
            self.gla_kmg = self.dram("gla_kmg", [2, 256, L], BF16)
            self.gla_kd = self.dram("gla_kd", [2, L, 256], BF16)
            self.gla_v = self.dram("gla_v", [L, 512], BF16)
            self.gla_o = self.dram("gla_o", [2, 512, L], F32)
        if "ssd" in self.mix:
            self.ssd_cexp = self.dram("ssd_cexp", [2, 8 * 128, L], BF16)
            self.ssd_B = self.dram("ssd_B", [L, 256], BF16)
            self.ssd_xw = self.dram("ssd_xw", [2, L, 512], BF16)
            self.ssd_ydiag = self.dram("ssd_ydiag", [512, L], F32)
            self.ssd_yoff = self.dram("ssd_yoff", [2, 512, L], F32)
        if "gdn" in self.mix:
            self.gdn_wn = self.dram("gdn_wn", [2, 512, L], BF16)
            self.gdn_u = self.dram("gdn_u", [2, L, 512], F32)
            self.gdn_qg = self.dram("gdn_qg", [2, 512, L], BF16)
            self.gdn_kg = self.dram("gdn_kg", [2, L, 512], BF16)
            self.gdn_att = self.dram("gdn_att", [2, NST * 128, 512], BF16)
            self.gdn_o = self.dram("gdn_o", [2, 512, L], F32)
        with ExitStack() as es:
            k = MK(nc, es)
            self.k = k
            self.es = es
            self.scope = None
            self.setup_common()
            self.cast_weights()
            for l in range(depth + 1):
                self.pass1(l)
                if l < depth and self.mix:
                    self.pass2(l)
                    self.scan(l)
            k.wait_all("gpsimd", self.out_bufs)
            k.run_block()
        return nc

    def setup_common(self):
        k = self.k
        self.out_bufs = []
        self.ones_bf = self.sb("ones_bf", [128, 128], BF16)
        k.op("gpsimd", lambda e: e.memset(self.ones_bf.t[:], 1.0), writes=[self.ones_bf.b])
        self.epsb = self.sb("epsb", [128, 1], F32)
        k.op("gpsimd", lambda e: e.memset(self.epsb.t[:], EPS), writes=[self.epsb.b])
        self.oneb = self.sb("oneb", [128, 1], F32)
        k.op("gpsimd", lambda e: e.memset(self.oneb.t[:], 1.0), writes=[self.oneb.b])
        self.U = [self.sb("Ublk%d" % d, [128, 128], F32) for d in range(2)]
        self.SU = [self.sb("SUblk%d" % d, [128, 128], F32) for d in range(2)]
        self.ident = self.sb("ident", [128, 128], F32)
        self.ident_bf = self.sb("ident_bf", [128, 128], BF16)
        self.bones = self.sb("bones", [128, 128], F32)

        def tri(tile, ge, strict):
            k.op("gpsimd", lambda e: e.memset(tile.t[:], 0.0), writes=[tile.b])
            for b in range(2):
                blk = tile.t[b * 64:(b + 1) * 64, b * 64:(b + 1) * 64]
                k.op("gpsimd", lambda e, blk=blk: e.memset(blk, 1.0), writes=[tile.b])
                sgn = 1 if ge else -1
                base = -1 if strict else 0
                k.op("gpsimd", lambda e, blk=blk, sgn=sgn, base=base: e.affine_select(
                    out=blk, in_=blk, pattern=[[sgn, 64]], compare_op=ALU.is_ge, fill=0.0, base=base,
                    channel_multiplier=-sgn), reads=[tile.b], writes=[tile.b])
        tri(self.U[0], True, False)
        tri(self.U[1], False, False)
        tri(self.SU[0], False, True)
        tri(self.SU[1], True, True)
        k.op("gpsimd", lambda e: e.memset(self.bones.t[:], 0.0), writes=[self.bones.b])
        for b in range(2):
            blk = self.bones.t[b * 64:(b + 1) * 64, b * 64:(b + 1) * 64]
            k.op("gpsimd", lambda e, blk=blk: e.memset(blk, 1.0), writes=[self.bones.b])
        k.op("gpsimd", lambda e: e.memset(self.ident.t[:], 1.0), writes=[self.ident.b])
        k.op("gpsimd", lambda e: e.affine_select(out=self.ident.t[:], in_=self.ident.t[:], pattern=[[-1, 128]],
                                                 compare_op=ALU.is_equal, fill=0.0, base=0, channel_multiplier=1),
             reads=[self.ident.b], writes=[self.ident.b])
        k.op("gpsimd", lambda e: e.tensor_copy(out=self.ident_bf.t[:], in_=self.ident.t[:]),
             reads=[self.ident.b], writes=[self.ident_bf.b])
        self.cind = [self.sb("cind%d" % c, [128, 128], F32) for c in range(2)]
        for c in range(2):
            k.op("gpsimd", lambda e, c=c: e.memset(self.cind[c].t[:], 0.0), writes=[self.cind[c].b])
            k.op("gpsimd", lambda e, c=c: e.memset(self.cind[c].t[c * 64:(c + 1) * 64, :], 1.0), writes=[self.cind[c].b])
        self.maskneg = [self.sb("maskneg%d" % d, [128, 128], F32) for d in range(2)]
        for d in range(2):
            k.op("vector", lambda e, d=d: e.tensor_scalar(out=self.maskneg[d].t[:], in0=self.U[d].t[:], scalar1=30000.0,
                                                          scalar2=-30000.0, op0=ALU.mult, op1=ALU.add),
                 reads=[self.U[d].b], writes=[self.maskneg[d].b])
        self.NSLOT = 8
        self.wslots = [self.sb("wslot%d" % i, [128, SLOT], BF16) for i in range(self.NSLOT)]
        self.wnext = 0
        self.NPS = 7
        self.pbanks = []
        for i in range(self.NPS):
            t = self.es.enter_context(self.nc.psum_tensor("pb%d" % i, [128, TT], F32))
            self.pbanks.append(Tile(t, k.buf("pb%d" % i)))
            self.pbanks[-1].b.excl = True
        self.pnext = 0
        t = self.es.enter_context(self.nc.psum_tensor("ptr", [128, 1024], BF16))
        self.ptr = Tile(t, k.buf("ptr"))
        self.ptr.b.excl = True
        self.pv = self.sb("pv", [128, self.depth, NPV], F32)
        self.fn = self.sb("fn", [128, KC], F32)
        self.fn.b = self.pv.b
        k.dma_batch("sync", [(self.pv.t[:, l, :], self.pvec[l]) for l in range(self.depth)] + [(self.fn.t[:], self.fnorm)],
                    writes=[self.pv.b], prim=self.pv.b)
        NCH = self.NCH
        if "gla" in self.mix:
            self.egl_gla = self.sb("egl_gla", [128, 2, 2, NCH], F32)
        if "ssd" in self.mix or "gdn" in self.mix:
            self.egl = self.sb("egl", [128, NCH, 24], F32)
        self.pe_bufs = [k.buf("pe%d" % i) for i in range(32)]
        self.pe_n = 0

    def pvc(self, l, name, i=0, n=1):
        o, s = PV[name]
        return self.pv.t[:, l, o + i:o + i + n]

    def cast_weights(self):
        k = self.k
        CH = 65536
        self.cast_chunks = []
        for l in range(self.depth):
            lo = 0
            lb = k.buf("wcL%d" % l)
            while lo < LAYER_W:
                hi = min(LAYER_W, lo + CH)
                b = k.buf("wc0_%d" % lo) if l == 0 else lb
                k.dma("gpsimd", self.wbf[l, :, lo:hi], self.wf32[l, :, lo:hi], writes=[b], max_dma_last_dim=4096)
                self.cast_chunks.append((l, lo, hi, b))
                lo = hi

    def cast_deps(self, l, o, s):
        out = []
        for (ll, lo, hi, b) in self.cast_chunks:
            if ll == l and lo < o + s and hi > o and b not in out:
                out.append(b)
        return out

    def load_unit(self, l, key):
        k = self.k
        o, s = UOFF[key]
        slot = self.wslots[self.wnext]
        self.wnext = (self.wnext + 1) % self.NSLOT
        k.dma("sync", slot.t[:, 0:s], self.wbf[l, :, o:o + s], reads=self.cast_deps(l, o, s), writes=[slot.b],
              prim=slot.b)
        return slot

    def pbank(self):
        p = self.pbanks[self.pnext]
        self.pnext = (self.pnext + 1) % self.NPS
        return p

    def mm(self, ps, out_ap, lh, rh, reads, start=True, stop=True, acc=False):
        self.k.op("tensor", lambda e: e.matmul(out_ap, lhsT=lh, rhs=rh, start=start, stop=stop),
                  reads=reads, writes=[ps.b], acc=(ps.b if acc else None))

    def mm_group(self, ps, out_ap, pairs, reads):
        n = len(pairs)
        for i, (lh, rh) in enumerate(pairs):
            self.mm(ps, out_ap, lh, rh, reads, start=(i == 0), stop=(i == n - 1), acc=(i > 0))

    def ts(self, eng, out, in0, s1, s2, op0, op1, reads, writes):
        if op1 is None:
            self.k.op(eng, lambda e: e.tensor_scalar(out=out, in0=in0, scalar1=s1, scalar2=None, op0=op0),
                      reads=reads, writes=writes)
        else:
            self.k.op(eng, lambda e: e.tensor_scalar(out=out, in0=in0, scalar1=s1, scalar2=s2, op0=op0, op1=op1),
                      reads=reads, writes=writes)

    def ms(self, ap, val, writes):
        self.k.op("gpsimd", lambda e: e.memset(ap, val), reads=(), writes=writes)

    def tr(self, out, in_, reads, ps):
        idt = self.ident_bf
        self.k.op("tensor", lambda e: e.transpose(out, in_, idt.t[:]), reads=list(reads) + [idt.b], writes=[ps.b])

    def act(self, out, in_, func, reads, writes, **kw):
        self.k.op("scalar", lambda e: e.activation(out=out, in_=in_, func=func, **kw), reads=reads, writes=writes)

    def tt(self, eng, out, in0, in1, op, reads, writes):
        self.k.op(eng, lambda e: e.tensor_tensor(out=out, in0=in0, in1=in1, op=op), reads=reads, writes=writes)

    def stt(self, out, in0, scalar, in1, op0, op1, reads, writes):
        self.k.op("vector", lambda e: e.scalar_tensor_tensor(out=out, in0=in0, scalar=scalar, in1=in1, op0=op0, op1=op1),
                  reads=reads, writes=writes)

    def cp(self, eng, out, in_, reads, writes):
        if eng == "scalar":
            self.k.op(eng, lambda e: e.copy(out=out, in_=in_), reads=reads, writes=writes)
        else:
            self.k.op(eng, lambda e: e.tensor_copy(out=out, in_=in_), reads=reads, writes=writes)

    def rstd_from_ss(self, ps_ap, psb, out, n, cols=None):
        self.act(out.t[:] if cols is None else cols, ps_ap, AF.Ln, [psb, self.epsb.b], [out.b], scale=1.0 / n,
                 bias=self.epsb.t[:, 0:1])
        o = out.t[:] if cols is None else cols
        self.act(o, o, AF.Exp, [out.b], [out.b], scale=-0.5)

    def rmsnorm(self, x, gcol, out_h, W, sq, rstd):
        k = self.k
        self.act(sq.t[:, :, 0:W], x.t[:, :, 0:W], AF.Square, [x.b], [sq.b])
        for (lo, hi) in ((0, min(W, 512)), (512, W)):
            if hi <= lo:
                continue
            ps = self.pbank()
            self.mm_group(ps, ps.t[:, 0:hi - lo], [(self.ones_bf.t[:], sq.t[:, kc, lo:hi]) for kc in range(KC)],
                          reads=[self.ones_bf.b, sq.b])
            self.rstd_from_ss(ps.t[:, 0:hi - lo], ps.b, rstd, D, cols=rstd.t[:, lo:hi])
        for kc in range(KC):
            self.stt(out_h.t[:, kc, 0:W], x.t[:, kc, 0:W], gcol[:, kc:kc + 1], rstd.t[:, 0:W], ALU.mult, ALU.mult,
                     [x.b, rstd.b, self.pv.b, self.fn.b], [out_h.b])

    def ffn(self, l, f, x):
        k = self.k
        o, s = PV["ffn_norm%d" % f]
        self.rmsnorm(x, self.pv.t[:, l, o:o + s], self.hT, TT, self.sq, self.rstd)
        hT, hid = self.hT, self.hid
        for j in range(NFF):
            w = self.load_unit(l, ("gu", f, j))
            wv = w.t[:, 0:KC * 256].rearrange("p (k c) -> p k c", k=KC)
            pg, pu = self.pbank(), self.pbank()
            self.mm_group(pg, pg.t[:], [(wv[:, kc, 0:128], hT.t[:, kc, :]) for kc in range(KC)], reads=[w.b, hT.b])
            self.mm_group(pu, pu.t[:], [(wv[:, kc, 128:256], hT.t[:, kc, :]) for kc in range(KC)], reads=[w.b, hT.b])
            sg = self.sg[j % 2]
            self.act(sg.t[:], pg.t[:], AF.Silu, [pg.b], [sg.b])
            self.tt("vector", hid.t[:, j, :], sg.t[:], pu.t[:], ALU.mult, [sg.b, pu.b], [hid.b])
        for oc in range(KC):
            w = self.load_unit(l, ("dn", f, oc))
            wv = w.t[:, 0:NFF * 128].rearrange("p (k c) -> p k c", k=NFF)
            po = self.pbank()
            self.mm_group(po, po.t[:], [(wv[:, j, :], hid.t[:, j, :]) for j in range(NFF)], reads=[w.b, hid.b])
            self.stt(x.t[:, oc, :], po.t[:], 0.5, x.t[:, oc, :], ALU.mult, ALU.add, [po.b, x.b], [x.b])

    def pass1(self, l):
        k = self.k
        first = (l == 0)
        last = (l == self.depth)
        self.begin_scope()
        self.xT = [self.sb("xT%d" % i, [128, KC, TT], F32) for i in range(2)]
        self.hT = self.sb("hT", [128, KC, TT], BF16)
        self.sq = self.sb("sq", [128, KC, TT], BF16)
        self.rstd = self.sb("rstd", [128, TT], F32)
        self.hid = self.sb("hid", [128, 24, TT], BF16)
        self.sg = [self.sb("sg%d" % i, [128, TT], F32) for i in range(2)]
        if not first and self.mix:
            self.alloc_B()
        src = self.xin if first else self.xres
        for t in range(self.NT):
            x = self.xT[t % 2]
            t0 = t * TT
            k.dma("sync", x.t[:], src[:, t0:t0 + TT].rearrange("(k p) t -> p k t", p=128), writes=[x.b])
            if not first:
                if self.mix:
                    self.phase_B(l - 1, t, x)
                self.ffn(l - 1, 1, x)
            if not last:
                self.ffn(l, 0, x)
                k.dma("gpsimd", self.xres[:, t0:t0 + TT].rearrange("(k p) t -> p k t", p=128), x.t[:],
                      reads=[x.b], prim=x.b)
            else:
                o, s = 0, KC
                self.final_norm(x)
                k.dma("gpsimd", self.yout[:, t0:t0 + TT].rearrange("(k p) t -> p k t", p=128), x.t[:],
                      reads=[x.b], prim=x.b)
                if x.b not in self.out_bufs:
                    self.out_bufs.append(x.b)
        self.end_scope()

    def final_norm(self, x):
        sq, rstd = self.sq, self.rstd
        self.act(sq.t[:], x.t[:], AF.Square, [x.b], [sq.b])
        ps = self.pbank()
        self.mm_group(ps, ps.t[:], [(self.ones_bf.t[:], sq.t[:, kc, :]) for kc in range(KC)],
                      reads=[self.ones_bf.b, sq.b])
        self.rstd_from_ss(ps.t[:], ps.b, rstd, D)
        for kc in range(KC):
            self.stt(x.t[:, kc, :], x.t[:, kc, :], self.fn.t[:, kc:kc + 1], rstd.t[:], ALU.mult, ALU.mult,
                     [x.b, rstd.b, self.fn.b], [x.b])

    def alloc_B(self):
        onv = self.hid.t[:].rearrange("p a b -> p (a b)").bitcast(F32).rearrange("p (a b) -> p a b", a=12)
        self.on = Tile(None, self.hid.b)
        self.on_v = onv
        self.ybr = self.sb("ybr", [128, 12, TT], BF16)
        self.ld = [self.sb("ldB%d" % i, [128, TT], F32) for i in range(4)]
        self.ldn = 0
        self.sqb = self.sb("sqb", [128, 4, TT], BF16)
        self.rsb = self.sb("rsb", [128, TT], F32)
        self.rsb4 = self.sb("rsb4", [128, 4, TT], F32)
        self.mg = [self.sb("mg%d" % i, [128, TT], F32) for i in range(2)]
        self.mtmp = [self.sb("mtmp%d" % i, [128, TT], F32) for i in range(2)]
        self.th = [self.sb("th%d" % i, [128, TT], F32) for i in range(3)]
        self.thn = 0
        self.merged = self.sq

    def ldB(self, src_ap):
        t = self.ld[self.ldn % 4]
        self.ldn += 1
        self.k.dma("sync", t.t[:], src_ap, writes=[t.b])
        return t

    def phase_B(self, l, t, x):
        k = self.k
        t0 = t * TT
        ts = slice(t0, t0 + TT)
        on, ybr, hT = self.on, self.ybr, self.hT
        onv = self.on_v
        self.rmsnorm(x, self.pvc(l, "mix_norm", 0, 8), hT, TT, self.sq, self.rstd)
        if "ssd" in self.mix:
            for u in range(2):
                w = self.load_unit(l, ("inB", 4 + u))
                wv = w.t[:, 0:KC * 256].rearrange("p (k c) -> p k c", k=KC)
                for cc in range(2):
                    c = 2 * u + cc
                    a = self.ldB(self.ssd_ydiag[c * 128:(c + 1) * 128, ts])
                    b = self.ldB(self.ssd_yoff[0, c * 128:(c + 1) * 128, ts])
                    d = self.ldB(self.ssd_yoff[1, c * 128:(c + 1) * 128, ts])
                    self.tt("gpsimd", a.t[:], a.t[:], b.t[:], ALU.add, [a.b, b.b], [a.b])
                    self.tt("gpsimd", a.t[:], a.t[:], d.t[:], ALU.add, [a.b, d.b], [a.b])
                    pz = self.pbank()
                    self.mm_group(pz, pz.t[:], [(wv[:, kc, cc * 128:(cc + 1) * 128], hT.t[:, kc, :]) for kc in range(KC)],
                                  reads=[w.b, hT.b])
                    sz = self.th[self.thn % 3]
                    self.thn += 1
                    self.act(sz.t[:], pz.t[:], AF.Silu, [pz.b], [sz.b])
                    self.tt("vector", onv[:, 8 + c, :], a.t[:], sz.t[:], ALU.mult, [a.b, sz.b], [on.b])
        for m, base, src in (("gdn", 0, getattr(self, "gdn_o", None)), ("gla", 4, getattr(self, "gla_o", None))):
            if m not in self.mix:
                continue
            for h in range(4):
                a = self.ldB(src[0, h * 128:(h + 1) * 128, ts])
                b = self.ldB(src[1, h * 128:(h + 1) * 128, ts])
                self.tt("vector" if h % 2 else "gpsimd", onv[:, base + h, :], a.t[:], b.t[:], ALU.add, [a.b, b.b], [on.b])
            self.act(self.sqb.t[:], onv[:, base:base + 4, :], AF.Square, [on.b], [self.sqb.b])
            pss = []
            for h in range(4):
                ps = self.pbank()
                self.mm(ps, ps.t[:], self.ones_bf.t[:], self.sqb.t[:, h, :], [self.ones_bf.b, self.sqb.b])
                pss.append(ps)
            for h in range(4):
                self.act(self.rsb4.t[:, h, :], pss[h].t[:], AF.Ln, [pss[h].b, self.epsb.b], [self.rsb4.b], scale=1.0 / 128,
                         bias=self.epsb.t[:, 0:1])
            self.act(self.rsb4.t[:], self.rsb4.t[:], AF.Exp, [self.rsb4.b], [self.rsb4.b], scale=-0.5)
            for h in range(4):
                self.stt(onv[:, base + h, :], onv[:, base + h, :], self.pvc(l, m + "_norm"), self.rsb4.t[:, h, :], ALU.mult,
                         ALU.mult, [on.b, self.rsb4.b, self.pv.b], [on.b])
        if "ssd" in self.mix:
            for g in range(2):
                self.act(self.sqb.t[:, 0:2, :], onv[:, 8 + 2 * g:10 + 2 * g, :], AF.Square, [on.b], [self.sqb.b])
                ps = self.pbank()
                self.mm_group(ps, ps.t[:], [(self.ones_bf.t[:], self.sqb.t[:, cc, :]) for cc in range(2)],
                              reads=[self.ones_bf.b, self.sqb.b])
                self.rstd_from_ss(ps.t[:], ps.b, self.rsb, 256)
                for cc in range(2):
                    c = 2 * g + cc
                    self.stt(ybr.t[:, 8 + c, :], onv[:, 8 + c, :], self.pvc(l, "ssd_norm", c), self.rsb.t[:],
                             ALU.mult, ALU.mult, [on.b, self.rsb.b, self.pv.b], [ybr.b])
        for m, base, u0 in (("gdn", 0, 0), ("gla", 4, 2)):
            if m not in self.mix:
                continue
            for u in range(2):
                w = self.load_unit(l, ("inB", u0 + u))
                wv = w.t[:, 0:KC * 256].rearrange("p (k c) -> p k c", k=KC)
                for cc in range(2):
                    h = 2 * u + cc
                    pz = self.pbank()
                    self.mm_group(pz, pz.t[:], [(wv[:, kc, cc * 128:(cc + 1) * 128], hT.t[:, kc, :]) for kc in range(KC)],
                                  reads=[w.b, hT.b])
                    sz = self.th[self.thn % 3]
                    self.thn += 1
                    self.act(sz.t[:], pz.t[:], AF.Silu, [pz.b], [sz.b])
                    self.tt("vector", ybr.t[:, base + h, :], onv[:, base + h, :], sz.t[:], ALU.mult, [on.b, sz.b], [ybr.b])
        mixl = [i for i, m in enumerate(("gdn", "gla", "ssd")) if m in self.mix]
        for op_ in range(4):
            wbs = [self.load_unit(l, ("br", 2 * op_ + cc)) for cc in range(2)]
            for bi, b in enumerate(mixl):
                wg_ = self.load_unit(l, ("inB", 6 + b * 4 + op_))
                wgv = wg_.t[:, 0:KC * 256].rearrange("p (k c) -> p k c", k=KC)
                for cc in range(2):
                    wbv = wbs[cc].t[:, 0:12 * 128].rearrange("p (k c) -> p k c", k=12)
                    pgt = self.pbank()
                    self.mm_group(pgt, pgt.t[:], [(wgv[:, kc, cc * 128:(cc + 1) * 128], hT.t[:, kc, :]) for kc in range(KC)],
                                  reads=[wg_.b, hT.b])
                    th = self.th[self.thn % 3]
                    self.thn += 1
                    self.act(th.t[:], pgt.t[:], AF.Tanh, [pgt.b], [th.b], scale=0.5)
                    pbr = self.pbank()
                    self.mm_group(pbr, pbr.t[:], [(wbv[:, b * 4 + kk, :], ybr.t[:, b * 4 + kk, :]) for kk in range(4)],
                                  reads=[wbs[cc].b, ybr.b])
                    if bi == 0:
                        self.stt(self.mg[cc].t[:], th.t[:], 1.0, pbr.t[:], ALU.add, ALU.mult, [th.b, pbr.b], [self.mg[cc].b])
                    else:
                        tmp = self.mtmp[cc]
                        self.stt(tmp.t[:], th.t[:], 1.0, pbr.t[:], ALU.add, ALU.mult, [th.b, pbr.b], [tmp.b])
                        self.tt("gpsimd", self.mg[cc].t[:], self.mg[cc].t[:], tmp.t[:], ALU.add,
                                [self.mg[cc].b, tmp.b], [self.mg[cc].b])
            for cc in range(2):
                oc = 2 * op_ + cc
                self.act(self.merged.t[:, oc, :], self.mg[cc].t[:], AF.Copy, [self.mg[cc].b], [self.merged.b], scale=0.5)
        for u in range(4):
            w = self.load_unit(l, ("wo", u))
            wv = w.t[:, 0:KC * 256].rearrange("p (k c) -> p k c", k=KC)
            for cc in range(2):
                oc = 2 * u + cc
                po = self.pbank()
                self.mm_group(po, po.t[:], [(wv[:, kc, cc * 128:(cc + 1) * 128], self.merged.t[:, kc, :]) for kc in range(KC)],
                              reads=[w.b, self.merged.b])
                self.tt("vector", x.t[:, oc, :], x.t[:, oc, :], po.t[:], ALU.add, [x.b, po.b], [x.b])

    def pass2(self, l):
        k = self.k
        L = self.L
        self.begin_scope()
        self.xE = self.sb("xE", [128, KC, TE2], F32)
        self.hE = self.sb("hE", [128, KC, TE2], BF16)
        self.sqE = self.sb("sqE", [128, KC, TE2], BF16)
        self.rstdE = self.sb("rstdE", [128, TE2], F32)
        self.prb = self.sb("prb", [128, NPR], F32)
        k.dma("sync", self.prb.t[:], self.prow[l:l + 1, :].to_broadcast([128, NPR]), writes=[self.prb.b])
        self.lrT = self.sb("lrT", [64, T2], BF16)
        self.sm = self.sb("sm", [128, NSUB2, 32], F32)
        if "gla" in self.mix:
            self.alloc_p2_gla(l)
        if "ssd" in self.mix:
            self.alloc_p2_ssd(l)
        if "gdn" in self.mix:
            self.alloc_p2_gdn(l)
        xE = self.xE
        self.hEs = [self.hE, self.sb("hE2", [128, KC, TE2], BF16)]
        self.pnext = 0
        if "ssd" in self.mix or "gdn" in self.mix:
            self.alloc_p2_common(l)

        def front(t):
            hE_ = self.hEs[t % 2]
            t0 = t * T2
            lo, hi = t0 - 2, t0 + T2 + 2
            j0, j1 = 0, TE2
            if lo < 0:
                self.ms(xE.t[:, :, 0:2], 0.0, [xE.b])
                j0, lo = 2, 0
            if hi > L:
                self.ms(xE.t[:, :, TE2 - 2:TE2], 0.0, [xE.b])
                j1, hi = TE2 - 2, L
            k.dma("sync", xE.t[:, :, j0:j1], self.xres[:, lo:hi].rearrange("(k p) t -> p k t", p=128), writes=[xE.b])
            self.rmsnorm(xE, self.pvc(l, "mix_norm", 0, 8), hE_, TE2, self.sqE, self.rstdE)

        front(0)
        for t in range(self.NT2):
            t0 = t * T2
            hE = self.hEs[t % 2]
            self.hE = hE
            w = self.load_unit(l, ("inA", 15))
            wv = w.t[:, 0:KC * 96].rearrange("p (k c) -> p k c", k=KC)
            ps = self.pbank()
            self.mm_group(ps, ps.t[0:64, 0:T2], [(wv[:, kc, 0:64], hE.t[:, kc, 2:2 + T2]) for kc in range(KC)], reads=[w.b, hE.b])
            self.cp("scalar", self.lrT.t[:], ps.t[0:64, 0:T2], [ps.b], [self.lrT.b])
            ps = self.pbank()
            for s in range(NSUB2):
                self.mm_group(ps, ps.t[:, s * 32:(s + 1) * 32],
                              [(hE.t[:, kc, 2 + s * 128:2 + (s + 1) * 128], wv[:, kc, 64:96]) for kc in range(KC)],
                              reads=[w.b, hE.b])
            o, _ = PR["sm_bias"]
            self.tt("vector", self.sm.t[:], ps.t[:, 0:NSUB2 * 32].rearrange("p (s c) -> p s c", s=NSUB2),
                    self.prb.t[:, None, o:o + 32].to_broadcast([128, NSUB2, 32]), ALU.add, [ps.b, self.prb.b], [self.sm.b])
            if t + 1 < self.NT2:
                front(t + 1)
            if "ssd" in self.mix or "gdn" in self.mix:
                self.p2_smalls_post(l, t)
            if "gla" in self.mix:
                self.p2_gla(l, t)
            if "ssd" in self.mix:
                self.p2_ssd(l, t)
            if "gdn" in self.mix:
                self.p2_gdn(l, t)
        self.NPS = 7
        self.end_scope()

    def alloc_p2_gla(self, l):
        k = self.k
        self.g_qT = self.sb("g_qT", [128, 2, T2], BF16)
        self.g_kT = self.sb("g_kT", [128, 2, T2], BF16)
        self.g_ktm = self.sb("g_ktm", [128, NSUB2, 256], F32)
        self.g_v = self.sb("g_v", [128, NSUB2, 512], BF16)
        self.g_qg = [self.sb("g_qg%d" % d, [128, 2, T2], BF16) for d in range(2)]
        self.g_kmg = [self.sb("g_kmg%d" % d, [128, 2, T2], BF16) for d in range(2)]
        self.g_kd = [self.sb("g_kd%d" % d, [128, NSUB2, 256], BF16) for d in range(2)]
        self.g_l = self.sb("g_l", [128, 256], F32)
        self.g_eg = [self.sb("g_eg%d" % i, [128, 128], F32) for i in range(2)]
        self.g_emg = [self.sb("g_emg%d" % i, [128, 128], F32) for i in range(2)]
        self.g_ed = self.sb("g_ed", [128, 256], F32)
        self.wg32 = self.sb("wg32", [64, 256], F32)
        self.wgb = self.sb("wgb", [64, 256], BF16)
        k.dma("sync", self.wg32.t[:], self.wgup[l], writes=[self.wg32.b])
        self.cp("vector", self.wgb.t[:], self.wg32.t[:], [self.wg32.b], [self.wgb.b])

    def p2_gla(self, l, t):
        k = self.k
        hE = self.hE
        t0 = t * T2
        for ui, dst in ((10, self.g_qT), (11, self.g_kT)):
            w = self.load_unit(l, ("inA", ui))
            wv = w.t[:, 0:KC * 256].rearrange("p (k c) -> p k c", k=KC)
            for c in range(2):
                ps = self.pbank()
                self.mm_group(ps, ps.t[:, 0:T2], [(wv[:, kc, c * 128:(c + 1) * 128], hE.t[:, kc, 2:2 + T2]) for kc in range(KC)],
                              reads=[w.b, hE.b])
                self.cp("scalar", dst.t[:, c, :], ps.t[:, 0:T2], [ps.b], [dst.b])
        w0 = self.load_unit(l, ("inA", 12))
        w1 = self.load_unit(l, ("inA", 13))
        w2 = self.load_unit(l, ("inA", 14))
        wv0 = w0.t[:, 0:KC * 256].rearrange("p (k c) -> p k c", k=KC)
        wv1 = w1.t[:, 0:KC * 256].rearrange("p (k c) -> p k c", k=KC)
        wv2 = w2.t[:, 0:KC * 256].rearrange("p (k c) -> p k c", k=KC)
        for s in range(NSUB2):
            hs = [hE.t[:, kc, 2 + s * 128:2 + (s + 1) * 128] for kc in range(KC)]
            ps = self.pbank()
            self.mm_group(ps, ps.t[:, 0:256], [(hs[kc], wv0[:, kc, :]) for kc in range(KC)], reads=[w0.b, hE.b])
            self.mm_group(ps, ps.t[:, 256:512], [(hs[kc], wv1[:, kc, :]) for kc in range(KC)], reads=[w1.b, hE.b])
            self.cp("scalar", self.g_v.t[:, s, :], ps.t[:], [ps.b], [self.g_v.b])
            ps = self.pbank()
            self.mm_group(ps, ps.t[:, 0:256], [(hs[kc], wv2[:, kc, :]) for kc in range(KC)], reads=[w2.b, hE.b])
            self.cp("vector", self.g_ktm.t[:, s, :], ps.t[:, 0:256], [ps.b], [self.g_ktm.b])
        ob, _ = PR["gla_bg"]
        for s in range(NSUB2):
            ss = slice(s * 128, (s + 1) * 128)
            ch0 = (t0 + s * 128) // 64
            for d in range(2):
                ps = self.pbank()
                self.mm(ps, ps.t[:, 0:256], self.lrT.t[d * 32:d * 32 + 16, ss], self.wgb.t[d * 32:d * 32 + 16, :],
                        [self.lrT.b, self.wgb.b])
                gl = self.g_l
                self.tt("vector", gl.t[:], ps.t[:, 0:256], self.prb.t[:, ob + d * 256:ob + (d + 1) * 256], ALU.add,
                        [ps.b, self.prb.b], [gl.b])
                self.act(gl.t[:], gl.t[:], AF.Exp, [gl.b], [gl.b], scale=-1.0)
                self.act(gl.t[:], gl.t[:], AF.Ln, [gl.b, self.oneb.b], [gl.b], bias=self.oneb.t[:, 0:1])
                for c in range(2):
                    ps2 = self.pbank()
                    self.mm(ps2, ps2.t[:, 0:128], gl.t[:, c * 128:(c + 1) * 128], self.U[d].t[:], [gl.b, self.U[d].b])
                    eg, emg = self.g_eg[c], self.g_emg[c]
                    self.act(eg.t[:], ps2.t[:, 0:128], AF.Exp, [ps2.b], [eg.b], scale=-1.0 / 16)
                    self.act(emg.t[:], ps2.t[:, 0:128], AF.Exp, [ps2.b], [emg.b], scale=1.0 / 16)
                    self.stt(self.g_qg[d].t[:, c, ss], self.g_qT.t[:, c, ss], 0.125, eg.t[:], ALU.mult, ALU.mult,
                             [self.g_qT.b, eg.b], [self.g_qg[d].b])
                    self.tt("gpsimd", self.g_kmg[d].t[:, c, ss], self.g_kT.t[:, c, ss], emg.t[:], ALU.mult,
                            [self.g_kT.b, emg.b], [self.g_kmg[d].b])
                    src = eg.t[:, 63::64] if d == 0 else eg.t[:, 0::64]
                    self.cp("gpsimd", self.egl_gla.t[:, c, d, ch0:ch0 + 2], src, [eg.b], [self.egl_gla.b])
                ps3 = self.pbank()
                self.mm(ps3, ps3.t[:, 0:256], self.SU[d].t[:], gl.t[:], [gl.b, self.SU[d].b])
                self.act(self.g_ed.t[:], ps3.t[:, 0:256], AF.Exp, [ps3.b], [self.g_ed.b], scale=-1.0 / 16)
                self.tt("vector", self.g_kd[d].t[:, s, :], self.g_ktm.t[:, s, :], self.g_ed.t[:], ALU.mult,
                        [self.g_ktm.b, self.g_ed.b], [self.g_kd[d].b])
        ts = slice(t0, t0 + T2)
        items, rd = [], []
        for d in range(2):
            items.append((self.gla_qg[d, :, ts].rearrange("(c p) t -> p c t", p=128), self.g_qg[d].t[:]))
            items.append((self.gla_kmg[d, :, ts].rearrange("(c p) t -> p c t", p=128), self.g_kmg[d].t[:]))
            items.append((self.gla_kd[d, ts, :].rearrange("(s p) c -> p s c", p=128), self.g_kd[d].t[:]))
            rd += [self.g_qg[d].b, self.g_kmg[d].b, self.g_kd[d].b]
        items.append((self.gla_v[ts, :].rearrange("(s p) c -> p s c", p=128), self.g_v.t[:]))
        rd.append(self.g_v.b)
        k.dma_batch("gpsimd", items, reads=rd, prim=k.buf("st_gla"))

    def scan(self, l):
        k = self.k
        self.begin_scope()
        NST = self.NST
        if "gla" in self.mix:
            self.alloc_sc_gla()
        if "ssd" in self.mix:
            self.alloc_sc_ssd()
        if "gdn" in self.mix:
            self.alloc_sc_gdn()
        for n in range(NST):
            pp = n % 2
            sts = (n, NST - 1 - n)
            self.ld_items = [[], []]
            self.ld_bufs = [[], []]
            if "gla" in self.mix:
                self.sc_gla_load(sts, pp)
            if "ssd" in self.mix:
                self.sc_ssd_load(sts, pp)
            if "gdn" in self.mix:
                self.sc_gdn_load(sts, pp)
            for d in range(2):
                k.dma_batch("sync", self.ld_items[d], writes=self.ld_bufs[d], prim=k.buf("scin%d%d" % (d, pp)))
            self.st_items = [[], []]
            self.st_bufs = [[], []]
            if "gla" in self.mix:
                self.sc_gla_pre(sts, pp)
            for ci in range(2):
                if "gdn" in self.mix:
                    self.sc_gdn_step1(sts, pp, ci)
                if "gla" in self.mix:
                    self.sc_gla_step(sts, pp, ci)
                if "ssd" in self.mix:
                    self.sc_ssd_step(sts, pp, ci)
                if "gdn" in self.mix:
                    self.sc_gdn_step2(sts, pp, ci)
            if "gla" in self.mix:
                self.sc_gla_out(sts, pp)
            if "ssd" in self.mix:
                self.sc_ssd_out(sts, pp)
            if "gdn" in self.mix:
                self.sc_gdn_out(sts, pp)
            for d in range(2):
                k.dma_batch("gpsimd", self.st_items[d], reads=self.st_bufs[d], prim=k.buf("scout%d%d" % (d, pp)))
        self.end_scope()

    def alloc_sc_gla(self):
        k = self.k
        R2 = range(2)
        self.s_gq = [[self.sb("s_gq%d%d" % (d, p), [128, 2, 128], BF16) for p in R2] for d in R2]
        self.s_gk = [[self.sb("s_gk%d%d" % (d, p), [128, 2, 128], BF16) for p in R2] for d in R2]
        self.s_gkd = [[self.sb("s_gkd%d%d" % (d, p), [128, 256], BF16) for p in R2] for d in R2]
        self.s_gv = [[self.sb("s_gv%d%d" % (d, p), [128, 512], BF16) for p in R2] for d in R2]
        self.s_gatt = [self.sb("s_gatt%d" % d, [128, 4, 128], BF16) for d in R2]
        self.s_gS = [self.sb("s_gS%d" % d, [128, 2, 128], F32) for d in R2]
        self.s_gSb = [self.sb("s_gSb%d" % d, [128, 2, 128], BF16) for d in R2]
        self.s_go = [[self.sb("s_go%d%d" % (d, p), [128, 4, 128], F32) for p in R2] for d in R2]
        self.s_gop = [None, None]
        for d in R2:
            self.ms(self.s_gS[d].t[:], 0.0, [self.s_gS[d].b])
            self.ms(self.s_gSb[d].t[:], 0.0, [self.s_gSb[d].b])

    def sc_gla_load(self, sts, pp):
        k = self.k
        for d in range(2):
            st = sts[d]
            ts = slice(st * 128, (st + 1) * 128)
            q, kk, kd, v = self.s_gq[d][pp], self.s_gk[d][pp], self.s_gkd[d][pp], self.s_gv[d][pp]
            self.ld_items[d] += [(q.t[:], self.gla_qg[d, :, ts].rearrange("(c p) t -> p c t", p=128)),
                                 (kk.t[:], self.gla_kmg[d, :, ts].rearrange("(c p) t -> p c t", p=128)),
                                 (kd.t[:], self.gla_kd[d, ts, :]), (v.t[:], self.gla_v[ts, :])]
            self.ld_bufs[d] += [q.b, kk.b, kd.b, v.b]

    def sc_gla_pre(self, sts, pp):
        for d in range(2):
            q, kk = self.s_gq[d][pp], self.s_gk[d][pp]
            ps = self.pbank()
            for h in range(4):
                c, r = h // 2, h % 2
                rs = slice(r * 64, (r + 1) * 64)
                self.mm(ps, ps.t[:, h * 128:(h + 1) * 128], kk.t[rs, c, :], q.t[rs, c, :], [kk.b, q.b])
            att = self.s_gatt[d]
            self.tt("vector", att.t[:], ps.t[:].rearrange("p (h i) -> p h i", h=4),
                    self.U[d].t[:, None, :].to_broadcast([128, 4, 128]), ALU.mult, [ps.b, self.U[d].b], [att.b])

    def sc_gla_step(self, sts, pp, ci):
        k = self.k
        for d in range(2):
            st = sts[d]
            ch = ci if d == 0 else 1 - ci
            cs = slice(ch * 64, (ch + 1) * 64)
            chabs = st * 2 + ch
            q, kd, v = self.s_gq[d][pp], self.s_gkd[d][pp], self.s_gv[d][pp]
            att, S, Sb = self.s_gatt[d], self.s_gS[d], self.s_gSb[d]
            op = self.pbank()
            for h in range(4):
                c, r = h // 2, h % 2
                rs = slice(r * 64, (r + 1) * 64)
                oap = op.t[:, h * 64:(h + 1) * 64]
                self.mm(op, oap, Sb.t[rs, c, :], q.t[rs, c, cs], [Sb.b, q.b], start=True, stop=False)
                self.mm(op, oap, v.t[:, h * 128:(h + 1) * 128], att.t[:, h, cs], [v.b, att.b], start=False, stop=True,
                        acc=True)
            self.cp("vector", self.s_go[d][pp].t[:, :, cs], op.t[:, 0:256].rearrange("p (h i) -> p h i", h=4), [op.b],
                    [self.s_go[d][pp].b])
            pP = self.pbank()
            for h in range(4):
                c, r = h // 2, h % 2
                self.mm(pP, pP.t[r * 64:(r + 1) * 64, c * 128:(c + 1) * 128], kd.t[cs, h * 64:(h + 1) * 64],
                        v.t[cs, h * 128:(h + 1) * 128], [kd.b, v.b])
            for c in range(2):
                self.stt(S.t[:, c, :], S.t[:, c, :], self.egl_gla.t[:, c, d, chabs:chabs + 1],
                         pP.t[:, c * 128:(c + 1) * 128], ALU.mult, ALU.add, [S.b, pP.b, self.egl_gla.b], [S.b])
            self.cp("scalar", Sb.t[:], S.t[:], [S.b], [Sb.b])

    def sc_gla_out(self, sts, pp):
        k = self.k
        for d in range(2):
            st = sts[d]
            ts = slice(st * 128, (st + 1) * 128)
            o = self.s_go[d][pp]
            self.st_items[d].append((self.gla_o[d, :, ts].rearrange("(h p) t -> p h t", p=128), o.t[:]))
            self.st_bufs[d].append(o.b)

    def alloc_p2_common(self, l):
        k = self.k
        self.nega = self.sb("nega", [128, 32], F32)
        o, _ = PR["sm_alog"]
        self.act(self.nega.t[:], self.prb.t[:, o:o + 32], AF.Exp, [self.prb.b], [self.nega.b])
        self.ts("vector", self.nega.t[:], self.nega.t[:], -1.0, None, ALU.mult, None, [self.nega.b], [self.nega.b])
        self.sp = self.sb("sp", [128, NSUB2, 24], F32)
        self.gd = self.sb("gd", [128, NSUB2, 24], F32)
        self.lnb = self.sb("lnb", [128, NSUB2, 8], F32)
        self.beta = self.sb("beta", [128, NSUB2, 8], F32)
        self.lndt = self.sb("lndt", [128, NSUB2, 16], F32)
        self.c_zr = [self.sb("c_zr%d" % i, [128, TE2], BF16) for i in range(2)]
        self.c_dg = [self.sb("c_dg%d" % i, [128, 5, 128], BF16) for i in range(2)]
        self.c_n = 0

    def p2_smalls_post(self, l, t):
        k = self.k
        sm, sp, gd = self.sm, self.sp, self.gd
        t0 = t * T2
        self.act(sp.t[:], sm.t[:, :, 8:32], AF.Exp, [sm.b], [sp.b])
        self.act(sp.t[:], sp.t[:], AF.Ln, [sp.b, self.oneb.b], [sp.b], bias=self.oneb.t[:, 0:1])
        self.tt("vector", gd.t[:], sp.t[:], self.nega.t[:, None, 8:32].to_broadcast([128, NSUB2, 24]), ALU.mult,
                [sp.b, self.nega.b], [gd.b])
        self.act(self.lndt.t[:], sp.t[:, :, 8:24], AF.Ln, [sp.b], [self.lndt.b])
        self.act(self.lnb.t[:], sm.t[:, :, 0:8], AF.Exp, [sm.b], [self.lnb.b], scale=-1.0)
        self.act(self.lnb.t[:], self.lnb.t[:], AF.Ln, [self.lnb.b, self.oneb.b], [self.lnb.b], bias=self.oneb.t[:, 0:1])
        self.act(self.beta.t[:], self.lnb.t[:], AF.Exp, [self.lnb.b], [self.beta.b], scale=-1.0)
        self.ts("vector", self.lnb.t[:], self.lnb.t[:], -1.0, None, ALU.mult, None, [self.lnb.b], [self.lnb.b])
        ps = self.pbank()
        for s in range(NSUB2):
            for c in range(2):
                j = s * 2 + c
                self.mm(ps, ps.t[:, j * 24:(j + 1) * 24], self.cind[c].t[:], gd.t[:, s, :], [self.cind[c].b, gd.b])
        ch0 = t0 // 64
        n = NSUB2 * 2
        self.act(self.egl.t[:, ch0:ch0 + n, :], ps.t[:, 0:n * 24].rearrange("p (j c) -> p j c", j=n), AF.Exp,
                 [ps.b], [self.egl.b])

    def conv_chunk(self, l, w, wcols, cwname, cc, bias_ap, out_ap, out_buf):
        k = self.k
        hE = self.hE
        ps = self.pbank()
        self.mm_group(ps, ps.t[:, 0:T2], [(wcols(kc), hE.t[:, kc, 0:T2]) for kc in range(KC)], reads=[w.b, hE.b])
        self.mm_group(ps, ps.t[:, T2:TE2], [(wcols(kc), hE.t[:, kc, T2:TE2]) for kc in range(KC)], reads=[w.b, hE.b])
        zr = self.c_zr[self.c_n % 2]
        dg = self.c_dg[self.c_n % 2]
        self.c_n += 1
        self.cp("scalar", zr.t[:], ps.t[:, 0:TE2], [ps.b], [zr.b])
        o, _ = PV[cwname]
        for kk in range(5):
            self.ts("vector", dg.t[:, kk, :], self.ident_bf.t[:], self.pv.t[:, l, o + cc * 5 + kk:o + cc * 5 + kk + 1], None,
                    ALU.mult, None, [self.ident_bf.b, self.pv.b], [dg.b])
        pc = self.pbank()
        self.mm_group(pc, pc.t[:, 0:T2], [(dg.t[:, kk, :], zr.t[:, kk:kk + T2]) for kk in range(5)], reads=[dg.b, zr.b])
        if bias_ap is not None:
            self.act(out_ap, pc.t[:, 0:T2], AF.Silu, [pc.b, self.pv.b], [out_buf], bias=bias_ap)
        else:
            self.act(out_ap, pc.t[:, 0:T2], AF.Silu, [pc.b], [out_buf])

    def alloc_p2_ssd(self, l):
        R2 = range(2)
        self.d_xbc = self.sb("d_xbc", [128, 8, T2], BF16)
        self.d_xtm = self.sb("d_xtm", [128, NSUB2, 512], BF16)
        self.d_Btm = self.sb("d_Btm", [128, NSUB2, 256], BF16)
        self.d_xw = [self.sb("d_xw%d" % d, [128, NSUB2, 512], BF16) for d in R2]
        self.d_yd = self.sb("d_yd", [128, 4, T2], F32)
        self.d_cexp = self.sb("d_cexp", [128, 16, 128], BF16)
        self.d_MT = self.sb("d_MT", [128, 16, 128], BF16)
        self.d_E = [self.sb("d_E%d" % i, [128, 128], F32) for i in range(4)]
        self.d_E1 = [self.sb("d_E1%d" % i, [128, 128], F32) for i in range(4)]
        self.d_sc = self.sb("d_sc", [128, 2, 128], F32)
        self.d_acum = self.sb("d_acum", [128, 16], F32)
        self.d_nb = self.sb("d_nb", [128, 16], F32)
        self.d_wst = self.sb("d_wst", [128, 16], F32)

    def p2_ssd(self, l, t):
        k = self.k
        t0 = t * T2
        ocb, _ = PV["ssd_cb"]
        od, _ = PV["ssd_d"]
        for ui in range(6, 10):
            w = self.load_unit(l, ("inA", ui))
            wv = w.t[:, 0:KC * 256].rearrange("p (k c) -> p k c", k=KC)
            for c2 in range(2):
                cc = (ui - 6) * 2 + c2
                self.conv_chunk(l, w, lambda kc, c2=c2, wv=wv: wv[:, kc, c2 * 128:(c2 + 1) * 128], "ssd_cw", cc,
                                self.pv.t[:, l, ocb + cc:ocb + cc + 1], self.d_xbc.t[:, cc, :], self.d_xbc.b)
        xbc = self.d_xbc
        for s in range(NSUB2):
            ss = slice(s * 128, (s + 1) * 128)
            tok = slice(t0 + s * 128, t0 + (s + 1) * 128)
            pT = self.ptr
            pTv = pT.t[:]
            if getattr(self, 'cut', 99) <= -1:
                continue
            for c in range(6):
                self.tr(pTv[:, c * 128:(c + 1) * 128], xbc.t[:, c, ss], [xbc.b], pT)
            if getattr(self, 'cut', 99) <= 0:
                continue
            self.cp("scalar", self.d_xtm.t[:, s, :], pTv[:, 0:512], [pT.b], [self.d_xtm.b])
            self.cp("vector", self.d_Btm.t[:, s, :], pTv[:, 512:768], [pT.b], [self.d_Btm.b])
            if getattr(self, 'cut', 99) <= 1:
                continue
            pa = self.pbank()
            for d in range(2):
                self.mm(pa, pa.t[:, d * 8:(d + 1) * 8], self.U[d].t[:], self.gd.t[:, s, 8 + d * 8:16 + d * 8],
                        [self.U[d].b, self.gd.b])
            for d in range(2):
                self.mm(pa, pa.t[:, 16 + d * 8:24 + d * 8], self.SU[d].t[:], self.gd.t[:, s, 8 + d * 8:16 + d * 8],
                        [self.SU[d].b, self.gd.b])
            self.cp("vector", self.d_acum.t[:], pa.t[:, 0:16], [pa.b], [self.d_acum.b])
            self.act(self.d_wst.t[:], pa.t[:, 16:32], AF.Exp, [pa.b], [self.d_wst.b])
            self.tt("vector", self.d_wst.t[:], self.d_wst.t[:], self.sp.t[:, s, 8:24], ALU.mult,
                    [self.d_wst.b, self.sp.b], [self.d_wst.b])
            self.tt("vector", self.d_nb.t[:], self.lndt.t[:, s, :], self.d_acum.t[:], ALU.subtract,
                    [self.lndt.b, self.d_acum.b], [self.d_nb.b])
            if getattr(self, 'cut', 99) <= 2:
                continue
            for d in range(2):
                self.tt("gpsimd", self.d_xw[d].t[:, s, :].rearrange("p (h q) -> p h q", h=8),
                        self.d_xtm.t[:, s, :].rearrange("p (h q) -> p h q", h=8),
                        self.d_wst.t[:, d * 8:(d + 1) * 8, None].to_broadcast([128, 8, 64]), ALU.mult,
                        [self.d_xtm.b, self.d_wst.b], [self.d_xw[d].b])
            if getattr(self, 'cut', 99) <= 3:
                continue
            pS = self.pbank()
            for g in range(2):
                self.mm(pS, pS.t[:, g * 128:(g + 1) * 128], xbc.t[:, 4 + g, ss], xbc.t[:, 6 + g, ss], [xbc.b])
            self.cp("scalar", self.d_sc.t[:], pS.t[:, 0:256].rearrange("p (g i) -> p g i", g=2), [pS.b], [self.d_sc.b])
            if getattr(self, 'cut', 99) <= 4:
                continue
            for d in range(2):
                for h in range(8):
                    dh = d * 8 + h
                    g = h // 4
                    pA = self.pbank()
                    bc = self.d_acum.t[:, dh:dh + 1].to_broadcast([128, 128])
                    self.mm(pA, pA.t[:, 0:128], bc, self.ident.t[:], [self.d_acum.b, self.ident.b])
                    self.mm(pA, pA.t[:, 128:256], bc, self.ident.t[:], [self.d_acum.b, self.ident.b], start=True, stop=False)
                    self.mm(pA, pA.t[:, 128:256], self.ident.t[:], self.maskneg[d].t[:], [self.ident.b, self.maskneg[d].b],
                            start=False, stop=True, acc=True)
                    E1, E = self.d_E1[dh % 4], self.d_E[dh % 4]
                    self.act(E1.t[:], pA.t[:, 0:128], AF.Exp, [pA.b], [E1.b])
                    self.act(E.t[:], pA.t[:, 128:256], AF.Exp, [pA.b, self.d_nb.b], [E.b], bias=self.d_nb.t[:, dh:dh + 1])
                    self.tt("gpsimd", self.d_cexp.t[:, dh, :], xbc.t[:, 6 + g, ss], E1.t[:], ALU.mult,
                            [xbc.b, E1.b], [self.d_cexp.b])
                    self.tt("vector", self.d_MT.t[:, dh, :], E.t[:], self.d_sc.t[:, g, :], ALU.mult,
                            [E.b, self.d_sc.b], [self.d_MT.b])
            if getattr(self, 'cut', 99) <= 5:
                continue
            pY = self.pbank()
            for h in range(8):
                c, r = h // 2, h % 2
                for d in range(2):
                    self.mm(pY, pY.t[r * 64:(r + 1) * 64, c * 128:(c + 1) * 128], self.d_xtm.t[:, s, h * 64:(h + 1) * 64],
                            self.d_MT.t[:, d * 8 + h, :], [self.d_xtm.b, self.d_MT.b], start=(d == 0), stop=(d == 1),
                            acc=(d == 1))
            for c in range(4):
                self.stt(self.d_yd.t[:, c, ss], xbc.t[:, c, ss], self.pv.t[:, l, od + c:od + c + 1],
                         pY.t[:, c * 128:(c + 1) * 128], ALU.mult, ALU.add, [xbc.b, pY.b, self.pv.b], [self.d_yd.b])
            if getattr(self, 'cut', 99) <= 6:
                continue
            k.dma_batch("gpsimd", [(self.ssd_cexp[d, :, tok].rearrange("(h p) t -> p h t", p=128),
                                    self.d_cexp.t[:, d * 8:(d + 1) * 8, :]) for d in range(2)],
                        reads=[self.d_cexp.b], prim=k.buf("st_ssd_s"))
        if getattr(self, 'cut', 99) <= 7:
            return
        ts = slice(t0, t0 + T2)
        items = [(self.ssd_B[ts, :].rearrange("(s p) c -> p s c", p=128), self.d_Btm.t[:]),
                 (self.ssd_ydiag[:, ts].rearrange("(c p) t -> p c t", p=128), self.d_yd.t[:])]
        for d in range(2):
            items.append((self.ssd_xw[d, ts, :].rearrange("(s p) c -> p s c", p=128), self.d_xw[d].t[:]))
        k.dma_batch("gpsimd", items, reads=[self.d_Btm.b, self.d_yd.b, self.d_xw[0].b, self.d_xw[1].b],
                    prim=k.buf("st_ssd_t"))

    def alloc_sc_ssd(self):
        k = self.k
        R2 = range(2)
        self.s_dc = [[self.sb("s_dc%d%d" % (d, p), [128, 8, 128], BF16) for p in R2] for d in R2]
        self.s_dB = [[self.sb("s_dB%d%d" % (d, p), [128, 256], BF16) for p in R2] for d in R2]
        self.s_dxw = [[self.sb("s_dxw%d%d" % (d, p), [128, 512], BF16) for p in R2] for d in R2]
        self.s_dS = [self.sb("s_dS%d" % d, [128, 512], F32) for d in R2]
        self.s_dSb = [self.sb("s_dSb%d" % d, [128, 512], BF16) for d in R2]
        self.s_dy = [[self.sb("s_dy%d%d" % (d, p), [128, 4, 128], F32) for p in R2] for d in R2]
        self.s_dop = [None, None]
        for d in R2:
            self.ms(self.s_dS[d].t[:], 0.0, [self.s_dS[d].b])
            self.ms(self.s_dSb[d].t[:], 0.0, [self.s_dSb[d].b])

    def sc_ssd_load(self, sts, pp):
        k = self.k
        for d in range(2):
            st = sts[d]
            ts = slice(st * 128, (st + 1) * 128)
            c, B, xw = self.s_dc[d][pp], self.s_dB[d][pp], self.s_dxw[d][pp]
            self.ld_items[d] += [(c.t[:], self.ssd_cexp[d, :, ts].rearrange("(h p) t -> p h t", p=128)),
                                 (B.t[:], self.ssd_B[ts, :]), (xw.t[:], self.ssd_xw[d, ts, :])]
            self.ld_bufs[d] += [c.b, B.b, xw.b]

    def sc_ssd_step(self, sts, pp, ci):
        k = self.k
        for d in range(2):
            st = sts[d]
            ch = ci if d == 0 else 1 - ci
            cs = slice(ch * 64, (ch + 1) * 64)
            chabs = st * 2 + ch
            c, B, xw = self.s_dc[d][pp], self.s_dB[d][pp], self.s_dxw[d][pp]
            S, Sb = self.s_dS[d], self.s_dSb[d]
            op = self.pbank()
            for h in range(8):
                cc, r = h // 2, h % 2
                self.mm(op, op.t[r * 64:(r + 1) * 64, cc * 64:(cc + 1) * 64],
                        Sb.t[:, h * 64:(h + 1) * 64], c.t[:, h, cs], [Sb.b, c.b])
            self.cp("vector", self.s_dy[d][pp].t[:, :, cs], op.t[:, 0:256].rearrange("p (c i) -> p c i", c=4), [op.b],
                    [self.s_dy[d][pp].b])
            pP = self.pbank()
            for g in range(2):
                self.mm(pP, pP.t[:, g * 256:(g + 1) * 256], B.t[cs, g * 128:(g + 1) * 128], xw.t[cs, g * 256:(g + 1) * 256],
                        [B.b, xw.b])
            for h in range(8):
                self.stt(S.t[:, h * 64:(h + 1) * 64], S.t[:, h * 64:(h + 1) * 64],
                         self.egl.t[:, chabs, 8 + d * 8 + h:8 + d * 8 + h + 1], pP.t[:, h * 64:(h + 1) * 64], ALU.mult, ALU.add,
                         [S.b, pP.b, self.egl.b], [S.b])
            self.cp("scalar", Sb.t[:], S.t[:], [S.b], [Sb.b])

    def sc_ssd_out(self, sts, pp):
        k = self.k
        for d in range(2):
            st = sts[d]
            ts = slice(st * 128, (st + 1) * 128)
            y = self.s_dy[d][pp]
            self.st_items[d].append((self.ssd_yoff[d, :, ts].rearrange("(c p) t -> p c t", p=128), y.t[:]))
            self.st_bufs[d].append(y.b)

    def alloc_p2_gdn(self, l):
        k = self.k
        R2 = range(2)
        self.e_raw = self.sb("e_raw", [128, 8, T2], F32)
        self.e_q = self.sb("e_q", [128, 4, T2], BF16)
        self.e_k = self.sb("e_k", [128, 4, T2], BF16)
        self.e_v = self.sb("e_v", [128, 4, T2], BF16)
        self.e_sq = self.sb("e_sq", [128, T2], BF16)
        self.e_rs = self.sb("e_rs", [128, T2], F32)
        self.e_ktm = self.sb("e_ktm", [128, 4, 128], BF16)
        self.e_vtm = self.sb("e_vtm", [128, 4, 128], BF16)
        self.e_kg = [self.sb("e_kg%d" % d, [128, NSUB2, 512], BF16) for d in R2]
        self.e_kbg = [self.sb("e_kbg%d" % d, [128, 4, 128], BF16) for d in R2]
        self.e_vb = [self.sb("e_vb%d" % d, [128, 4, 128], BF16) for d in R2]
        self.e_kkqk = [self.sb("e_kkqk%d" % h, [128, 2, 128], F32) for h in range(4)]
        self.e_B = [[self.sb("e_B%d_%d" % (u, p), [128, 2, 128], BF16) for p in R2] for u in range(8)]
        self.e_XT = [[self.sb("e_XT%d_%d" % (u, p), [128, 128], BF16) for p in R2] for u in range(8)]
        self.e_E = [self.sb("e_E%d" % i, [128, 128], F32) for i in range(8)]
        self.e_En = 0
        self.e_att = [self.sb("e_att%d" % d, [128, 4, 128], BF16) for d in R2]
        self.e_u = [self.sb("e_u%d" % d, [128, 4, 128], F32) for d in R2]
        self.e_wn = [self.sb("e_wn%d" % d, [128, 4, 128], BF16) for d in R2]
        self.e_qg = [self.sb("e_qg%d" % d, [128, 4, T2], BF16) for d in R2]
        self.e_G = self.sb("e_G", [128, 8], F32)
        self.e_nG = self.sb("e_nG", [128, 8], F32)
        self.e_rA = self.sb("e_rA", [128, 8], F32)
        self.e_egs = self.sb("e_egs", [128, 8], F32)
        self.e_bG = self.sb("e_bG", [128, 8], F32)
        self.mposA = [self.sb("mposA%d" % d, [128, 128], F32) for d in R2]
        self.mnegAT = [self.sb("mnegAT%d" % d, [128, 128], F32) for d in R2]
        for d in R2:
            self.ts("vector", self.mposA[d].t[:], self.SU[d].t[:], -30000.0, 30000.0, ALU.mult, ALU.add,
                    [self.SU[d].b], [self.mposA[d].b])
            self.ts("vector", self.mnegAT[d].t[:], self.SU[1 - d].t[:], 30000.0, -30000.0, ALU.mult, ALU.add,
                    [self.SU[1 - d].b], [self.mnegAT[d].b])

    def nextE(self):
        t = self.e_E[self.e_En % 8]
        self.e_En += 1
        return t

    def p2_gdn(self, l, t):
        k = self.k
        t0 = t * T2
        for ui in range(6):
            w = self.load_unit(l, ("inA", ui))
            wv = w.t[:, 0:KC * 256].rearrange("p (k c) -> p k c", k=KC)
            for c2 in range(2):
                cc = ui * 2 + c2
                if cc < 8:
                    out_ap, ob = self.e_raw.t[:, cc, :], self.e_raw.b
                else:
                    out_ap, ob = self.e_v.t[:, cc - 8, :], self.e_v.b
                self.conv_chunk(l, w, lambda kc, c2=c2, wv=wv: wv[:, kc, c2 * 128:(c2 + 1) * 128], "gdn_cw", cc, None,
                                out_ap, ob)
        for cc in range(8):
            raw = self.e_raw.t[:, cc, :]
            self.act(self.e_sq.t[:], raw, AF.Square, [self.e_raw.b], [self.e_sq.b])
            ps = self.pbank()
            self.mm(ps, ps.t[:, 0:T2], self.ones_bf.t[:], self.e_sq.t[:], [self.ones_bf.b, self.e_sq.b])
            self.rstd_from_ss(ps.t[:, 0:T2], ps.b, self.e_rs, 1)
            dst = self.e_q if cc < 4 else self.e_k
            self.stt(dst.t[:, cc % 4, :], raw, (128 ** -0.5) if cc < 4 else 1.0, self.e_rs.t[:], ALU.mult, ALU.mult,
                     [self.e_raw.b, self.e_rs.b], [dst.b])
        for s in range(NSUB2):
            ss = slice(s * 128, (s + 1) * 128)
            tok = slice(t0 + s * 128, t0 + (s + 1) * 128)
            st = (t0 + s * 128) // 128
            pa = self.pbank()
            for d in range(2):
                self.mm(pa, pa.t[:, d * 4:(d + 1) * 4], self.U[d].t[:], self.gd.t[:, s, d * 4:(d + 1) * 4],
                        [self.U[d].b, self.gd.b])
            for d in range(2):
                self.mm(pa, pa.t[:, 8 + d * 4:12 + d * 4], self.SU[d].t[:], self.gd.t[:, s, d * 4:(d + 1) * 4],
                        [self.SU[d].b, self.gd.b])
            self.cp("vector", self.e_G.t[:], pa.t[:, 0:8], [pa.b], [self.e_G.b])
            self.act(self.e_egs.t[:], pa.t[:, 8:16], AF.Exp, [pa.b], [self.e_egs.b])
            self.act(self.e_bG.t[:], self.e_G.t[:], AF.Exp, [self.e_G.b], [self.e_bG.b])
            self.tt("vector", self.e_bG.t[:], self.e_bG.t[:], self.beta.t[:, s, :], ALU.mult, [self.e_bG.b, self.beta.b],
                    [self.e_bG.b])
            self.tt("vector", self.e_rA.t[:], self.e_G.t[:], self.lnb.t[:, s, :], ALU.add, [self.e_G.b, self.lnb.b],
                    [self.e_rA.b])
            self.ts("vector", self.e_nG.t[:], self.e_G.t[:], -1.0, None, ALU.mult, None, [self.e_G.b], [self.e_nG.b])
            pT = self.ptr
            for h in range(4):
                self.tr(pT.t[:, h * 128:(h + 1) * 128], self.e_k.t[:, h, ss], [self.e_k.b], pT)
            for h in range(4):
                self.tr(pT.t[:, 512 + h * 128:512 + (h + 1) * 128], self.e_v.t[:, h, ss], [self.e_v.b], pT)
            self.cp("scalar", self.e_ktm.t[:], pT.t[:, 0:512].rearrange("p (h c) -> p h c", h=4), [pT.b], [self.e_ktm.b])
            self.cp("vector", self.e_vtm.t[:], pT.t[:, 512:1024].rearrange("p (h c) -> p h c", h=4), [pT.b], [self.e_vtm.b])
            for d in range(2):
                ds_ = slice(d * 4, (d + 1) * 4)
                bc = lambda tl: tl.t[:, ds_, None].to_broadcast([128, 4, 128])
                self.tt("gpsimd", self.e_kbg[d].t[:], self.e_ktm.t[:], self.e_bG.t[:, d * 4:(d + 1) * 4, None].to_broadcast([128, 4, 128]),
                        ALU.mult, [self.e_ktm.b, self.e_bG.b], [self.e_kbg[d].b])
                self.tt("gpsimd", self.e_kg[d].t[:, s, :].rearrange("p (h c) -> p h c", h=4), self.e_ktm.t[:],
                        self.e_egs.t[:, d * 4:(d + 1) * 4, None].to_broadcast([128, 4, 128]), ALU.mult,
                        [self.e_ktm.b, self.e_egs.b], [self.e_kg[d].b])
                self.tt("gpsimd", self.e_vb[d].t[:], self.e_vtm.t[:],
                        self.beta.t[:, s, d * 4:(d + 1) * 4, None].to_broadcast([128, 4, 128]), ALU.mult,
                        [self.e_vtm.b, self.beta.b], [self.e_vb[d].b])
            for h in range(4):
                ps = self.pbank()
                self.mm(ps, ps.t[:, 0:128], self.e_k.t[:, h, ss], self.e_k.t[:, h, ss], [self.e_k.b])
                self.mm(ps, ps.t[:, 128:256], self.e_k.t[:, h, ss], self.e_q.t[:, h, ss], [self.e_k.b, self.e_q.b])
                self.cp("scalar", self.e_kkqk[h].t[:], ps.t[:, 0:256].rearrange("p (a i) -> p a i", a=2), [ps.b],
                        [self.e_kkqk[h].b])
            units = [(d, h) for d in range(2) for h in range(4)]
            for ui_, (d, h) in enumerate(units):
                dh = d * 4 + h
                kk, qk = self.e_kkqk[h].t[:, 0, :], self.e_kkqk[h].t[:, 1, :]
                kb_ = self.e_kkqk[h].b
                B0 = self.e_B[ui_][0]
                pA = self.pbank()
                bcG = self.e_G.t[:, dh:dh + 1].to_broadcast([128, 128])
                bcR = self.e_rA.t[:, dh:dh + 1].to_broadcast([128, 128])
                idt = self.ident
                self.mm(pA, pA.t[:, 0:128], bcG, idt.t[:], [self.e_G.b, idt.b], start=True, stop=False)
                self.mm(pA, pA.t[:, 0:128], idt.t[:], self.mposA[d].t[:], [idt.b, self.mposA[d].b], start=False, stop=True, acc=True)
                self.mm(pA, pA.t[:, 128:256], bcR, idt.t[:], [self.e_rA.b, idt.b], start=True, stop=False)
                self.mm(pA, pA.t[:, 128:256], idt.t[:], self.mnegAT[d].t[:], [idt.b, self.mnegAT[d].b], start=False, stop=True, acc=True)
                self.mm(pA, pA.t[:, 256:384], bcG, idt.t[:], [self.e_G.b, idt.b], start=True, stop=False)
                self.mm(pA, pA.t[:, 256:384], idt.t[:], self.maskneg[d].t[:], [idt.b, self.maskneg[d].b], start=False, stop=True, acc=True)
                self.mm(pA, pA.t[:, 384:512], bcG, idt.t[:], [self.e_G.b, idt.b])
                E = self.nextE()
                self.act(E.t[:], pA.t[:, 0:128], AF.Exp, [pA.b, self.e_rA.b], [E.b], scale=-1.0, bias=self.e_rA.t[:, dh:dh + 1])
                self.stt(B0.t[:, 0, :], E.t[:], -1.0, kk, ALU.mult, ALU.mult, [E.b, kb_], [B0.b])
                E = self.nextE()
                self.act(E.t[:], pA.t[:, 128:256], AF.Exp, [pA.b, self.e_nG.b], [E.b], bias=self.e_nG.t[:, dh:dh + 1])
                self.stt(B0.t[:, 1, :], E.t[:], -1.0, kk, ALU.mult, ALU.mult, [E.b, kb_], [B0.b])
                E = self.nextE()
                self.act(E.t[:], pA.t[:, 256:384], AF.Exp, [pA.b, self.e_nG.b], [E.b], bias=self.e_nG.t[:, dh:dh + 1])
                self.tt("vector", self.e_att[d].t[:, h, :], E.t[:], qk, ALU.mult, [E.b, kb_], [self.e_att[d].b])
                E = self.nextE()
                self.act(E.t[:], pA.t[:, 384:512], AF.Exp, [pA.b], [E.b])
                self.tt("gpsimd", self.e_qg[d].t[:, h, ss], self.e_q.t[:, h, ss], E.t[:], ALU.mult, [self.e_q.b, E.b],
                        [self.e_qg[d].b])
                self.tt("gpsimd", self.e_XT[ui_][0].t[:], B0.t[:, 1, :], self.ident.t[:], ALU.add, [B0.b, self.ident.b],
                        [self.e_XT[ui_][0].b])
            for lev in range(1, 6):
                pi, po = (lev - 1) % 2, lev % 2
                n = 2 if lev < 5 else 1
                sqb = []
                for pr in range(4):
                    ps = self.pbank()
                    for a in range(2):
                        Bp = self.e_B[2 * pr + a][pi]
                        self.mm(ps, ps.t[:, a * 256:a * 256 + 128], Bp.t[:, 1, :], Bp.t[:, 0, :], [Bp.b])
                        if lev < 5:
                            self.mm(ps, ps.t[:, a * 256 + 128:a * 256 + 256], Bp.t[:, 0, :], Bp.t[:, 1, :], [Bp.b])
                    sqb.append(ps)
                for ui_ in range(8):
                    ps = sqb[ui_ // 2]
                    a = ui_ % 2
                    Bn = self.e_B[ui_][po]
                    self.cp("scalar", Bn.t[:, 0:n, :], ps.t[:, a * 256:a * 256 + n * 128].rearrange("p (a i) -> p a i", a=n),
                            [ps.b], [Bn.b])
                prb_ = []
                for hf in range(2):
                    ps2 = self.pbank()
                    for a in range(4):
                        ui_ = hf * 4 + a
                        Bn, Xp = self.e_B[ui_][po], self.e_XT[ui_][pi]
                        self.mm(ps2, ps2.t[:, a * 128:(a + 1) * 128], Bn.t[:, 0, :], Xp.t[:], [Bn.b, Xp.b])
                    prb_.append(ps2)
                for ui_ in range(8):
                    ps2 = prb_[ui_ // 4]
                    a = ui_ % 4
                    Xp, Xn = self.e_XT[ui_][pi], self.e_XT[ui_][po]
                    self.tt("vector", Xn.t[:], Xp.t[:], ps2.t[:, a * 128:(a + 1) * 128], ALU.add, [Xp.b, ps2.b], [Xn.b])
            fin = 5 % 2
            for d in range(2):
                pu = self.pbank()
                pw = self.pbank()
                for h in range(4):
                    X = self.e_XT[d * 4 + h][fin]
                    self.mm(pu, pu.t[:, h * 128:(h + 1) * 128], X.t[:], self.e_vb[d].t[:, h, :], [X.b, self.e_vb[d].b])
                for h in range(4):
                    X = self.e_XT[d * 4 + h][fin]
                    self.mm(pw, pw.t[:, h * 128:(h + 1) * 128], self.e_kbg[d].t[:, h, :], X.t[:], [X.b, self.e_kbg[d].b])
                self.cp("scalar", self.e_u[d].t[:], pu.t[:].rearrange("p (h c) -> p h c", h=4), [pu.b], [self.e_u[d].b])
                self.ts("vector", self.e_wn[d].t[:], pw.t[:].rearrange("p (h c) -> p h c", h=4), -1.0, None, ALU.mult, None,
                        [pw.b], [self.e_wn[d].b])
            items, rd = [], []
            for d in range(2):
                items.append((self.gdn_u[d, tok, :], self.e_u[d].t[:].rearrange("p h c -> p (h c)")))
                items.append((self.gdn_wn[d, :, tok].rearrange("(h p) t -> p h t", p=128), self.e_wn[d].t[:]))
                items.append((self.gdn_att[d, st * 128:(st + 1) * 128, :], self.e_att[d].t[:].rearrange("p h c -> p (h c)")))
                rd += [self.e_u[d].b, self.e_wn[d].b, self.e_att[d].b]
            k.dma_batch("gpsimd", items, reads=rd, prim=k.buf("st_gdn_s"))
        ts = slice(t0, t0 + T2)
        items, rd = [], []
        for d in range(2):
            items.append((self.gdn_kg[d, ts, :].rearrange("(s p) c -> p s c", p=128), self.e_kg[d].t[:]))
            items.append((self.gdn_qg[d, :, ts].rearrange("(h p) t -> p h t", p=128), self.e_qg[d].t[:]))
            rd += [self.e_kg[d].b, self.e_qg[d].b]
        k.dma_batch("gpsimd", items, reads=rd, prim=k.buf("st_gdn_t"))

    def alloc_sc_gdn(self):
        k = self.k
        R2 = range(2)
        self.s_ewn = [[self.sb("s_ewn%d%d" % (d, p), [128, 4, 128], BF16) for p in R2] for d in R2]
        self.s_eu = [[self.sb("s_eu%d%d" % (d, p), [128, 512], F32) for p in R2] for d in R2]
        self.s_eqg = [[self.sb("s_eqg%d%d" % (d, p), [128, 4, 128], BF16) for p in R2] for d in R2]
        self.s_ekg = [[self.sb("s_ekg%d%d" % (d, p), [128, 512], BF16) for p in R2] for d in R2]
        self.s_eatt = [[self.sb("s_eatt%d%d" % (d, p), [128, 512], BF16) for p in R2] for d in R2]
        self.s_eS = [self.sb("s_eS%d" % d, [128, 4, 128], F32) for d in R2]
        self.s_eSb = [self.sb("s_eSb%d" % d, [128, 4, 128], BF16) for d in R2]
        self.s_evn = [self.sb("s_evn%d" % d, [128, 512], BF16) for d in R2]
        self.s_eo = [[self.sb("s_eo%d%d" % (d, p), [128, 4, 128], F32) for p in R2] for d in R2]
        self.s_eop = [None, None]
        for d in R2:
            self.ms(self.s_eS[d].t[:], 0.0, [self.s_eS[d].b])
            self.ms(self.s_eSb[d].t[:], 0.0, [self.s_eSb[d].b])

    def sc_gdn_load(self, sts, pp):
        k = self.k
        for d in range(2):
            st = sts[d]
            ts = slice(st * 128, (st + 1) * 128)
            wn, u, qg, kg, att = self.s_ewn[d][pp], self.s_eu[d][pp], self.s_eqg[d][pp], self.s_ekg[d][pp], self.s_eatt[d][pp]
            self.ld_items[d] += [(wn.t[:], self.gdn_wn[d, :, ts].rearrange("(h p) t -> p h t", p=128)),
                                 (u.t[:], self.gdn_u[d, ts, :]),
                                 (qg.t[:], self.gdn_qg[d, :, ts].rearrange("(h p) t -> p h t", p=128)),
                                 (kg.t[:], self.gdn_kg[d, ts, :]), (att.t[:], self.gdn_att[d, ts, :])]
            self.ld_bufs[d] += [wn.b, u.b, qg.b, kg.b, att.b]

    def sc_gdn_step1(self, sts, pp, ci):
        for d in range(2):
            ch = ci if d == 0 else 1 - ci
            cs = slice(ch * 64, (ch + 1) * 64)
            wn, u = self.s_ewn[d][pp], self.s_eu[d][pp]
            Sb, vn = self.s_eSb[d], self.s_evn[d]
            pv = self.pbank()
            for h in range(4):
                self.mm(pv, pv.t[cs, h * 128:(h + 1) * 128], wn.t[:, h, cs], Sb.t[:, h, :], [wn.b, Sb.b])
            self.tt("vector", vn.t[cs, :], u.t[cs, :], pv.t[cs, :], ALU.add, [u.b, pv.b], [vn.b])

    def sc_gdn_step2(self, sts, pp, ci):
        for d in range(2):
            st = sts[d]
            ch = ci if d == 0 else 1 - ci
            cs = slice(ch * 64, (ch + 1) * 64)
            chabs = st * 2 + ch
            qg, kg, att = self.s_eqg[d][pp], self.s_ekg[d][pp], self.s_eatt[d][pp]
            S, Sb, vn = self.s_eS[d], self.s_eSb[d], self.s_evn[d]
            op = self.pbank()
            for h in range(4):
                oap = op.t[:, h * 64:(h + 1) * 64]
                self.mm(op, oap, Sb.t[:, h, :], qg.t[:, h, cs], [Sb.b, qg.b], start=True, stop=False)
                self.mm(op, oap, vn.t[cs, h * 128:(h + 1) * 128], att.t[cs, h * 128 + ch * 64:h * 128 + (ch + 1) * 64],
                        [vn.b, att.b], start=False, stop=True, acc=True)
            self.cp("scalar", self.s_eo[d][pp].t[:, :, cs], op.t[:, 0:256].rearrange("p (h i) -> p h i", h=4), [op.b],
                    [self.s_eo[d][pp].b])
            pP = self.pbank()
            for h in range(4):
                self.mm(pP, pP.t[:, h * 128:(h + 1) * 128], kg.t[cs, h * 128:(h + 1) * 128], vn.t[cs, h * 128:(h + 1) * 128],
                        [kg.b, vn.b])
            for h in range(4):
                self.stt(S.t[:, h, :], S.t[:, h, :], self.egl.t[:, chabs, d * 4 + h:d * 4 + h + 1],
                         pP.t[:, h * 128:(h + 1) * 128], ALU.mult, ALU.add, [S.b, pP.b, self.egl.b], [S.b])
            self.cp("scalar", Sb.t[:], S.t[:], [S.b], [Sb.b])

    def sc_gdn_out(self, sts, pp):
        k = self.k
        for d in range(2):
            st = sts[d]
            ts = slice(st * 128, (st + 1) * 128)
            o = self.s_eo[d][pp]
            self.st_items[d].append((self.gdn_o[d, :, ts].rearrange("(h p) t -> p h t", p=128), o.t[:]))
            self.st_bufs[d].append(o.b)
_CACHE = {}


def run(inputs, depth, mix=("gdn", "gla", "ssd")):
    xp = np.asarray(inputs["x_prompt"], np.float32)
    xs = np.asarray(inputs["x_sample"], np.float32)
    L = xp.shape[1]
    assert xs.shape[1] == L
    seqs = [xp[i] for i in range(xp.shape[0])] + [xs[i] for i in range(xs.shape[0])]
    nseq = len(seqs)
    key = (L, depth, tuple(mix))
    if key not in _CACHE:
        _CACHE[key] = Builder(L, depth, mix).build()
    nc = _CACHE[key]
    inp = {n: np.asarray(v, np.float32) for n, v in inputs.items()}
    wf32 = np.stack([pack_layer_weights(inp, l) for l in range(depth)])
    pvec = np.stack([pack_pvec(inp, l) for l in range(depth)])
    prow = np.stack([pack_prow(inp, l) for l in range(depth)])
    wgup = np.stack([pack_wgup(inp, l) for l in range(depth)])
    fnorm = np.ascontiguousarray(inp["final_norm"].reshape(KC, 128).T)
    in_maps = []
    for c in range(8):
        s = seqs[c % nseq]
        in_maps.append({"xin": np.ascontiguousarray(s.T), "wf32": wf32, "pvec": pvec, "prow": prow, "wgup": wgup,
                        "fnorm": fnorm})
    res = run_bass_kernel_spmd(nc, in_maps, core_ids=list(range(8)))
    outs = [np.ascontiguousarray(res.results[c]["yout"].T) for c in range(nseq)]
    yp = np.stack(outs[:xp.shape[0]]).astype(np.float32)
    ys = np.stack(outs[xp.shape[0]:]).astype(np.float32)
    return yp, ys


def kernel(**inputs):
    return run(inputs, 4)
```

```python
import numpy as np
from contextlib import ExitStack
import concourse.bass as bass
import concourse.mybir as mybir
from concourse.bass_utils import run_bass_kernel_spmd

F32 = mybir.dt.float32
BF16 = mybir.dt.bfloat16
AF = mybir.ActivationFunctionType
ALU = mybir.AluOpType
ENGS = ("tensor", "vector", "scalar", "gpsimd", "sync")

D = 1024
DFF = 2816
NFF = DFF // 128
KC = D // 128
EPS = 1e-6
TT = 512
SLOT = 2816


class DSem:
    __slots__ = ("dsem", "dcount")

    def __init__(self, sem):
        self.dsem = sem
        self.dcount = 0


class Buf:
    __slots__ = ("name", "w", "r", "ds", "excl")

    def __init__(self, name):
        self.name = name
        self.w = []
        self.r = []
        self.ds = {}
        self.excl = False


class MK:
    def __init__(self, nc, es):
        self.nc = nc
        self.es = es
        self.prog = {e: [] for e in ENGS}
        self.esem = {}
        self.ecount = {e: 0 for e in ENGS}
        self.waited = {e: {} for e in ENGS}
        for e in ENGS:
            self.esem[e] = es.enter_context(nc.semaphore("s_" + e))
        self.nbuf = 0
        self.ninst = 0
        self.reg = {}

    def buf(self, name=None):
        if name is not None and name in self.reg:
            return self.reg[name]
        self.nbuf += 1
        b = Buf(name or ("b_%d" % self.nbuf))
        self.reg[b.name] = b
        return b

    def barrier(self):
        for eng in ENGS:
            need = {}
            for o in ENGS:
                if self.ecount[o] > self.waited[eng].get(("e", o), 0):
                    need[("e", o)] = (self.esem[o], self.ecount[o])
            for b in self.reg.values():
                for q in b.ds.values():
                    if q.dcount > self.waited[eng].get(("d", id(q)), 0):
                        need[("d", id(q))] = (q.dsem, q.dcount)
            self._emit_waits(eng, need)

    def _dsem(self, b, kind):
        if kind not in b.ds:
            b.ds[kind] = DSem(self.es.enter_context(self.nc.semaphore("d%s_%s" % (kind, b.name))))
        return b.ds[kind]

    def _need(self, eng, reads, writes, acc=None):
        need = {}

        def add(ev):
            kind, ref, val = ev
            if kind == "e":
                key = ("e", ref)
                sem = self.esem[ref]
                v = val
            else:
                key = ("d", id(ref))
                sem = ref.dsem
                v = ref.dcount
            if self.waited[eng].get(key, 0) >= v:
                return
            cur = need.get(key)
            if cur is None or cur[1] < v:
                need[key] = (sem, v)

        for b in reads:
            for ev in b.w:
                add(ev)
        for b in writes:
            for ev in b.w:
                if acc is not None and b is acc and ev[0] == "e" and ev[1] == "tensor":
                    continue
                add(ev)
            for ev in b.r:
                add(ev)
        return need

    def _emit_waits(self, eng, need):
        for key, (sem, v) in need.items():
            self.waited[eng][key] = v
            self.prog[eng].append(lambda e, sem=sem, v=v: e.wait_ge(sem, v))

    def _record(self, ev, reads, writes):
        for b in writes:
            b.w = [ev]
            b.r = []
        for b in reads:
            if any(b is w for w in writes):
                continue
            b.r = [x for x in b.r if not (x[0] == ev[0] and x[1] is ev[1])] + [ev]

    def op(self, eng, fn, reads=(), writes=(), acc=None):
        xr = [b for b in reads if b.excl]
        if xr:
            reads = [b for b in reads if not b.excl]
            writes = list(writes) + [b for b in xr if not any(b is w for w in writes)]
        need = self._need(eng, reads, writes, acc)
        self._emit_waits(eng, need)
        self.ecount[eng] += 1
        sem = self.esem[eng]
        self.prog[eng].append(lambda e, fn=fn, sem=sem: fn(e).then_inc(sem, 1))
        ev = ("e", eng, self.ecount[eng])
        self._record(ev, reads, writes)
        self.ninst += 1

    def dma(self, eng, out, in_, reads=(), writes=(), prim=None, **kw):
        if prim is None:
            prim = writes[0] if writes else reads[0]
        need = self._need(eng, reads, writes)
        self._emit_waits(eng, need)
        q = self._dsem(prim, "sw" if eng == "gpsimd" else "hw")
        q.dcount += 16
        sem = q.dsem
        self.prog[eng].append(
            lambda e, out=out, in_=in_, sem=sem, kw=kw: e.dma_start(out=out, in_=in_, **kw).then_inc(sem, 16))
        ev = ("d", q, q.dcount)
        self._record(ev, reads, writes)
        self.ninst += 1

    def dma_batch(self, eng, items, reads=(), writes=(), prim=None):
        need = self._need(eng, reads, writes)
        self._emit_waits(eng, need)
        q = self._dsem(prim, "sw" if eng == "gpsimd" else "hw")
        sem = q.dsem
        for (out, in_) in items:
            q.dcount += 16
            self.prog[eng].append(
                lambda e, out=out, in_=in_, sem=sem: e.dma_start(out=out, in_=in_).then_inc(sem, 16))
            self.ninst += 1
        ev = ("d", q, q.dcount)
        self._record(ev, reads, writes)

    def wait_all(self, eng, bufs):
        need = self._need(eng, bufs, ())
        self._emit_waits(eng, need)

    def run_block(self):
        nc = self.nc
        with nc.Block() as block:
            @block.tensor
            def _(e):
                for f in self.prog["tensor"]:
                    f(e)

            @block.vector
            def _(e):
                for f in self.prog["vector"]:
                    f(e)

            @block.scalar
            def _(e):
                for f in self.prog["scalar"]:
                    f(e)

            @block.gpsimd
            def _(e):
                for f in self.prog["gpsimd"]:
                    f(e)

            @block.sync
            def _(e):
                for f in self.prog["sync"]:
                    f(e)


class Tile:
    __slots__ = ("t", "b")

    def __init__(self, t, b):
        self.t = t
        self.b = b


IN_OFF = {}
_o = 0
for _n, _s in (("a_q", 512), ("a_k", 512), ("a_v", 512), ("a_z", 512), ("a_b", 8), ("a_a", 8),
               ("b_q", 256), ("b_k", 256), ("b_v", 512), ("b_r", 512), ("b_g", 32),
               ("c_z", 512), ("c_x", 512), ("c_B", 256), ("c_C", 256), ("c_dt", 16), ("gate", 3072)):
    IN_OFF[_n] = (_o, _s)
    _o += _s
assert _o == 8256


def _unit(W, cols):
    K = W.shape[0]
    kc = K // 128
    cols = np.asarray(cols)
    sub = np.zeros((K, len(cols)), np.float32)
    ok = cols >= 0
    sub[:, ok] = W[:, cols[ok]]
    return np.ascontiguousarray(sub.reshape(kc, 128, len(cols)).transpose(1, 0, 2)).reshape(128, kc * len(cols))


def col_range(name, lo=0, hi=None):
    o, s = IN_OFF[name]
    if hi is None:
        hi = s
    return list(range(o + lo, o + hi))


def make_cfg():
    cfg = {}
    A_chunks = []
    for nm, n in (("a_q", 4), ("a_k", 4), ("a_v", 4), ("c_x", 4), ("c_B", 2), ("c_C", 2), ("b_q", 2), ("b_k", 2)):
        for c in range(n):
            A_chunks.append(col_range(nm, c * 128, (c + 1) * 128))
    A_units = [A_chunks[2 * i] + A_chunks[2 * i + 1] for i in range(len(A_chunks) // 2)]
    A_units.append(col_range("b_v", 0, 256))
    A_units.append(col_range("b_v", 256, 512))
    A_units.append(col_range("b_k"))
    A_units.append(col_range("b_g", 0, 16) + [-1] * 16 + col_range("b_g", 16, 32) + [-1] * 16
                   + col_range("a_b") + col_range("a_a") + col_range("c_dt"))
    cfg["A_units"] = A_units
    B_chunks = []
    for nm in ("a_z", "b_r", "c_z"):
        for c in range(4):
            B_chunks.append(col_range(nm, c * 128, (c + 1) * 128))
    for c in range(24):
        B_chunks.append(col_range("gate", c * 128, (c + 1) * 128))
    cfg["B_units"] = [B_chunks[2 * i] + B_chunks[2 * i + 1] for i in range(len(B_chunks) // 2)]
    return cfg


CFG = make_cfg()


def unit_plan(cfg):
    plan = []
    for f in range(2):
        for j in range(NFF):
            plan.append((("gu", f, j), KC * 256))
        for oc in range(KC):
            plan.append((("dn", f, oc), NFF * 128))
    for u in range(len(cfg["A_units"])):
        plan.append((("inA", u), KC * len(cfg["A_units"][u])))
    for u in range(len(cfg["B_units"])):
        plan.append((("inB", u), KC * len(cfg["B_units"][u])))
    for oc in range(KC):
        plan.append((("br", oc), 12 * 128))
    for u in range(4):
        plan.append((("wo", u), KC * 256))
    return plan


PLAN = unit_plan(CFG)
UOFF = {}
_o = 0
for _k, _s in PLAN:
    UOFF[_k] = (_o, _s)
    _o += _s
LAYER_W = _o


def pack_layer_weights(inp, l):
    out = np.empty((128, LAYER_W), np.float32)
    for key, size in PLAN:
        o, s = UOFF[key]
        if key[0] == "gu":
            _, f, j = key
            cols = list(range(j * 128, (j + 1) * 128))
            g = _unit(inp["ffn_w_gate"][l, f], cols).reshape(128, KC, 128)
            u = _unit(inp["ffn_w_up"][l, f], cols).reshape(128, KC, 128)
            blk = np.concatenate([g, u], axis=2).reshape(128, KC * 256)
        elif key[0] == "dn":
            _, f, oc = key
            blk = _unit(inp["ffn_w_down"][l, f], list(range(oc * 128, (oc + 1) * 128)))
        elif key[0] == "inA":
            blk = _unit(inp["w_in"][l], CFG["A_units"][key[1]])
        elif key[0] == "inB":
            blk = _unit(inp["w_in"][l], CFG["B_units"][key[1]])
        elif key[0] == "br":
            oc = key[1]
            parts = [_unit(inp["w_branch"][l, b], list(range(oc * 128, (oc + 1) * 128))).reshape(128, 4, 128)
                     for b in range(3)]
            blk = np.concatenate(parts, axis=1).reshape(128, 12 * 128)
        elif key[0] == "wo":
            u = key[1]
            blk = _unit(inp["w_out"][l], list(range(u * 256, (u + 1) * 256)))
        assert blk.shape == (128, s), (key, blk.shape, s)
        out[:, o:o + s] = blk
    return out


PV = {}
_o = 0
for _n, _s in (("ffn_norm0", 8), ("ffn_norm1", 8), ("mix_norm", 8), ("gdn_norm", 1), ("gla_norm", 1),
               ("ssd_norm", 4), ("ssd_d", 4), ("gdn_cw", 60), ("ssd_cw", 40), ("ssd_cb", 8)):
    PV[_n] = (_o, _s)
    _o += _s
NPV = _o

PR = {}
_o = 0
for _n, _s in (("gla_bg", 512), ("sm_bias", 32), ("sm_alog", 32)):
    PR[_n] = (_o, _s)
    _o += _s
NPR = _o


def pack_pvec(inp, l):
    out = np.zeros((128, NPV), np.float32)

    def put(name, vec):
        o, s = PV[name]
        out[:, o:o + s] = np.asarray(vec, np.float32).reshape(s, 128).T
    put("ffn_norm0", inp["ffn_norm"][l, 0])
    put("ffn_norm1", inp["ffn_norm"][l, 1])
    put("mix_norm", inp["mix_norm"][l])
    put("gdn_norm", inp["gdn_norm"][l])
    put("gla_norm", inp["gla_norm"][l])
    put("ssd_norm", inp["ssd_norm"][l])
    put("ssd_d", np.repeat(inp["ssd_d"][l], 64))
    o, s = PV["gdn_cw"]
    cw = inp["gdn_conv_w"][l]
    out[:, o:o + s] = cw.reshape(5, 12, 128).transpose(2, 1, 0).reshape(128, 60)
    o, s = PV["ssd_cw"]
    cw = inp["ssd_conv_w"][l]
    out[:, o:o + s] = cw.reshape(5, 8, 128).transpose(2, 1, 0).reshape(128, 40)
    put("ssd_cb", inp["ssd_conv_b"][l])
    return out


def pack_prow(inp, l):
    out = np.zeros((NPR,), np.float32)
    o, s = PR["gla_bg"]
    out[o:o + s] = inp["gla_b_g"][l].reshape(512)
    o, s = PR["sm_bias"]
    out[o + 8:o + 16] = inp["gdn_dt_bias"][l].reshape(8)
    out[o + 16:o + 32] = inp["ssd_dt_bias"][l].reshape(16)
    o, s = PR["sm_alog"]
    out[o + 8:o + 16] = inp["gdn_a_log"][l].reshape(8)
    out[o + 16:o + 32] = inp["ssd_a_log"][l].reshape(16)
    return out


def pack_wgup(inp, l):
    out = np.zeros((64, 256), np.float32)
    out[0:16] = inp["gla_w_gup"][l, 0]
    out[32:48] = inp["gla_w_gup"][l, 1]
    return out
T2 = 256
TE2 = T2 + 4
NSUB2 = T2 // 128


class Builder:
    def __init__(self, L, depth, mix=("gdn", "gla", "ssd")):
        self.L = L
        self.depth = depth
        self.mix = tuple(mix)
        self.NT = L // TT
        self.NT2 = L // T2
        self.NST = L // 128
        self.NCH = L // 64

    def sb(self, name, shape, dt):
        es = self.scope if self.scope is not None else self.es
        self.tcount = getattr(self, "tcount", 0) + 1
        t = es.enter_context(self.nc.sbuf_tensor("%s_%d" % (name, self.tcount), list(shape), dt))
        return Tile(t, self.k.buf(name))

    def begin_scope(self):
        self.scope = ExitStack()
        self.scope.__enter__()

    def end_scope(self):
        self.k.barrier()
        self.scope.__exit__(None, None, None)
        self.scope = None

    def dram(self, name, shape, dt):
        return self.nc.dram_tensor(name, list(shape), dt).ap()

    def build(self):
        nc = bass.Bass("TRN2", target_bir_lowering=False)
        self.nc = nc
        L, depth = self.L, self.depth
        self.xin = nc.dram_tensor("xin", [D, L], F32, kind="ExternalInput").ap()
        self.wf32 = nc.dram_tensor("wf32", [depth, 128, LAYER_W], F32, kind="ExternalInput").ap()
        self.pvec = nc.dram_tensor("pvec", [depth, 128, NPV], F32, kind="ExternalInput").ap()
        self.prow = nc.dram_tensor("prow", [depth, NPR], F32, kind="ExternalInput").ap()
        self.wgup = nc.dram_tensor("wgup", [depth, 64, 256], F32, kind="ExternalInput").ap()
        self.fnorm = nc.dram_tensor("fnorm", [128, KC], F32, kind="ExternalInput").ap()
        self.yout = nc.dram_tensor("yout", [D, L], F32, kind="ExternalOutput").ap()
        self.wbf = self.dram("wbf", [depth, 128, LAYER_W], BF16)
        self.xres = self.dram("xres", [D, L], F32)
        NST = self.NST
        if "gla" in self.mix:
            self.gla_qg = self.dram("gla_qg", [2, 256, L], BF16)
            self.gla_kmg = self.dram("gla_kmg", [2, 256, L], BF16)
            self.gla_kd = self.dram("gla_kd", [2, L, 256], BF16)
            self.gla_v = self.dram("gla_v", [L, 512], BF16)
            self.gla_o = self.dram("gla_o", [2, 512, L], F32)
        if "ssd" in self.mix:
            self.ssd_cexp = self.dram("ssd_cexp", [2, 8 * 128, L], BF16)
            self.ssd_B = self.dram("ssd_B", [L, 256], BF16)
            self.ssd_xw = self.dram("ssd_xw", [2, L, 512], BF16)
            self.ssd_ydiag = self.dram("ssd_ydiag", [512, L], F32)
            self.ssd_yoff = self.dram("ssd_yoff", [2, 512, L], F32)
        if "gdn" in self.mix:
            self.gdn_wn = self.dram("gdn_wn", [2, 512, L], BF16)
            self.gdn_u = self.dram("gdn_u", [2, L, 512], F32)
            self.gdn_qg = self.dram("gdn_qg", [2, 512, L], BF16)
            self.gdn_kg = self.dram("gdn_kg", [2, L, 512], BF16)
            self.gdn_att = self.dram("gdn_att", [2, NST * 128, 512], BF16)
            self.gdn_o = self.dram("gdn_o", [2, 512, L], F32)
        with ExitStack() as es:
            k = MK(nc, es)
            self.k = k
            self.es = es
            self.scope = None
            self.setup_common()
            self.cast_weights()
            for l in range(depth + 1):
                self.pass1(l)
                if l < depth and self.mix:
                    self.pass2(l)
                    self.scan(l)
            k.wait_all("gpsimd", self.out_bufs)
            k.run_block()
        return nc

    def setup_common(self):
        k = self.k
        self.out_bufs = []
        self.ones_bf = self.sb("ones_bf", [128, 128], BF16)
        k.op("gpsimd", lambda e: e.memset(self.ones_bf.t[:], 1.0), writes=[self.ones_bf.b])
        self.epsb = self.sb("epsb", [128, 1], F32)
        k.op("gpsimd", lambda e: e.memset(self.epsb.t[:], EPS), writes=[self.epsb.b])
        self.oneb = self.sb("oneb", [128, 1], F32)
        k.op("gpsimd", lambda e: e.memset(self.oneb.t[:], 1.0), writes=[self.oneb.b])
        self.U = [self.sb("Ublk%d" % d, [128, 128], F32) for d in range(2)]
        self.SU = [self.sb("SUblk%d" % d, [128, 128], F32) for d in range(2)]
        self.ident = self.sb("ident", [128, 128], F32)
        self.ident_bf = self.sb("ident_bf", [128, 128], BF16)
        self.bones = self.sb("bones", [128, 128], F32)

        def tri(tile, ge, strict):
            k.op("gpsimd", lambda e: e.memset(tile.t[:], 0.0), writes=[tile.b])
            for b in range(2):
                blk = tile.t[b * 64:(b + 1) * 64, b * 64:(b + 1) * 64]
                k.op("gpsimd", lambda e, blk=blk: e.memset(blk, 1.0), writes=[tile.b])
                sgn = 1 if ge else -1
                base = -1 if strict else 0
                k.op("gpsimd", lambda e, blk=blk, sgn=sgn, base=base: e.affine_select(
                    out=blk, in_=blk, pattern=[[sgn, 64]], compare_op=ALU.is_ge, fill=0.0, base=base,
                    channel_multiplier=-sgn), reads=[tile.b], writes=[tile.b])
        tri(self.U[0], True, False)
        tri(self.U[1], False, False)
        tri(self.SU[0], False, True)
        tri(self.SU[1], True, True)
        k.op("gpsimd", lambda e: e.memset(self.bones.t[:], 0.0), writes=[self.bones.b])
        for b in range(2):
            blk = self.bones.t[b * 64:(b + 1) * 64, b * 64:(b + 1) * 64]
            k.op("gpsimd", lambda e, blk=blk: e.memset(blk, 1.0), writes=[self.bones.b])
        k.op("gpsimd", lambda e: e.memset(self.ident.t[:], 1.0), writes=[self.ident.b])
        k.op("gpsimd", lambda e: e.affine_select(out=self.ident.t[:], in_=self.ident.t[:], pattern=[[-1, 128]],
                                                 compare_op=ALU.is_equal, fill=0.0, base=0, channel_multiplier=1),
             reads=[self.ident.b], writes=[self.ident.b])
        k.op("gpsimd", lambda e: e.tensor_copy(out=self.ident_bf.t[:], in_=self.ident.t[:]),
             reads=[self.ident.b], writes=[self.ident_bf.b])
        self.cind = [self.sb("cind%d" % c, [128, 128], F32) for c in range(2)]
        for c in range(2):
            k.op("gpsimd", lambda e, c=c: e.memset(self.cind[c].t[:], 0.0), writes=[self.cind[c].b])
            k.op("gpsimd", lambda e, c=c: e.memset(self.cind[c].t[c * 64:(c + 1) * 64, :], 1.0), writes=[self.cind[c].b])
        self.maskneg = [self.sb("maskneg%d" % d, [128, 128], F32) for d in range(2)]
        for d in range(2):
            k.op("vector", lambda e, d=d: e.tensor_scalar(out=self.maskneg[d].t[:], in0=self.U[d].t[:], scalar1=30000.0,
                                                          scalar2=-30000.0, op0=ALU.mult, op1=ALU.add),
                 reads=[self.U[d].b], writes=[self.maskneg[d].b])
        self.NSLOT = 8
        self.wslots = [self.sb("wslot%d" % i, [128, SLOT], BF16) for i in range(self.NSLOT)]
        self.wnext = 0
        self.NPS = 7
        self.pbanks = []
        for i in range(self.NPS):
            t = self.es.enter_context(self.nc.psum_tensor("pb%d" % i, [128, TT], F32))
            self.pbanks.append(Tile(t, k.buf("pb%d" % i)))
            self.pbanks[-1].b.excl = True
        self.pnext = 0
        t = self.es.enter_context(self.nc.psum_tensor("ptr", [128, 1024], BF16))
        self.ptr = Tile(t, k.buf("ptr"))
        self.ptr.b.excl = True
        self.pv = self.sb("pv", [128, self.depth, NPV], F32)
        self.fn = self.sb("fn", [128, KC], F32)
        self.fn.b = self.pv.b
        k.dma_batch("sync", [(self.pv.t[:, l, :], self.pvec[l]) for l in range(self.depth)] + [(self.fn.t[:], self.fnorm)],
                    writes=[self.pv.b], prim=self.pv.b)
        NCH = self.NCH
        if "gla" in self.mix:
            self.egl_gla = self.sb("egl_gla", [128, 2, 2, NCH], F32)
        if "ssd" in self.mix or "gdn" in self.mix:
            self.egl = self.sb("egl", [128, NCH, 24], F32)
        self.pe_bufs = [k.buf("pe%d" % i) for i in range(32)]
        self.pe_n = 0

    def pvc(self, l, name, i=0, n=1):
        o, s = PV[name]
        return self.pv.t[:, l, o + i:o + i + n]

    def cast_weights(self):
        k = self.k
        CH = 65536
        self.cast_chunks = []
        for l in range(self.depth):
            lo = 0
            lb = k.buf("wcL%d" % l)
            while lo < LAYER_W:
                hi = min(LAYER_W, lo + CH)
                b = k.buf("wc0_%d" % lo) if l == 0 else lb
                k.dma("gpsimd", self.wbf[l, :, lo:hi], self.wf32[l, :, lo:hi], writes=[b], max_dma_last_dim=4096)
                self.cast_chunks.append((l, lo, hi, b))
                lo = hi

    def cast_deps(self, l, o, s):
        out = []
        for (ll, lo, hi, b) in self.cast_chunks:
            if ll == l and lo < o + s and hi > o and b not in out:
                out.append(b)
        return out

    def load_unit(self, l, key):
        k = self.k
        o, s = UOFF[key]
        slot = self.wslots[self.wnext]
        self.wnext = (self.wnext + 1) % self.NSLOT
        k.dma("sync", slot.t[:, 0:s], self.wbf[l, :, o:o + s], reads=self.cast_deps(l, o, s), writes=[slot.b],
              prim=slot.b)
        return slot

    def pbank(self):
        p = self.pbanks[self.pnext]
        self.pnext = (self.pnext + 1) % self.NPS
        return p

    def mm(self, ps, out_ap, lh, rh, reads, start=True, stop=True, acc=False):
        self.k.op("tensor", lambda e: e.matmul(out_ap, lhsT=lh, rhs=rh, start=start, stop=stop),
                  reads=reads, writes=[ps.b], acc=(ps.b if acc else None))

    def mm_group(self, ps, out_ap, pairs, reads):
        n = len(pairs)
        for i, (lh, rh) in enumerate(pairs):
            self.mm(ps, out_ap, lh, rh, reads, start=(i == 0), stop=(i == n - 1), acc=(i > 0))

    def ts(self, eng, out, in0, s1, s2, op0, op1, reads, writes):
        if op1 is None:
            self.k.op(eng, lambda e: e.tensor_scalar(out=out, in0=in0, scalar1=s1, scalar2=None, op0=op0),
                      reads=reads, writes=writes)
        else:
            self.k.op(eng, lambda e: e.tensor_scalar(out=out, in0=in0, scalar1=s1, scalar2=s2, op0=op0, op1=op1),
                      reads=reads, writes=writes)

    def ms(self, ap, val, writes):
        self.k.op("gpsimd", lambda e: e.memset(ap, val), reads=(), writes=writes)

    def tr(self, out, in_, reads, ps):
        idt = self.ident_bf
        self.k.op("tensor", lambda e: e.transpose(out, in_, idt.t[:]), reads=list(reads) + [idt.b], writes=[ps.b])

    def act(self, out, in_, func, reads, writes, **kw):
        self.k.op("scalar", lambda e: e.activation(out=out, in_=in_, func=func, **kw), reads=reads, writes=writes)

    def tt(self, eng, out, in0, in1, op, reads, writes):
        self.k.op(eng, lambda e: e.tensor_tensor(out=out, in0=in0, in1=in1, op=op), reads=reads, writes=writes)

    def stt(self, out, in0, scalar, in1, op0, op1, reads, writes):
        self.k.op("vector", lambda e: e.scalar_tensor_tensor(out=out, in0=in0, scalar=scalar, in1=in1, op0=op0, op1=op1),
                  reads=reads, writes=writes)

    def cp(self, eng, out, in_, reads, writes):
        if eng == "scalar":
            self.k.op(eng, lambda e: e.copy(out=out, in_=in_), reads=reads, writes=writes)
        else:
            self.k.op(eng, lambda e: e.tensor_copy(out=out, in_=in_), reads=reads, writes=writes)

    def rstd_from_ss(self, ps_ap, psb, out, n, cols=None):
        self.act(out.t[:] if cols is None else cols, ps_ap, AF.Ln, [psb, self.epsb.b], [out.b], scale=1.0 / n,
                 bias=self.epsb.t[:, 0:1])
        o = out.t[:] if cols is None else cols
        self.act(o, o, AF.Exp, [out.b], [out.b], scale=-0.5)

    def rmsnorm(self, x, gcol, out_h, W, sq, rstd):
        k = self.k
        self.act(sq.t[:, :, 0:W], x.t[:, :, 0:W], AF.Square, [x.b], [sq.b])
        for (lo, hi) in ((0, min(W, 512)), (512, W)):
            if hi <= lo:
                continue
            ps = self.pbank()
            self.mm_group(ps, ps.t[:, 0:hi - lo], [(self.ones_bf.t[:], sq.t[:, kc, lo:hi]) for kc in range(KC)],
                          reads=[self.ones_bf.b, sq.b])
            self.rstd_from_ss(ps.t[:, 0:hi - lo], ps.b, rstd, D, cols=rstd.t[:, lo:hi])
        for kc in range(KC):
            self.stt(out_h.t[:, kc, 0:W], x.t[:, kc, 0:W], gcol[:, kc:kc + 1], rstd.t[:, 0:W], ALU.mult, ALU.mult,
                     [x.b, rstd.b, self.pv.b, self.fn.b], [out_h.b])

    def ffn(self, l, f, x):
        k = self.k
        o, s = PV["ffn_norm%d" % f]
        self.rmsnorm(x, self.pv.t[:, l, o:o + s], self.hT, TT, self.sq, self.rstd)
        hT, hid = self.hT, self.hid
        for j in range(NFF):
            w = self.load_unit(l, ("gu", f, j))
            wv = w.t[:, 0:KC * 256].rearrange("p (k c) -> p k c", k=KC)
            pg, pu = self.pbank(), self.pbank()
            self.mm_group(pg, pg.t[:], [(wv[:, kc, 0:128], hT.t[:, kc, :]) for kc in range(KC)], reads=[w.b, hT.b])
            self.mm_group(pu, pu.t[:], [(wv[:, kc, 128:256], hT.t[:, kc, :]) for kc in range(KC)], reads=[w.b, hT.b])
            sg = self.sg[j % 2]
            self.act(sg.t[:], pg.t[:], AF.Silu, [pg.b], [sg.b])
            self.tt("vector", hid.t[:, j, :], sg.t[:], pu.t[:], ALU.mult, [sg.b, pu.b], [hid.b])
        for oc in range(KC):
            w = self.load_unit(l, ("dn", f, oc))
            wv = w.t[:, 0:NFF * 128].rearrange("p (k c) -> p k c", k=NFF)
            po = self.pbank()
            self.mm_group(po, po.t[:], [(wv[:, j, :], hid.t[:, j, :]) for j in range(NFF)], reads=[w.b, hid.b])
            self.stt(x.t[:, oc, :], po.t[:], 0.5, x.t[:, oc, :], ALU.mult, ALU.add, [po.b, x.b], [x.b])

    def pass1(self, l):
        k = self.k
        first = (l == 0)
        last = (l == self.depth)
        self.begin_scope()
        self.xT = [self.sb("xT%d" % i, [128, KC, TT], F32) for i in range(2)]
        self.hT = self.sb("hT", [128, KC, TT], BF16)
        self.sq = self.sb("sq", [128, KC, TT], BF16)
        self.rstd = self.sb("rstd", [128, TT], F32)
        self.hid = self.sb("hid", [128, 24, TT], BF16)
        self.sg = [self.sb("sg%d" % i, [128, TT], F32) for i in range(2)]
        if not first and self.mix:
            self.alloc_B()
        src = self.xin if first else self.xres
        for t in range(self.NT):
            x = self.xT[t % 2]
            t0 = t * TT
            k.dma("sync", x.t[:], src[:, t0:t0 + TT].rearrange("(k p) t -> p k t", p=128), writes=[x.b])
            if not first:
                if self.mix:
                    self.phase_B(l - 1, t, x)
                self.ffn(l - 1, 1, x)
            if not last:
                self.ffn(l, 0, x)
                k.dma("gpsimd", self.xres[:, t0:t0 + TT].rearrange("(k p) t -> p k t", p=128), x.t[:],
                      reads=[x.b], prim=x.b)
            else:
                o, s = 0, KC
                self.final_norm(x)
                k.dma("gpsimd", self.yout[:, t0:t0 + TT].rearrange("(k p) t -> p k t", p=128), x.t[:],
                      reads=[x.b], prim=x.b)
                if x.b not in self.out_bufs:
                    self.out_bufs.append(x.b)
        self.end_scope()

    def final_norm(self, x):
        sq, rstd = self.sq, self.rstd
        self.act(sq.t[:], x.t[:], AF.Square, [x.b], [sq.b])
        ps = self.pbank()
        self.mm_group(ps, ps.t[:], [(self.ones_bf.t[:], sq.t[:, kc, :]) for kc in range(KC)],
                      reads=[self.ones_bf.b, sq.b])
        self.rstd_from_ss(ps.t[:], ps.b, rstd, D)
        for kc in range(KC):
            self.stt(x.t[:, kc, :], x.t[:, kc, :], self.fn.t[:, kc:kc + 1], rstd.t[:], ALU.mult, ALU.mult,
                     [x.b, rstd.b, self.fn.b], [x.b])

    def alloc_B(self):
        onv = self.hid.t[:].rearrange("p a b -> p (a b)").bitcast(F32).rearrange("p (a b) -> p a b", a=12)
        self.on = Tile(None, self.hid.b)
        self.on_v = onv
        self.ybr = self.sb("ybr", [128, 12, TT], BF16)
        self.ld = [self.sb("ldB%d" % i, [128, TT], F32) for i in range(4)]
        self.ldn = 0
        self.sqb = self.sb("sqb", [128, 4, TT], BF16)
        self.rsb = self.sb("rsb", [128, TT], F32)
        self.rsb4 = self.sb("rsb4", [128, 4, TT], F32)
        self.mg = [self.sb("mg%d" % i, [128, TT], F32) for i in range(2)]
        self.mtmp = [self.sb("mtmp%d" % i, [128, TT], F32) for i in range(2)]
        self.th = [self.sb("th%d" % i, [128, TT], F32) for i in range(3)]
        self.thn = 0
        self.merged = self.sq

    def ldB(self, src_ap):
        t = self.ld[self.ldn % 4]
        self.ldn += 1
        self.k.dma("sync", t.t[:], src_ap, writes=[t.b])
        return t

    def phase_B(self, l, t, x):
        k = self.k
        t0 = t * TT
        ts = slice(t0, t0 + TT)
        on, ybr, hT = self.on, self.ybr, self.hT
        onv = self.on_v
        self.rmsnorm(x, self.pvc(l, "mix_norm", 0, 8), hT, TT, self.sq, self.rstd)
        if "ssd" in self.mix:
            for u in range(2):
                w = self.load_unit(l, ("inB", 4 + u))
                wv = w.t[:, 0:KC * 256].rearrange("p (k c) -> p k c", k=KC)
                for cc in range(2):
                    c = 2 * u + cc
                    a = self.ldB(self.ssd_ydiag[c * 128:(c + 1) * 128, ts])
                    b = self.ldB(self.ssd_yoff[0, c * 128:(c + 1) * 128, ts])
                    d = self.ldB(self.ssd_yoff[1, c * 128:(c + 1) * 128, ts])
                    self.tt("gpsimd", a.t[:], a.t[:], b.t[:], ALU.add, [a.b, b.b], [a.b])
                    self.tt("gpsimd", a.t[:], a.t[:], d.t[:], ALU.add, [a.b, d.b], [a.b])
                    pz = self.pbank()
                    self.mm_group(pz, pz.t[:], [(wv[:, kc, cc * 128:(cc + 1) * 128], hT.t[:, kc, :]) for kc in range(KC)],
                                  reads=[w.b, hT.b])
                    sz = self.th[self.thn % 3]
                    self.thn += 1
                    self.act(sz.t[:], pz.t[:], AF.Silu, [pz.b], [sz.b])
                    self.tt("vector", onv[:, 8 + c, :], a.t[:], sz.t[:], ALU.mult, [a.b, sz.b], [on.b])
        for m, base, src in (("gdn", 0, getattr(self, "gdn_o", None)), ("gla", 4, getattr(self, "gla_o", None))):
            if m not in self.mix:
                continue
            for h in range(4):
                a = self.ldB(src[0, h * 128:(h + 1) * 128, ts])
                b = self.ldB(src[1, h * 128:(h + 1) * 128, ts])
                self.tt("vector" if h % 2 else "gpsimd", onv[:, base + h, :], a.t[:], b.t[:], ALU.add, [a.b, b.b], [on.b])
            self.act(self.sqb.t[:], onv[:, base:base + 4, :], AF.Square, [on.b], [self.sqb.b])
            pss = []
            for h in range(4):
                ps = self.pbank()
                self.mm(ps, ps.t[:], self.ones_bf.t[:], self.sqb.t[:, h, :], [self.ones_bf.b, self.sqb.b])
                pss.append(ps)
            for h in range(4):
                self.act(self.rsb4.t[:, h, :], pss[h].t[:], AF.Ln, [pss[h].b, self.epsb.b], [self.rsb4.b], scale=1.0 / 128,
                         bias=self.epsb.t[:, 0:1])
            self.act(self.rsb4.t[:], self.rsb4.t[:], AF.Exp, [self.rsb4.b], [self.rsb4.b], scale=-0.5)
            for h in range(4):
                self.stt(onv[:, base + h, :], onv[:, base + h, :], self.pvc(l, m + "_norm"), self.rsb4.t[:, h, :], ALU.mult,
                         ALU.mult, [on.b, self.rsb4.b, self.pv.b], [on.b])
        if "ssd" in self.mix:
            for g in range(2):
                self.act(self.sqb.t[:, 0:2, :], onv[:, 8 + 2 * g:10 + 2 * g, :], AF.Square, [on.b], [self.sqb.b])
                ps = self.pbank()
                self.mm_group(ps, ps.t[:], [(self.ones_bf.t[:], self.sqb.t[:, cc, :]) for cc in range(2)],
                              reads=[self.ones_bf.b, self.sqb.b])
                self.rstd_from_ss(ps.t[:], ps.b, self.rsb, 256)
                for cc in range(2):
                    c = 2 * g + cc
                    self.stt(ybr.t[:, 8 + c, :], onv[:, 8 + c, :], self.pvc(l, "ssd_norm", c), self.rsb.t[:],
                             ALU.mult, ALU.mult, [on.b, self.rsb.b, self.pv.b], [ybr.b])
        for m, base, u0 in (("gdn", 0, 0), ("gla", 4, 2)):
            if m not in self.mix:
                continue
            for u in range(2):
                w = self.load_unit(l, ("inB", u0 + u))
                wv = w.t[:, 0:KC * 256].rearrange("p (k c) -> p k c", k=KC)
                for cc in range(2):
                    h = 2 * u + cc
                    pz = self.pbank()
                    self.mm_group(pz, pz.t[:], [(wv[:, kc, cc * 128:(cc + 1) * 128], hT.t[:, kc, :]) for kc in range(KC)],
                                  reads=[w.b, hT.b])
                    sz = self.th[self.thn % 3]
                    self.thn += 1
                    self.act(sz.t[:], pz.t[:], AF.Silu, [pz.b], [sz.b])
                    self.tt("vector", ybr.t[:, base + h, :], onv[:, base + h, :], sz.t[:], ALU.mult, [on.b, sz.b], [ybr.b])
        mixl = [i for i, m in enumerate(("gdn", "gla", "ssd")) if m in self.mix]
        for op_ in range(4):
            wbs = [self.load_unit(l, ("br", 2 * op_ + cc)) for cc in range(2)]
            for bi, b in enumerate(mixl):
                wg_ = self.load_unit(l, ("inB", 6 + b * 4 + op_))
                wgv = wg_.t[:, 0:KC * 256].rearrange("p (k c) -> p k c", k=KC)
                for cc in range(2):
                    wbv = wbs[cc].t[:, 0:12 * 128].rearrange("p (k c) -> p k c", k=12)
                    pgt = self.pbank()
                    self.mm_group(pgt, pgt.t[:], [(wgv[:, kc, cc * 128:(cc + 1) * 128], hT.t[:, kc, :]) for kc in range(KC)],
                                  reads=[wg_.b, hT.b])
                    th = self.th[self.thn % 3]
                    self.thn += 1
                    self.act(th.t[:], pgt.t[:], AF.Tanh, [pgt.b], [th.b], scale=0.5)
                    pbr = self.pbank()
                    self.mm_group(pbr, pbr.t[:], [(wbv[:, b * 4 + kk, :], ybr.t[:, b * 4 + kk, :]) for kk in range(4)],
                                  reads=[wbs[cc].b, ybr.b])
                    if bi == 0:
                        self.stt(self.mg[cc].t[:], th.t[:], 1.0, pbr.t[:], ALU.add, ALU.mult, [th.b, pbr.b], [self.mg[cc].b])
                    else:
                        tmp = self.mtmp[cc]
                        self.stt(tmp.t[:], th.t[:], 1.0, pbr.t[:], ALU.add, ALU.mult, [th.b, pbr.b], [tmp.b])
                        self.tt("gpsimd", self.mg[cc].t[:], self.mg[cc].t[:], tmp.t[:], ALU.add,
                                [self.mg[cc].b, tmp.b], [self.mg[cc].b])
            for cc in range(2):
                oc = 2 * op_ + cc
                self.act(self.merged.t[:, oc, :], self.mg[cc].t[:], AF.Copy, [self.mg[cc].b], [self.merged.b], scale=0.5)
        for u in range(4):
            w = self.load_unit(l, ("wo", u))
            wv = w.t[:, 0:KC * 256].rearrange("p (k c) -> p k c", k=KC)
            for cc in range(2):
                oc = 2 * u + cc
                po = self.pbank()
                self.mm_group(po, po.t[:], [(wv[:, kc, cc * 128:(cc + 1) * 128], self.merged.t[:, kc, :]) for kc in range(KC)],
                              reads=[w.b, self.merged.b])
                self.tt("vector", x.t[:, oc, :], x.t[:, oc, :], po.t[:], ALU.add, [x.b, po.b], [x.b])

    def pass2(self, l):
        k = self.k
        L = self.L
        self.begin_scope()
        self.xE = self.sb("xE", [128, KC, TE2], F32)
        self.hE = self.sb("hE", [128, KC, TE2], BF16)
        self.sqE = self.sb("sqE", [128, KC, TE2], BF16)
        self.rstdE = self.sb("rstdE", [128, TE2], F32)
        self.prb = self.sb("prb", [128, NPR], F32)
        k.dma("sync", self.prb.t[:], self.prow[l:l + 1, :].to_broadcast([128, NPR]), writes=[self.prb.b])
        self.lrT = self.sb("lrT", [64, T2], BF16)
        self.sm = self.sb("sm", [128, NSUB2, 32], F32)
        if "gla" in self.mix:
            self.alloc_p2_gla(l)
        if "ssd" in self.mix:
            self.alloc_p2_ssd(l)
        if "gdn" in self.mix:
            self.alloc_p2_gdn(l)
        xE = self.xE
        self.hEs = [self.hE, self.sb("hE2", [128, KC, TE2], BF16)]
        self.pnext = 0
        if "ssd" in self.mix or "gdn" in self.mix:
            self.alloc_p2_common(l)

        def front(t):
            hE_ = self.hEs[t % 2]
            t0 = t * T2
            lo, hi = t0 - 2, t0 + T2 + 2
            j0, j1 = 0, TE2
            if lo < 0:
                self.ms(xE.t[:, :, 0:2], 0.0, [xE.b])
                j0, lo = 2, 0
            if hi > L:
                self.ms(xE.t[:, :, TE2 - 2:TE2], 0.0, [xE.b])
                j1, hi = TE2 - 2, L
            k.dma("sync", xE.t[:, :, j0:j1], self.xres[:, lo:hi].rearrange("(k p) t -> p k t", p=128), writes=[xE.b])
            self.rmsnorm(xE, self.pvc(l, "mix_norm", 0, 8), hE_, TE2, self.sqE, self.rstdE)

        front(0)
        for t in range(self.NT2):
            t0 = t * T2
            hE = self.hEs[t % 2]
            self.hE = hE
            w = self.load_unit(l, ("inA", 15))
            wv = w.t[:, 0:KC * 96].rearrange("p (k c) -> p k c", k=KC)
            ps = self.pbank()
            self.mm_group(ps, ps.t[0:64, 0:T2], [(wv[:, kc, 0:64], hE.t[:, kc, 2:2 + T2]) for kc in range(KC)], reads=[w.b, hE.b])
            self.cp("scalar", self.lrT.t[:], ps.t[0:64, 0:T2], [ps.b], [self.lrT.b])
            ps = self.pbank()
            for s in range(NSUB2):
                self.mm_group(ps, ps.t[:, s * 32:(s + 1) * 32],
                              [(hE.t[:, kc, 2 + s * 128:2 + (s + 1) * 128], wv[:, kc, 64:96]) for kc in range(KC)],
                              reads=[w.b, hE.b])
            o, _ = PR["sm_bias"]
            self.tt("vector", self.sm.t[:], ps.t[:, 0:NSUB2 * 32].rearrange("p (s c) -> p s c", s=NSUB2),
                    self.prb.t[:, None, o:o + 32].to_broadcast([128, NSUB2, 32]), ALU.add, [ps.b, self.prb.b], [self.sm.b])
            if t + 1 < self.NT2:
                front(t + 1)
            if "ssd" in self.mix or "gdn" in self.mix:
                self.p2_smalls_post(l, t)
            if "gla" in self.mix:
                self.p2_gla(l, t)
            if "ssd" in self.mix:
                self.p2_ssd(l, t)
            if "gdn" in self.mix:
                self.p2_gdn(l, t)
        self.NPS = 7
        self.end_scope()

    def alloc_p2_gla(self, l):
        k = self.k
        self.g_qT = self.sb("g_qT", [128, 2, T2], BF16)
        self.g_kT = self.sb("g_kT", [128, 2, T2], BF16)
        self.g_ktm = self.sb("g_ktm", [128, NSUB2, 256], F32)
        self.g_v = self.sb("g_v", [128, NSUB2, 512], BF16)
        self.g_qg = [self.sb("g_qg%d" % d, [128, 2, T2], BF16) for d in range(2)]
        self.g_kmg = [self.sb("g_kmg%d" % d, [128, 2, T2], BF16) for d in range(2)]
        self.g_kd = [self.sb("g_kd%d" % d, [128, NSUB2, 256], BF16) for d in range(2)]
        self.g_l = self.sb("g_l", [128, 256], F32)
        self.g_eg = [self.sb("g_eg%d" % i, [128, 128], F32) for i in range(2)]
        self.g_emg = [self.sb("g_emg%d" % i, [128, 128], F32) for i in range(2)]
        self.g_ed = self.sb("g_ed", [128, 256], F32)
        self.wg32 = self.sb("wg32", [64, 256], F32)
        self.wgb = self.sb("wgb", [64, 256], BF16)
        k.dma("sync", self.wg32.t[:], self.wgup[l], writes=[self.wg32.b])
        self.cp("vector", self.wgb.t[:], self.wg32.t[:], [self.wg32.b], [self.wgb.b])

    def p2_gla(self, l, t):
        k = self.k
        hE = self.hE
        t0 = t * T2
        for ui, dst in ((10, self.g_qT), (11, self.g_kT)):
            w = self.load_unit(l, ("inA", ui))
            wv = w.t[:, 0:KC * 256].rearrange("p (k c) -> p k c", k=KC)
            for c in range(2):
                ps = self.pbank()
                self.mm_group(ps, ps.t[:, 0:T2], [(wv[:, kc, c * 128:(c + 1) * 128], hE.t[:, kc, 2:2 + T2]) for kc in range(KC)],
                              reads=[w.b, hE.b])
                self.cp("scalar", dst.t[:, c, :], ps.t[:, 0:T2], [ps.b], [dst.b])
        w0 = self.load_unit(l, ("inA", 12))
        w1 = self.load_unit(l, ("inA", 13))
        w2 = self.load_unit(l, ("inA", 14))
        wv0 = w0.t[:, 0:KC * 256].rearrange("p (k c) -> p k c", k=KC)
        wv1 = w1.t[:, 0:KC * 256].rearrange("p (k c) -> p k c", k=KC)
        wv2 = w2.t[:, 0:KC * 256].rearrange("p (k c) -> p k c", k=KC)
        for s in range(NSUB2):
            hs = [hE.t[:, kc, 2 + s * 128:2 + (s + 1) * 128] for kc in range(KC)]
            ps = self.pbank()
            self.mm_group(ps, ps.t[:, 0:256], [(hs[kc], wv0[:, kc, :]) for kc in range(KC)], reads=[w0.b, hE.b])
            self.mm_group(ps, ps.t[:, 256:512], [(hs[kc], wv1[:, kc, :]) for kc in range(KC)], reads=[w1.b, hE.b])
            self.cp("scalar", self.g_v.t[:, s, :], ps.t[:], [ps.b], [self.g_v.b])
            ps = self.pbank()
            self.mm_group(ps, ps.t[:, 0:256], [(hs[kc], wv2[:, kc, :]) for kc in range(KC)], reads=[w2.b, hE.b])
            self.cp("vector", self.g_ktm.t[:, s, :], ps.t[:, 0:256], [ps.b], [self.g_ktm.b])
        ob, _ = PR["gla_bg"]
        for s in range(NSUB2):
            ss = slice(s * 128, (s + 1) * 128)
            ch0 = (t0 + s * 128) // 64
            for d in range(2):
                ps = self.pbank()
                self.mm(ps, ps.t[:, 0:256], self.lrT.t[d * 32:d * 32 + 16, ss], self.wgb.t[d * 32:d * 32 + 16, :],
                        [self.lrT.b, self.wgb.b])
                gl = self.g_l
                self.tt("vector", gl.t[:], ps.t[:, 0:256], self.prb.t[:, ob + d * 256:ob + (d + 1) * 256], ALU.add,
                        [ps.b, self.prb.b], [gl.b])
                self.act(gl.t[:], gl.t[:], AF.Exp, [gl.b], [gl.b], scale=-1.0)
                self.act(gl.t[:], gl.t[:], AF.Ln, [gl.b, self.oneb.b], [gl.b], bias=self.oneb.t[:, 0:1])
                for c in range(2):
                    ps2 = self.pbank()
                    self.mm(ps2, ps2.t[:, 0:128], gl.t[:, c * 128:(c + 1) * 128], self.U[d].t[:], [gl.b, self.U[d].b])
                    eg, emg = self.g_eg[c], self.g_emg[c]
                    self.act(eg.t[:], ps2.t[:, 0:128], AF.Exp, [ps2.b], [eg.b], scale=-1.0 / 16)
                    self.act(emg.t[:], ps2.t[:, 0:128], AF.Exp, [ps2.b], [emg.b], scale=1.0 / 16)
                    self.stt(self.g_qg[d].t[:, c, ss], self.g_qT.t[:, c, ss], 0.125, eg.t[:], ALU.mult, ALU.mult,
                             [self.g_qT.b, eg.b], [self.g_qg[d].b])
                    self.tt("gpsimd", self.g_kmg[d].t[:, c, ss], self.g_kT.t[:, c, ss], emg.t[:], ALU.mult,
                            [self.g_kT.b, emg.b], [self.g_kmg[d].b])
                    src = eg.t[:, 63::64] if d == 0 else eg.t[:, 0::64]
                    self.cp("gpsimd", self.egl_gla.t[:, c, d, ch0:ch0 + 2], src, [eg.b], [self.egl_gla.b])
                ps3 = self.pbank()
                self.mm(ps3, ps3.t[:, 0:256], self.SU[d].t[:], gl.t[:], [gl.b, self.SU[d].b])
                self.act(self.g_ed.t[:], ps3.t[:, 0:256], AF.Exp, [ps3.b], [self.g_ed.b], scale=-1.0 / 16)
                self.tt("vector", self.g_kd[d].t[:, s, :], self.g_ktm.t[:, s, :], self.g_ed.t[:], ALU.mult,
                        [self.g_ktm.b, self.g_ed.b], [self.g_kd[d].b])
        ts = slice(t0, t0 + T2)
        items, rd = [], []
        for d in range(2):
            items.append((self.gla_qg[d, :, ts].rearrange("(c p) t -> p c t", p=128), self.g_qg[d].t[:]))
            items.append((self.gla_kmg[d, :, ts].rearrange("(c p) t -> p c t", p=128), self.g_kmg[d].t[:]))
            items.append((self.gla_kd[d, ts, :].rearrange("(s p) c -> p s c", p=128), self.g_kd[d].t[:]))
            rd += [self.g_qg[d].b, self.g_kmg[d].b, self.g_kd[d].b]
        items.append((self.gla_v[ts, :].rearrange("(s p) c -> p s c", p=128), self.g_v.t[:]))
        rd.append(self.g_v.b)
        k.dma_batch("gpsimd", items, reads=rd, prim=k.buf("st_gla"))

    def scan(self, l):
        k = self.k
        self.begin_scope()
        NST = self.NST
        if "gla" in self.mix:
            self.alloc_sc_gla()
        if "ssd" in self.mix:
            self.alloc_sc_ssd()
        if "gdn" in self.mix:
            self.alloc_sc_gdn()
        for n in range(NST):
            pp = n % 2
            sts = (n, NST - 1 - n)
            self.ld_items = [[], []]
            self.ld_bufs = [[], []]
            if "gla" in self.mix:
                self.sc_gla_load(sts, pp)
            if "ssd" in self.mix:
                self.sc_ssd_load(sts, pp)
            if "gdn" in self.mix:
                self.sc_gdn_load(sts, pp)
            for d in range(2):
                k.dma_batch("sync", self.ld_items[d], writes=self.ld_bufs[d], prim=k.buf("scin%d%d" % (d, pp)))
            self.st_items = [[], []]
            self.st_bufs = [[], []]
            if "gla" in self.mix:
                self.sc_gla_pre(sts, pp)
            for ci in range(2):
                if "gdn" in self.mix:
                    self.sc_gdn_step1(sts, pp, ci)
                if "gla" in self.mix:
                    self.sc_gla_step(sts, pp, ci)
                if "ssd" in self.mix:
                    self.sc_ssd_step(sts, pp, ci)
                if "gdn" in self.mix:
                    self.sc_gdn_step2(sts, pp, ci)
            if "gla" in self.mix:
                self.sc_gla_out(sts, pp)
            if "ssd" in self.mix:
                self.sc_ssd_out(sts, pp)
            if "gdn" in self.mix:
                self.sc_gdn_out(sts, pp)
            for d in range(2):
                k.dma_batch("gpsimd", self.st_items[d], reads=self.st_bufs[d], prim=k.buf("scout%d%d" % (d, pp)))
        self.end_scope()

    def alloc_sc_gla(self):
        k = self.k
        R2 = range(2)
        self.s_gq = [[self.sb("s_gq%d%d" % (d, p), [128, 2, 128], BF16) for p in R2] for d in R2]
        self.s_gk = [[self.sb("s_gk%d%d" % (d, p), [128, 2, 128], BF16) for p in R2] for d in R2]
        self.s_gkd = [[self.sb("s_gkd%d%d" % (d, p), [128, 256], BF16) for p in R2] for d in R2]
        self.s_gv = [[self.sb("s_gv%d%d" % (d, p), [128, 512], BF16) for p in R2] for d in R2]
        self.s_gatt = [self.sb("s_gatt%d" % d, [128, 4, 128], BF16) for d in R2]
        self.s_gS = [self.sb("s_gS%d" % d, [128, 2, 128], F32) for d in R2]
        self.s_gSb = [self.sb("s_gSb%d" % d, [128, 2, 128], BF16) for d in R2]
        self.s_go = [[self.sb("s_go%d%d" % (d, p), [128, 4, 128], F32) for p in R2] for d in R2]
        self.s_gop = [None, None]
        for d in R2:
            self.ms(self.s_gS[d].t[:], 0.0, [self.s_gS[d].b])
            self.ms(self.s_gSb[d].t[:], 0.0, [self.s_gSb[d].b])

    def sc_gla_load(self, sts, pp):
        k = self.k
        for d in range(2):
            st = sts[d]
            ts = slice(st * 128, (st + 1) * 128)
            q, kk, kd, v = self.s_gq[d][pp], self.s_gk[d][pp], self.s_gkd[d][pp], self.s_gv[d][pp]
            self.ld_items[d] += [(q.t[:], self.gla_qg[d, :, ts].rearrange("(c p) t -> p c t", p=128)),
                                 (kk.t[:], self.gla_kmg[d, :, ts].rearrange("(c p) t -> p c t", p=128)),
                                 (kd.t[:], self.gla_kd[d, ts, :]), (v.t[:], self.gla_v[ts, :])]
            self.ld_bufs[d] += [q.b, kk.b, kd.b, v.b]

    def sc_gla_pre(self, sts, pp):
        for d in range(2):
            q, kk = self.s_gq[d][pp], self.s_gk[d][pp]
            ps = self.pbank()
            for h in range(4):
                c, r = h // 2, h % 2
                rs = slice(r * 64, (r + 1) * 64)
                self.mm(ps, ps.t[:, h * 128:(h + 1) * 128], kk.t[rs, c, :], q.t[rs, c, :], [kk.b, q.b])
            att = self.s_gatt[d]
            self.tt("vector", att.t[:], ps.t[:].rearrange("p (h i) -> p h i", h=4),
                    self.U[d].t[:, None, :].to_broadcast([128, 4, 128]), ALU.mult, [ps.b, self.U[d].b], [att.b])

    def sc_gla_step(self, sts, pp, ci):
        k = self.k
        for d in range(2):
            st = sts[d]
            ch = ci if d == 0 else 1 - ci
            cs = slice(ch * 64, (ch + 1) * 64)
            chabs = st * 2 + ch
            q, kd, v = self.s_gq[d][pp], self.s_gkd[d][pp], self.s_gv[d][pp]
            att, S, Sb = self.s_gatt[d], self.s_gS[d], self.s_gSb[d]
            op = self.pbank()
            for h in range(4):
                c, r = h // 2, h % 2
                rs = slice(r * 64, (r + 1) * 64)
                oap = op.t[:, h * 64:(h + 1) * 64]
                self.mm(op, oap, Sb.t[rs, c, :], q.t[rs, c, cs], [Sb.b, q.b], start=True, stop=False)
                self.mm(op, oap, v.t[:, h * 128:(h + 1) * 128], att.t[:, h, cs], [v.b, att.b], start=False, stop=True,
                        acc=True)
            self.cp("vector", self.s_go[d][pp].t[:, :, cs], op.t[:, 0:256].rearrange("p (h i) -> p h i", h=4), [op.b],
                    [self.s_go[d][pp].b])
            pP = self.pbank()
            for h in range(4):
                c, r = h // 2, h % 2
                self.mm(pP, pP.t[r * 64:(r + 1) * 64, c * 128:(c + 1) * 128], kd.t[cs, h * 64:(h + 1) * 64],
                        v.t[cs, h * 128:(h + 1) * 128], [kd.b, v.b])
            for c in range(2):
                self.stt(S.t[:, c, :], S.t[:, c, :], self.egl_gla.t[:, c, d, chabs:chabs + 1],
                         pP.t[:, c * 128:(c + 1) * 128], ALU.mult, ALU.add, [S.b, pP.b, self.egl_gla.b], [S.b])
            self.cp("scalar", Sb.t[:], S.t[:], [S.b], [Sb.b])

    def sc_gla_out(self, sts, pp):
        k = self.k
        for d in range(2):
            st = sts[d]
            ts = slice(st * 128, (st + 1) * 128)
            o = self.s_go[d][pp]
            self.st_items[d].append((self.gla_o[d, :, ts].rearrange("(h p) t -> p h t", p=128), o.t[:]))
            self.st_bufs[d].append(o.b)

    def alloc_p2_common(self, l):
        k = self.k
        self.nega = self.sb("nega", [128, 32], F32)
        o, _ = PR["sm_alog"]
        self.act(self.nega.t[:], self.prb.t[:, o:o + 32], AF.Exp, [self.prb.b], [self.nega.b])
        self.ts("vector", self.nega.t[:], self.nega.t[:], -1.0, None, ALU.mult, None, [self.nega.b], [self.nega.b])
        self.sp = self.sb("sp", [128, NSUB2, 24], F32)
        self.gd = self.sb("gd", [128, NSUB2, 24], F32)
        self.lnb = self.sb("lnb", [128, NSUB2, 8], F32)
        self.beta = self.sb("beta", [128, NSUB2, 8], F32)
        self.lndt = self.sb("lndt", [128, NSUB2, 16], F32)
        self.c_zr = [self.sb("c_zr%d" % i, [128, TE2], BF16) for i in range(2)]
        self.c_dg = [self.sb("c_dg%d" % i, [128, 5, 128], BF16) for i in range(2)]
        self.c_n = 0

    def p2_smalls_post(self, l, t):
        k = self.k
        sm, sp, gd = self.sm, self.sp, self.gd
        t0 = t * T2
        self.act(sp.t[:], sm.t[:, :, 8:32], AF.Exp, [sm.b], [sp.b])
        self.act(sp.t[:], sp.t[:], AF.Ln, [sp.b, self.oneb.b], [sp.b], bias=self.oneb.t[:, 0:1])
        self.tt("vector", gd.t[:], sp.t[:], self.nega.t[:, None, 8:32].to_broadcast([128, NSUB2, 24]), ALU.mult,
                [sp.b, self.nega.b], [gd.b])
        self.act(self.lndt.t[:], sp.t[:, :, 8:24], AF.Ln, [sp.b], [self.lndt.b])
        self.act(self.lnb.t[:], sm.t[:, :, 0:8], AF.Exp, [sm.b], [self.lnb.b], scale=-1.0)
        self.act(self.lnb.t[:], self.lnb.t[:], AF.Ln, [self.lnb.b, self.oneb.b], [self.lnb.b], bias=self.oneb.t[:, 0:1])
        self.act(self.beta.t[:], self.lnb.t[:], AF.Exp, [self.lnb.b], [self.beta.b], scale=-1.0)
        self.ts("vector", self.lnb.t[:], self.lnb.t[:], -1.0, None, ALU.mult, None, [self.lnb.b], [self.lnb.b])
        ps = self.pbank()
        for s in range(NSUB2):
            for c in range(2):
                j = s * 2 + c
                self.mm(ps, ps.t[:, j * 24:(j + 1) * 24], self.cind[c].t[:], gd.t[:, s, :], [self.cind[c].b, gd.b])
        ch0 = t0 // 64
        n = NSUB2 * 2
        self.act(self.egl.t[:, ch0:ch0 + n, :], ps.t[:, 0:n * 24].rearrange("p (j c) -> p j c", j=n), AF.Exp,
                 [ps.b], [self.egl.b])

    def conv_chunk(self, l, w, wcols, cwname, cc, bias_ap, out_ap, out_buf):
        k = self.k
        hE = self.hE
        ps = self.pbank()
        self.mm_group(ps, ps.t[:, 0:TE2], [(wcols(kc), hE.t[:, kc, 0:TE2]) for kc in range(KC)], reads=[w.b, hE.b])
        zr = self.c_zr[self.c_n % 2]
        dg = self.c_dg[self.c_n % 2]
        self.c_n += 1
        self.cp("scalar", zr.t[:], ps.t[:, 0:TE2], [ps.b], [zr.b])
        o, _ = PV[cwname]
        for kk in range(5):
            self.ts("vector", dg.t[:, kk, :], self.ident_bf.t[:], self.pv.t[:, l, o + cc * 5 + kk:o + cc * 5 + kk + 1], None,
                    ALU.mult, None, [self.ident_bf.b, self.pv.b], [dg.b])
        pc = self.pbank()
        self.mm_group(pc, pc.t[:, 0:T2], [(dg.t[:, kk, :], zr.t[:, kk:kk + T2]) for kk in range(5)], reads=[dg.b, zr.b])
        if bias_ap is not None:
            self.act(out_ap, pc.t[:, 0:T2], AF.Silu, [pc.b, self.pv.b], [out_buf], bias=bias_ap)
        else:
            self.act(out_ap, pc.t[:, 0:T2], AF.Silu, [pc.b], [out_buf])

    def alloc_p2_ssd(self, l):
        R2 = range(2)
        self.d_xbc = self.sb("d_xbc", [128, 8, T2], BF16)
        self.d_xtm = self.sb("d_xtm", [128, NSUB2, 512], BF16)
        self.d_Btm = self.sb("d_Btm", [128, NSUB2, 256], BF16)
        self.d_xw = [self.sb("d_xw%d" % d, [128, NSUB2, 512], BF16) for d in R2]
        self.d_yd = self.sb("d_yd", [128, 4, T2], F32)
        self.d_cexp = self.sb("d_cexp", [128, 16, 128], BF16)
        self.d_MT = self.sb("d_MT", [128, 16, 128], BF16)
        self.d_E = [self.sb("d_E%d" % i, [128, 128], F32) for i in range(4)]
        self.d_E1 = [self.sb("d_E1%d" % i, [128, 128], F32) for i in range(4)]
        self.d_sc = self.sb("d_sc", [128, 2, 128], F32)
        self.d_acum = self.sb("d_acum", [128, 16], F32)
        self.d_nb = self.sb("d_nb", [128, 16], F32)
        self.d_wst = self.sb("d_wst", [128, 16], F32)

    def p2_ssd(self, l, t):
        k = self.k
        t0 = t * T2
        ocb, _ = PV["ssd_cb"]
        od, _ = PV["ssd_d"]
        for ui in range(6, 10):
            w = self.load_unit(l, ("inA", ui))
            wv = w.t[:, 0:KC * 256].rearrange("p (k c) -> p k c", k=KC)
            for c2 in range(2):
                cc = (ui - 6) * 2 + c2
                self.conv_chunk(l, w, lambda kc, c2=c2, wv=wv: wv[:, kc, c2 * 128:(c2 + 1) * 128], "ssd_cw", cc,
                                self.pv.t[:, l, ocb + cc:ocb + cc + 1], self.d_xbc.t[:, cc, :], self.d_xbc.b)
        xbc = self.d_xbc
        for s in range(NSUB2):
            ss = slice(s * 128, (s + 1) * 128)
            tok = slice(t0 + s * 128, t0 + (s + 1) * 128)
            pT = self.ptr
            pTv = pT.t[:]
            if getattr(self, 'cut', 99) <= -1:
                continue
            for c in range(6):
                self.tr(pTv[:, c * 128:(c + 1) * 128], xbc.t[:, c, ss], [xbc.b], pT)
            if getattr(self, 'cut', 99) <= 0:
                continue
            self.cp("scalar", self.d_xtm.t[:, s, :], pTv[:, 0:512], [pT.b], [self.d_xtm.b])
            self.cp("vector", self.d_Btm.t[:, s, :], pTv[:, 512:768], [pT.b], [self.d_Btm.b])
            if getattr(self, 'cut', 99) <= 1:
                continue
            pa = self.pbank()
            for d in range(2):
                self.mm(pa, pa.t[:, d * 8:(d + 1) * 8], self.U[d].t[:], self.gd.t[:, s, 8 + d * 8:16 + d * 8],
                        [self.U[d].b, self.gd.b])
            for d in range(2):
                self.mm(pa, pa.t[:, 16 + d * 8:24 + d * 8], self.SU[d].t[:], self.gd.t[:, s, 8 + d * 8:16 + d * 8],
                        [self.SU[d].b, self.gd.b])
            self.cp("vector", self.d_acum.t[:], pa.t[:, 0:16], [pa.b], [self.d_acum.b])
            self.act(self.d_wst.t[:], pa.t[:, 16:32], AF.Exp, [pa.b], [self.d_wst.b])
            self.tt("vector", self.d_wst.t[:], self.d_wst.t[:], self.sp.t[:, s, 8:24], ALU.mult,
                    [self.d_wst.b, self.sp.b], [self.d_wst.b])
            self.tt("vector", self.d_nb.t[:], self.lndt.t[:, s, :], self.d_acum.t[:], ALU.subtract,
                    [self.lndt.b, self.d_acum.b], [self.d_nb.b])
            if getattr(self, 'cut', 99) <= 2:
                continue
            for d in range(2):
                self.tt("gpsimd", self.d_xw[d].t[:, s, :].rearrange("p (h q) -> p h q", h=8),
                        self.d_xtm.t[:, s, :].rearrange("p (h q) -> p h q", h=8),
                        self.d_wst.t[:, d * 8:(d + 1) * 8, None].to_broadcast([128, 8, 64]), ALU.mult,
                        [self.d_xtm.b, self.d_wst.b], [self.d_xw[d].b])
            if getattr(self, 'cut', 99) <= 3:
                continue
            pS = self.pbank()
            for g in range(2):
                self.mm(pS, pS.t[:, g * 128:(g + 1) * 128], xbc.t[:, 4 + g, ss], xbc.t[:, 6 + g, ss], [xbc.b])
            self.cp("scalar", self.d_sc.t[:], pS.t[:, 0:256].rearrange("p (g i) -> p g i", g=2), [pS.b], [self.d_sc.b])
            if getattr(self, 'cut', 99) <= 4:
                continue
            for d in range(2):
                for h in range(8):
                    dh = d * 8 + h
                    g = h // 4
                    pA = self.pbank()
                    bc = self.d_acum.t[:, dh:dh + 1].to_broadcast([128, 128])
                    self.mm(pA, pA.t[:, 0:128], bc, self.ident.t[:], [self.d_acum.b, self.ident.b])
                    self.mm(pA, pA.t[:, 128:256], bc, self.ident.t[:], [self.d_acum.b, self.ident.b], start=True, stop=False)
                    self.mm(pA, pA.t[:, 128:256], self.ident.t[:], self.maskneg[d].t[:], [self.ident.b, self.maskneg[d].b],
                            start=False, stop=True, acc=True)
                    E1, E = self.d_E1[dh % 4], self.d_E[dh % 4]
                    self.act(E1.t[:], pA.t[:, 0:128], AF.Exp, [pA.b], [E1.b])
                    self.act(E.t[:], pA.t[:, 128:256], AF.Exp, [pA.b, self.d_nb.b], [E.b], bias=self.d_nb.t[:, dh:dh + 1])
                    self.tt("gpsimd", self.d_cexp.t[:, dh, :], xbc.t[:, 6 + g, ss], E1.t[:], ALU.mult,
                            [xbc.b, E1.b], [self.d_cexp.b])
                    self.tt("vector", self.d_MT.t[:, dh, :], E.t[:], self.d_sc.t[:, g, :], ALU.mult,
                            [E.b, self.d_sc.b], [self.d_MT.b])
            if getattr(self, 'cut', 99) <= 5:
                continue
            pY = self.pbank()
            for h in range(8):
                c, r = h // 2, h % 2
                for d in range(2):
                    self.mm(pY, pY.t[r * 64:(r + 1) * 64, c * 128:(c + 1) * 128], self.d_xtm.t[:, s, h * 64:(h + 1) * 64],
                            self.d_MT.t[:, d * 8 + h, :], [self.d_xtm.b, self.d_MT.b], start=(d == 0), stop=(d == 1),
                            acc=(d == 1))
            for c in range(4):
                self.stt(self.d_yd.t[:, c, ss], xbc.t[:, c, ss], self.pv.t[:, l, od + c:od + c + 1],
                         pY.t[:, c * 128:(c + 1) * 128], ALU.mult, ALU.add, [xbc.b, pY.b, self.pv.b], [self.d_yd.b])
            if getattr(self, 'cut', 99) <= 6:
                continue
            k.dma_batch("gpsimd", [(self.ssd_cexp[d, :, tok].rearrange("(h p) t -> p h t", p=128),
                                    self.d_cexp.t[:, d * 8:(d + 1) * 8, :]) for d in range(2)],
                        reads=[self.d_cexp.b], prim=k.buf("st_ssd_s"))
        if getattr(self, 'cut', 99) <= 7:
            return
        ts = slice(t0, t0 + T2)
        items = [(self.ssd_B[ts, :].rearrange("(s p) c -> p s c", p=128), self.d_Btm.t[:]),
                 (self.ssd_ydiag[:, ts].rearrange("(c p) t -> p c t", p=128), self.d_yd.t[:])]
        for d in range(2):
            items.append((self.ssd_xw[d, ts, :].rearrange("(s p) c -> p s c", p=128), self.d_xw[d].t[:]))
        k.dma_batch("gpsimd", items, reads=[self.d_Btm.b, self.d_yd.b, self.d_xw[0].b, self.d_xw[1].b],
                    prim=k.buf("st_ssd_t"))

    def alloc_sc_ssd(self):
        k = self.k
        R2 = range(2)
        self.s_dc = [[self.sb("s_dc%d%d" % (d, p), [128, 8, 128], BF16) for p in R2] for d in R2]
        self.s_dB = [[self.sb("s_dB%d%d" % (d, p), [128, 256], BF16) for p in R2] for d in R2]
        self.s_dxw = [[self.sb("s_dxw%d%d" % (d, p), [128, 512], BF16) for p in R2] for d in R2]
        self.s_dS = [self.sb("s_dS%d" % d, [128, 512], F32) for d in R2]
        self.s_dSb = [self.sb("s_dSb%d" % d, [128, 512], BF16) for d in R2]
        self.s_dy = [[self.sb("s_dy%d%d" % (d, p), [128, 4, 128], F32) for p in R2] for d in R2]
        self.s_dop = [None, None]
        for d in R2:
            self.ms(self.s_dS[d].t[:], 0.0, [self.s_dS[d].b])
            self.ms(self.s_dSb[d].t[:], 0.0, [self.s_dSb[d].b])

    def sc_ssd_load(self, sts, pp):
        k = self.k
        for d in range(2):
            st = sts[d]
            ts = slice(st * 128, (st + 1) * 128)
            c, B, xw = self.s_dc[d][pp], self.s_dB[d][pp], self.s_dxw[d][pp]
            self.ld_items[d] += [(c.t[:], self.ssd_cexp[d, :, ts].rearrange("(h p) t -> p h t", p=128)),
                                 (B.t[:], self.ssd_B[ts, :]), (xw.t[:], self.ssd_xw[d, ts, :])]
            self.ld_bufs[d] += [c.b, B.b, xw.b]

    def sc_ssd_step(self, sts, pp, ci):
        k = self.k
        for d in range(2):
            st = sts[d]
            ch = ci if d == 0 else 1 - ci
            cs = slice(ch * 64, (ch + 1) * 64)
            chabs = st * 2 + ch
            c, B, xw = self.s_dc[d][pp], self.s_dB[d][pp], self.s_dxw[d][pp]
            S, Sb = self.s_dS[d], self.s_dSb[d]
            op = self.pbank()
            for h in range(8):
                cc, r = h // 2, h % 2
                self.mm(op, op.t[r * 64:(r + 1) * 64, cc * 64:(cc + 1) * 64],
                        Sb.t[:, h * 64:(h + 1) * 64], c.t[:, h, cs], [Sb.b, c.b])
            self.cp("vector", self.s_dy[d][pp].t[:, :, cs], op.t[:, 0:256].rearrange("p (c i) -> p c i", c=4), [op.b],
                    [self.s_dy[d][pp].b])
            pP = self.pbank()
            for g in range(2):
                self.mm(pP, pP.t[:, g * 256:(g + 1) * 256], B.t[cs, g * 128:(g + 1) * 128], xw.t[cs, g * 256:(g + 1) * 256],
                        [B.b, xw.b])
            for h in range(8):
                self.stt(S.t[:, h * 64:(h + 1) * 64], S.t[:, h * 64:(h + 1) * 64],
                         self.egl.t[:, chabs, 8 + d * 8 + h:8 + d * 8 + h + 1], pP.t[:, h * 64:(h + 1) * 64], ALU.mult, ALU.add,
                         [S.b, pP.b, self.egl.b], [S.b])
            self.cp("scalar", Sb.t[:], S.t[:], [S.b], [Sb.b])

    def sc_ssd_out(self, sts, pp):
        k = self.k
        for d in range(2):
            st = sts[d]
            ts = slice(st * 128, (st + 1) * 128)
            y = self.s_dy[d][pp]
            self.st_items[d].append((self.ssd_yoff[d, :, ts].rearrange("(c p) t -> p c t", p=128), y.t[:]))
            self.st_bufs[d].append(y.b)

    def alloc_p2_gdn(self, l):
        k = self.k
        R2 = range(2)
        self.e_raw = self.sb("e_raw", [128, 8, T2], F32)
        self.e_q = self.sb("e_q", [128, 4, T2], BF16)
        self.e_k = self.sb("e_k", [128, 4, T2], BF16)
        self.e_v = self.sb("e_v", [128, 4, T2], BF16)
        self.e_sq = self.sb("e_sq", [128, T2], BF16)
        self.e_rs = self.sb("e_rs", [128, T2], F32)
        self.e_ktm = self.sb("e_ktm", [128, 4, 128], BF16)
        self.e_vtm = self.sb("e_vtm", [128, 4, 128], BF16)
        self.e_kg = [self.sb("e_kg%d" % d, [128, NSUB2, 512], BF16) for d in R2]
        self.e_kbg = [self.sb("e_kbg%d" % d, [128, 4, 128], BF16) for d in R2]
        self.e_vb = [self.sb("e_vb%d" % d, [128, 4, 128], BF16) for d in R2]
        self.e_kkqk = [self.sb("e_kkqk%d" % h, [128, 2, 128], F32) for h in range(4)]
        self.e_B = [[self.sb("e_B%d_%d" % (u, p), [128, 2, 128], BF16) for p in R2] for u in range(8)]
        self.e_XT = [[self.sb("e_XT%d_%d" % (u, p), [128, 128], BF16) for p in R2] for u in range(8)]
        self.e_E = [self.sb("e_E%d" % i, [128, 128], F32) for i in range(8)]
        self.e_En = 0
        self.e_att = [self.sb("e_att%d" % d, [128, 4, 128], BF16) for d in R2]
        self.e_u = [self.sb("e_u%d" % d, [128, 4, 128], F32) for d in R2]
        self.e_wn = [self.sb("e_wn%d" % d, [128, 4, 128], BF16) for d in R2]
        self.e_qg = [self.sb("e_qg%d" % d, [128, 4, T2], BF16) for d in R2]
        self.e_G = self.sb("e_G", [128, 8], F32)
        self.e_nG = self.sb("e_nG", [128, 8], F32)
        self.e_rA = self.sb("e_rA", [128, 8], F32)
        self.e_egs = self.sb("e_egs", [128, 8], F32)
        self.e_bG = self.sb("e_bG", [128, 8], F32)
        self.mposA = [self.sb("mposA%d" % d, [128, 128], F32) for d in R2]
        self.mnegAT = [self.sb("mnegAT%d" % d, [128, 128], F32) for d in R2]
        for d in R2:
            self.ts("vector", self.mposA[d].t[:], self.SU[d].t[:], -30000.0, 30000.0, ALU.mult, ALU.add,
                    [self.SU[d].b], [self.mposA[d].b])
            self.ts("vector", self.mnegAT[d].t[:], self.SU[1 - d].t[:], 30000.0, -30000.0, ALU.mult, ALU.add,
                    [self.SU[1 - d].b], [self.mnegAT[d].b])

    def nextE(self):
        t = self.e_E[self.e_En % 8]
        self.e_En += 1
        return t

    def p2_gdn(self, l, t):
        k = self.k
        t0 = t * T2
        for ui in range(6):
            w = self.load_unit(l, ("inA", ui))
            wv = w.t[:, 0:KC * 256].rearrange("p (k c) -> p k c", k=KC)
            for c2 in range(2):
                cc = ui * 2 + c2
                if cc < 8:
                    out_ap, ob = self.e_raw.t[:, cc, :], self.e_raw.b
                else:
                    out_ap, ob = self.e_v.t[:, cc - 8, :], self.e_v.b
                self.conv_chunk(l, w, lambda kc, c2=c2, wv=wv: wv[:, kc, c2 * 128:(c2 + 1) * 128], "gdn_cw", cc, None,
                                out_ap, ob)
        for cc in range(8):
            raw = self.e_raw.t[:, cc, :]
            self.act(self.e_sq.t[:], raw, AF.Square, [self.e_raw.b], [self.e_sq.b])
            ps = self.pbank()
            self.mm(ps, ps.t[:, 0:T2], self.ones_bf.t[:], self.e_sq.t[:], [self.ones_bf.b, self.e_sq.b])
            self.rstd_from_ss(ps.t[:, 0:T2], ps.b, self.e_rs, 1)
            dst = self.e_q if cc < 4 else self.e_k
            self.stt(dst.t[:, cc % 4, :], raw, (128 ** -0.5) if cc < 4 else 1.0, self.e_rs.t[:], ALU.mult, ALU.mult,
                     [self.e_raw.b, self.e_rs.b], [dst.b])
        for s in range(NSUB2):
            ss = slice(s * 128, (s + 1) * 128)
            tok = slice(t0 + s * 128, t0 + (s + 1) * 128)
            st = (t0 + s * 128) // 128
            pa = self.pbank()
            for d in range(2):
                self.mm(pa, pa.t[:, d * 4:(d + 1) * 4], self.U[d].t[:], self.gd.t[:, s, d * 4:(d + 1) * 4],
                        [self.U[d].b, self.gd.b])
            for d in range(2):
                self.mm(pa, pa.t[:, 8 + d * 4:12 + d * 4], self.SU[d].t[:], self.gd.t[:, s, d * 4:(d + 1) * 4],
                        [self.SU[d].b, self.gd.b])
            self.cp("vector", self.e_G.t[:], pa.t[:, 0:8], [pa.b], [self.e_G.b])
            self.act(self.e_egs.t[:], pa.t[:, 8:16], AF.Exp, [pa.b], [self.e_egs.b])
            self.act(self.e_bG.t[:], self.e_G.t[:], AF.Exp, [self.e_G.b], [self.e_bG.b])
            self.tt("vector", self.e_bG.t[:], self.e_bG.t[:], self.beta.t[:, s, :], ALU.mult, [self.e_bG.b, self.beta.b],
                    [self.e_bG.b])
            self.tt("vector", self.e_rA.t[:], self.e_G.t[:], self.lnb.t[:, s, :], ALU.add, [self.e_G.b, self.lnb.b],
                    [self.e_rA.b])
            self.ts("vector", self.e_nG.t[:], self.e_G.t[:], -1.0, None, ALU.mult, None, [self.e_G.b], [self.e_nG.b])
            pT = self.ptr
            for h in range(4):
                self.tr(pT.t[:, h * 128:(h + 1) * 128], self.e_k.t[:, h, ss], [self.e_k.b], pT)
            for h in range(4):
                self.tr(pT.t[:, 512 + h * 128:512 + (h + 1) * 128], self.e_v.t[:, h, ss], [self.e_v.b], pT)
            self.cp("scalar", self.e_ktm.t[:], pT.t[:, 0:512].rearrange("p (h c) -> p h c", h=4), [pT.b], [self.e_ktm.b])
            self.cp("vector", self.e_vtm.t[:], pT.t[:, 512:1024].rearrange("p (h c) -> p h c", h=4), [pT.b], [self.e_vtm.b])
            for d in range(2):
                ds_ = slice(d * 4, (d + 1) * 4)
                bc = lambda tl: tl.t[:, ds_, None].to_broadcast([128, 4, 128])
                self.tt("gpsimd", self.e_kbg[d].t[:], self.e_ktm.t[:], self.e_bG.t[:, d * 4:(d + 1) * 4, None].to_broadcast([128, 4, 128]),
                        ALU.mult, [self.e_ktm.b, self.e_bG.b], [self.e_kbg[d].b])
                self.tt("gpsimd", self.e_kg[d].t[:, s, :].rearrange("p (h c) -> p h c", h=4), self.e_ktm.t[:],
                        self.e_egs.t[:, d * 4:(d + 1) * 4, None].to_broadcast([128, 4, 128]), ALU.mult,
                        [self.e_ktm.b, self.e_egs.b], [self.e_kg[d].b])
                self.tt("gpsimd", self.e_vb[d].t[:], self.e_vtm.t[:],
                        self.beta.t[:, s, d * 4:(d + 1) * 4, None].to_broadcast([128, 4, 128]), ALU.mult,
                        [self.e_vtm.b, self.beta.b], [self.e_vb[d].b])
            for h in range(4):
                ps = self.pbank()
                self.mm(ps, ps.t[:, 0:128], self.e_k.t[:, h, ss], self.e_k.t[:, h, ss], [self.e_k.b])
                self.mm(ps, ps.t[:, 128:256], self.e_k.t[:, h, ss], self.e_q.t[:, h, ss], [self.e_k.b, self.e_q.b])
                self.cp("scalar", self.e_kkqk[h].t[:], ps.t[:, 0:256].rearrange("p (a i) -> p a i", a=2), [ps.b],
                        [self.e_kkqk[h].b])
            units = [(d, h) for d in range(2) for h in range(4)]
            for ui_, (d, h) in enumerate(units):
                dh = d * 4 + h
                kk, qk = self.e_kkqk[h].t[:, 0, :], self.e_kkqk[h].t[:, 1, :]
                kb_ = self.e_kkqk[h].b
                B0 = self.e_B[ui_][0]
                pA = self.pbank()
                bcG = self.e_G.t[:, dh:dh + 1].to_broadcast([128, 128])
                bcR = self.e_rA.t[:, dh:dh + 1].to_broadcast([128, 128])
                idt = self.ident
                self.mm(pA, pA.t[:, 0:128], bcG, idt.t[:], [self.e_G.b, idt.b], start=True, stop=False)
                self.mm(pA, pA.t[:, 0:128], idt.t[:], self.mposA[d].t[:], [idt.b, self.mposA[d].b], start=False, stop=True, acc=True)
                self.mm(pA, pA.t[:, 128:256], bcR, idt.t[:], [self.e_rA.b, idt.b], start=True, stop=False)
                self.mm(pA, pA.t[:, 128:256], idt.t[:], self.mnegAT[d].t[:], [idt.b, self.mnegAT[d].b], start=False, stop=True, acc=True)
                self.mm(pA, pA.t[:, 256:384], bcG, idt.t[:], [self.e_G.b, idt.b], start=True, stop=False)
                self.mm(pA, pA.t[:, 256:384], idt.t[:], self.maskneg[d].t[:], [idt.b, self.maskneg[d].b], start=False, stop=True, acc=True)
                self.mm(pA, pA.t[:, 384:512], bcG, idt.t[:], [self.e_G.b, idt.b])
                E = self.nextE()
                self.act(E.t[:], pA.t[:, 0:128], AF.Exp, [pA.b, self.e_rA.b], [E.b], scale=-1.0, bias=self.e_rA.t[:, dh:dh + 1])
                self.stt(B0.t[:, 0, :], E.t[:], -1.0, kk, ALU.mult, ALU.mult, [E.b, kb_], [B0.b])
                E = self.nextE()
                self.act(E.t[:], pA.t[:, 128:256], AF.Exp, [pA.b, self.e_nG.b], [E.b], bias=self.e_nG.t[:, dh:dh + 1])
                self.stt(B0.t[:, 1, :], E.t[:], -1.0, kk, ALU.mult, ALU.mult, [E.b, kb_], [B0.b])
                E = self.nextE()
                self.act(E.t[:], pA.t[:, 256:384], AF.Exp, [pA.b, self.e_nG.b], [E.b], bias=self.e_nG.t[:, dh:dh + 1])
                self.tt("vector", self.e_att[d].t[:, h, :], E.t[:], qk, ALU.mult, [E.b, kb_], [self.e_att[d].b])
                E = self.nextE()
                self.act(E.t[:], pA.t[:, 384:512], AF.Exp, [pA.b], [E.b])
                self.tt("gpsimd", self.e_qg[d].t[:, h, ss], self.e_q.t[:, h, ss], E.t[:], ALU.mult, [self.e_q.b, E.b],
                        [self.e_qg[d].b])
                self.tt("gpsimd", self.e_XT[ui_][0].t[:], B0.t[:, 1, :], self.ident.t[:], ALU.add, [B0.b, self.ident.b],
                        [self.e_XT[ui_][0].b])
            for lev in range(1, 6):
                pi, po = (lev - 1) % 2, lev % 2
                n = 2 if lev < 5 else 1
                sqb = []
                for pr in range(4):
                    ps = self.pbank()
                    for a in range(2):
                        Bp = self.e_B[2 * pr + a][pi]
                        self.mm(ps, ps.t[:, a * 256:a * 256 + 128], Bp.t[:, 1, :], Bp.t[:, 0, :], [Bp.b])
                        if lev < 5:
                            self.mm(ps, ps.t[:, a * 256 + 128:a * 256 + 256], Bp.t[:, 0, :], Bp.t[:, 1, :], [Bp.b])
                    sqb.append(ps)
                for ui_ in range(8):
                    ps = sqb[ui_ // 2]
                    a = ui_ % 2
                    Bn = self.e_B[ui_][po]
                    self.cp("scalar", Bn.t[:, 0:n, :], ps.t[:, a * 256:a * 256 + n * 128].rearrange("p (a i) -> p a i", a=n),
                            [ps.b], [Bn.b])
                prb_ = []
                for hf in range(2):
                    ps2 = self.pbank()
                    for a in range(4):
                        ui_ = hf * 4 + a
                        Bn, Xp = self.e_B[ui_][po], self.e_XT[ui_][pi]
                        self.mm(ps2, ps2.t[:, a * 128:(a + 1) * 128], Bn.t[:, 0, :], Xp.t[:], [Bn.b, Xp.b])
                    prb_.append(ps2)
                for ui_ in range(8):
                    ps2 = prb_[ui_ // 4]
                    a = ui_ % 4
                    Xp, Xn = self.e_XT[ui_][pi], self.e_XT[ui_][po]
                    self.tt("vector", Xn.t[:], Xp.t[:], ps2.t[:, a * 128:(a + 1) * 128], ALU.add, [Xp.b, ps2.b], [Xn.b])
            fin = 5 % 2
            for d in range(2):
                pu = self.pbank()
                pw = self.pbank()
                for h in range(4):
                    X = self.e_XT[d * 4 + h][fin]
                    self.mm(pu, pu.t[:, h * 128:(h + 1) * 128], X.t[:], self.e_vb[d].t[:, h, :], [X.b, self.e_vb[d].b])
                for h in range(4):
                    X = self.e_XT[d * 4 + h][fin]
                    self.mm(pw, pw.t[:, h * 128:(h + 1) * 128], self.e_kbg[d].t[:, h, :], X.t[:], [X.b, self.e_kbg[d].b])
                self.cp("scalar", self.e_u[d].t[:], pu.t[:].rearrange("p (h c) -> p h c", h=4), [pu.b], [self.e_u[d].b])
                self.ts("vector", self.e_wn[d].t[:], pw.t[:].rearrange("p (h c) -> p h c", h=4), -1.0, None, ALU.mult, None,
                        [pw.b], [self.e_wn[d].b])
            items, rd = [], []
            for d in range(2):
                items.append((self.gdn_u[d, tok, :], self.e_u[d].t[:].rearrange("p h c -> p (h c)")))
                items.append((self.gdn_wn[d, :, tok].rearrange("(h p) t -> p h t", p=128), self.e_wn[d].t[:]))
                items.append((self.gdn_att[d, st * 128:(st + 1) * 128, :], self.e_att[d].t[:].rearrange("p h c -> p (h c)")))
                rd += [self.e_u[d].b, self.e_wn[d].b, self.e_att[d].b]
            k.dma_batch("gpsimd", items, reads=rd, prim=k.buf("st_gdn_s"))
        ts = slice(t0, t0 + T2)
        items, rd = [], []
        for d in range(2):
            items.append((self.gdn_kg[d, ts, :].rearrange("(s p) c -> p s c", p=128), self.e_kg[d].t[:]))
            items.append((self.gdn_qg[d, :, ts].rearrange("(h p) t -> p h t", p=128), self.e_qg[d].t[:]))
            rd += [self.e_kg[d].b, self.e_qg[d].b]
        k.dma_batch("gpsimd", items, reads=rd, prim=k.buf("st_gdn_t"))

    def alloc_sc_gdn(self):
        k = self.k
        R2 = range(2)
        self.s_ewn = [[self.sb("s_ewn%d%d" % (d, p), [128, 4, 128], BF16) for p in R2] for d in R2]
        self.s_eu = [[self.sb("s_eu%d%d" % (d, p), [128, 512], F32) for p in R2] for d in R2]
        self.s_eqg = [[self.sb("s_eqg%d%d" % (d, p), [128, 4, 128], BF16) for p in R2] for d in R2]
        self.s_ekg = [[self.sb("s_ekg%d%d" % (d, p), [128, 512], BF16) for p in R2] for d in R2]
        self.s_eatt = [[self.sb("s_eatt%d%d" % (d, p), [128, 512], BF16) for p in R2] for d in R2]
        self.s_eS = [self.sb("s_eS%d" % d, [128, 4, 128], F32) for d in R2]
        self.s_eSb = [self.sb("s_eSb%d" % d, [128, 4, 128], BF16) for d in R2]
        self.s_evn = [self.sb("s_evn%d" % d, [128, 512], BF16) for d in R2]
        self.s_eo = [[self.sb("s_eo%d%d" % (d, p), [128, 4, 128], F32) for p in R2] for d in R2]
        self.s_eop = [None, None]
        for d in R2:
            self.ms(self.s_eS[d].t[:], 0.0, [self.s_eS[d].b])
            self.ms(self.s_eSb[d].t[:], 0.0, [self.s_eSb[d].b])

    def sc_gdn_load(self, sts, pp):
        k = self.k
        for d in range(2):
            st = sts[d]
            ts = slice(st * 128, (st + 1) * 128)
            wn, u, qg, kg, att = self.s_ewn[d][pp], self.s_eu[d][pp], self.s_eqg[d][pp], self.s_ekg[d][pp], self.s_eatt[d][pp]
            self.ld_items[d] += [(wn.t[:], self.gdn_wn[d, :, ts].rearrange("(h p) t -> p h t", p=128)),
                                 (u.t[:], self.gdn_u[d, ts, :]),
                                 (qg.t[:], self.gdn_qg[d, :, ts].rearrange("(h p) t -> p h t", p=128)),
                                 (kg.t[:], self.gdn_kg[d, ts, :]), (att.t[:], self.gdn_att[d, ts, :])]
            self.ld_bufs[d] += [wn.b, u.b, qg.b, kg.b, att.b]

    def sc_gdn_step1(self, sts, pp, ci):
        for d in range(2):
            ch = ci if d == 0 else 1 - ci
            cs = slice(ch * 64, (ch + 1) * 64)
            wn, u = self.s_ewn[d][pp], self.s_eu[d][pp]
            Sb, vn = self.s_eSb[d], self.s_evn[d]
            pv = self.pbank()
            for h in range(4):
                self.mm(pv, pv.t[cs, h * 128:(h + 1) * 128], wn.t[:, h, cs], Sb.t[:, h, :], [wn.b, Sb.b])
            self.tt("vector", vn.t[cs, :], u.t[cs, :], pv.t[cs, :], ALU.add, [u.b, pv.b], [vn.b])

    def sc_gdn_step2(self, sts, pp, ci):
        for d in range(2):
            st = sts[d]
            ch = ci if d == 0 else 1 - ci
            cs = slice(ch * 64, (ch + 1) * 64)
            chabs = st * 2 + ch
            qg, kg, att = self.s_eqg[d][pp], self.s_ekg[d][pp], self.s_eatt[d][pp]
            S, Sb, vn = self.s_eS[d], self.s_eSb[d], self.s_evn[d]
            op = self.pbank()
            for h in range(4):
                oap = op.t[:, h * 64:(h + 1) * 64]
                self.mm(op, oap, Sb.t[:, h, :], qg.t[:, h, cs], [Sb.b, qg.b], start=True, stop=False)
                self.mm(op, oap, vn.t[cs, h * 128:(h + 1) * 128], att.t[cs, h * 128 + ch * 64:h * 128 + (ch + 1) * 64],
                        [vn.b, att.b], start=False, stop=True, acc=True)
            self.cp("scalar", self.s_eo[d][pp].t[:, :, cs], op.t[:, 0:256].rearrange("p (h i) -> p h i", h=4), [op.b],
                    [self.s_eo[d][pp].b])
            pP = self.pbank()
            for h in range(4):
                self.mm(pP, pP.t[:, h * 128:(h + 1) * 128], kg.t[cs, h * 128:(h + 1) * 128], vn.t[cs, h * 128:(h + 1) * 128],
                        [kg.b, vn.b])
            for h in range(4):
                self.stt(S.t[:, h, :], S.t[:, h, :], self.egl.t[:, chabs, d * 4 + h:d * 4 + h + 1],
                         pP.t[:, h * 128:(h + 1) * 128], ALU.mult, ALU.add, [S.b, pP.b, self.egl.b], [S.b])
            self.cp("scalar", Sb.t[:], S.t[:], [S.b], [Sb.b])

    def sc_gdn_out(self, sts, pp):
        k = self.k
        for d in range(2):
            st = sts[d]
            ts = slice(st * 128, (st + 1) * 128)
            o = self.s_eo[d][pp]
            self.st_items[d].append((self.gdn_o[d, :, ts].rearrange("(h p) t -> p h t", p=128), o.t[:]))
            self.st_bufs[d].append(o.b)
_CACHE = {}


def run(inputs, depth, mix=("gdn", "gla", "ssd")):
    xp = np.asarray(inputs["x_prompt"], np.float32)
    xs = np.asarray(inputs["x_sample"], np.float32)
    L = xp.shape[1]
    assert xs.shape[1] == L
    seqs = [xp[i] for i in range(xp.shape[0])] + [xs[i] for i in range(xs.shape[0])]
    nseq = len(seqs)
    key = (L, depth, tuple(mix))
    if key not in _CACHE:
        _CACHE[key] = Builder(L, depth, mix).build()
    nc = _CACHE[key]
    inp = {n: np.asarray(v, np.float32) for n, v in inputs.items()}
    wf32 = np.stack([pack_layer_weights(inp, l) for l in range(depth)])
    pvec = np.stack([pack_pvec(inp, l) for l in range(depth)])
    prow = np.stack([pack_prow(inp, l) for l in range(depth)])
    wgup = np.stack([pack_wgup(inp, l) for l in range(depth)])
    fnorm = np.ascontiguousarray(inp["final_norm"].reshape(KC, 128).T)
    in_maps = []
    for c in range(8):
        s = seqs[c % nseq]
        in_maps.append({"xin": np.ascontiguousarray(s.T), "wf32": wf32, "pvec": pvec, "prow": prow, "wgup": wgup,
                        "fnorm": fnorm})
    res = run_bass_kernel_spmd(nc, in_maps, core_ids=list(range(8)))
    outs = [np.ascontiguousarray(res.results[c]["yout"].T) for c in range(nseq)]
    yp = np.stack(outs[:xp.shape[0]]).astype(np.float32)
    ys = np.stack(outs[xp.shape[0]:]).astype(np.float32)
    return yp, ys


def kernel(**inputs):
    return run(inputs, 4)
```
